# Optimizing a Trainium2 kernel written in Bass

```python
import math
import jax, jax.numpy as jnp
from jax import lax
import numpy as np

D_MODEL = 1024
BATCH = 32
SEQ = 2048
DEPTH = 1

MIX_WIDTH = D_MODEL
SSM_WIDTH = MIX_WIDTH // 2
ATTN_WIDTH = MIX_WIDTH - SSM_WIDTH
SSM_GROUP = 16
SSM_GROUPS = SSM_WIDTH // SSM_GROUP
SSM_STATE = 64
HEAD_DIM = 64
N_HEADS = ATTN_WIDTH // HEAD_DIM
IDX_HEADS = 8
IDX_DIM = 64
TOPK_MAX = 256
QBLOCK = 32
NUM_BUCKETS = 32
MAX_DISTANCE = 128
EPS = 1e-6
DT_MIN = 1e-3
DT_MAX = 1e-1
SPLITS = (SSM_WIDTH, SSM_WIDTH, ATTN_WIDTH, ATTN_WIDTH, ATTN_WIDTH, ATTN_WIDTH,
          IDX_HEADS * IDX_DIM, IDX_DIM, IDX_HEADS)
IN_WIDTH = 2 * SSM_WIDTH + 4 * ATTN_WIDTH + IDX_HEADS * IDX_DIM + IDX_DIM + IDX_HEADS

kernel_name = "hybrid_s5_dsa_parallel_heads"


def rms_norm(x, g):
    xf = x.astype(jnp.float32)
    xf = xf * lax.rsqrt(jnp.mean(xf * xf, axis=-1, keepdims=True) + EPS)
    return (xf * g.astype(jnp.float32)).astype(x.dtype)


def t5_causal_bucket(dist):
    max_exact = NUM_BUCKETS // 2
    is_small = dist < max_exact
    d = jnp.maximum(dist, 1).astype(jnp.float32)
    large = max_exact + (jnp.log(d / max_exact) / math.log(MAX_DISTANCE / max_exact)
                         * (NUM_BUCKETS - max_exact)).astype(jnp.int32)
    large = jnp.minimum(large, NUM_BUCKETS - 1)
    return jnp.where(is_small, dist, large)


def _ssm_combine(e1, e2):
    a1, b1 = e1
    a2, b2 = e2
    return a1 * a2, a2 * b1 + b2


def s5_mixer(u, a_re, a_im, log_dt, b_re, b_im, c_re, c_im, d_skip, w_glu, b_glu):
    bsz, seq, _ = u.shape
    uf = u.astype(jnp.float32)
    ug = uf.reshape(bsz, seq, SSM_GROUPS, SSM_GROUP)
    lam = lax.complex(a_re.astype(jnp.float32), a_im.astype(jnp.float32))
    dt = jnp.exp(log_dt.astype(jnp.float32))[:, None]
    lam_bar = jnp.exp(lam * dt)
    b_mat = lax.complex(b_re.astype(jnp.float32), b_im.astype(jnp.float32))
    b_bar = ((lam_bar - 1.0) / lam)[..., None] * b_mat
    bu = jnp.einsum('gpc,bsgc->bsgp', b_bar, ug.astype(jnp.complex64))
    a_seq = jnp.broadcast_to(lam_bar, (1, seq) + lam_bar.shape)
    _, states = lax.associative_scan(_ssm_combine, (a_seq, bu), axis=1)
    c_mat = lax.complex(c_re.astype(jnp.float32), c_im.astype(jnp.float32))
    y = jnp.einsum('gcp,bsgp->bsgc', c_mat, states).real.reshape(bsz, seq, SSM_WIDTH)
    y = y + d_skip.astype(jnp.float32) * uf
    z = jax.nn.gelu(y)
    z = z * jax.nn.sigmoid(z @ w_glu.astype(jnp.float32) + b_glu.astype(jnp.float32))
    return z.astype(u.dtype)


def dsa_mixer(q, k, v, q_idx, k_idx, w_idx, q_gain, k_gain, rel_bias):
    bsz, seq = q.shape[0], q.shape[1]
    topk = min(TOPK_MAX, seq // 4)
    nb = seq // QBLOCK
    scale = HEAD_DIM ** -0.5
    q = rms_norm(q.reshape(bsz, seq, N_HEADS, HEAD_DIM), q_gain)
    k = rms_norm(k.reshape(bsz, seq, N_HEADS, HEAD_DIM), k_gain)
    v = v.reshape(bsz, seq, N_HEADS, HEAD_DIM)
    q_idx = q_idx.reshape(bsz, seq, IDX_HEADS, IDX_DIM)
    w_idx = w_idx * (IDX_HEADS ** -0.5)
    key_pos = jnp.arange(seq, dtype=jnp.int32)

    def to_blocks(t):
        return jnp.moveaxis(t.reshape((bsz, nb, QBLOCK) + t.shape[2:]), 1, 0)

    gather = jax.vmap(lambda src, idx: src[idx])

    def block(args):
        qb, qib, wib, pos = args
        rel = jnp.einsum('bqhd,bsd->bqhs', qib, k_idx)
        score = jnp.einsum('bqh,bqhs->bqs', wib, jax.nn.relu(rel)).astype(jnp.float32)
        causal = key_pos[None, :] <= pos[:, None]
        score = jnp.where(causal[None], score, -jnp.inf)
        _, sel = lax.top_k(score, topk)
        k_sel = gather(k, sel)
        v_sel = gather(v, sel)
        logits = jnp.einsum('bqhd,bqkhd->bqhk', qb, k_sel).astype(jnp.float32) * scale
        dist = pos[None, :, None] - sel
        bias = rel_bias[t5_causal_bucket(jnp.maximum(dist, 0))]
        logits = logits + jnp.transpose(bias, (0, 1, 3, 2)).astype(jnp.float32)
        logits = jnp.where((dist >= 0)[:, :, None, :], logits, -jnp.inf)
        p = jax.nn.softmax(logits, axis=-1).astype(v.dtype)
        return jnp.einsum('bqhk,bqkhd->bqhd', p, v_sel)

    out = lax.map(block, (to_blocks(q), to_blocks(q_idx), to_blocks(w_idx),
                          key_pos.reshape(nb, QBLOCK)))
    return jnp.moveaxis(out, 0, 1).reshape(bsz, seq, ATTN_WIDTH)


def setup_inputs(seed: int = 0) -> dict:
    key = jax.random.key(seed)
    ks = jax.random.split(key, 24)
    f32 = jnp.float32
    x = jax.random.normal(ks[0], (BATCH, SEQ, D_MODEL), f32)
    c = jax.random.normal(ks[1], (BATCH, D_MODEL), f32)
    rel_bias = 0.5 * jax.random.normal(ks[2], (NUM_BUCKETS, N_HEADS), f32)
    norm_g = 1.0 + 0.05 * jax.random.normal(ks[3], (DEPTH, D_MODEL), f32)
    w_ada = 0.5 * D_MODEL ** -0.5 * jax.random.normal(ks[4], (DEPTH, D_MODEL, 3 * D_MODEL), f32)
    b_ada = 0.01 * jax.random.normal(ks[5], (DEPTH, 3 * D_MODEL), f32)
    w_in = D_MODEL ** -0.5 * jax.random.normal(ks[6], (DEPTH, D_MODEL, IN_WIDTH), f32)
    q_gain = 1.0 + 0.05 * jax.random.normal(ks[7], (DEPTH, HEAD_DIM), f32)
    k_gain = 1.0 + 0.05 * jax.random.normal(ks[8], (DEPTH, HEAD_DIM), f32)
    n = jnp.arange(SSM_STATE, dtype=f32)
    a_re = -0.5 + 0.01 * jax.random.normal(ks[9], (DEPTH, SSM_GROUPS, SSM_STATE), f32)
    a_im = math.pi * n + 0.01 * jax.random.normal(ks[10], (DEPTH, SSM_GROUPS, SSM_STATE), f32)
    log_dt = jax.random.uniform(ks[11], (DEPTH, SSM_GROUPS), f32,
                                minval=math.log(DT_MIN), maxval=math.log(DT_MAX))
    b_re = (2 * SSM_GROUP) ** -0.5 * jax.random.normal(ks[12], (DEPTH, SSM_GROUPS, SSM_STATE, SSM_GROUP), f32)
    b_im = (2 * SSM_GROUP) ** -0.5 * jax.random.normal(ks[13], (DEPTH, SSM_GROUPS, SSM_STATE, SSM_GROUP), f32)
    c_re = (2 * SSM_STATE) ** -0.5 * jax.random.normal(ks[14], (DEPTH, SSM_GROUPS, SSM_GROUP, SSM_STATE), f32)
    c_im = (2 * SSM_STATE) ** -0.5 * jax.random.normal(ks[15], (DEPTH, SSM_GROUPS, SSM_GROUP, SSM_STATE), f32)
    d_skip = jax.random.normal(ks[16], (DEPTH, SSM_WIDTH), f32)
    w_glu = SSM_WIDTH ** -0.5 * jax.random.normal(ks[17], (DEPTH, SSM_WIDTH, SSM_WIDTH), f32)
    b_glu = 0.01 * jax.random.normal(ks[18], (DEPTH, SSM_WIDTH), f32)
    w_out = MIX_WIDTH ** -0.5 * jax.random.normal(ks[19], (DEPTH, MIX_WIDTH, D_MODEL), f32)
    return {"x": x, "c": c, "rel_bias": rel_bias, "norm_g": norm_g, "w_ada": w_ada,
            "b_ada": b_ada, "w_in": w_in, "q_gain": q_gain, "k_gain": k_gain,
            "a_re": a_re, "a_im": a_im, "log_dt": log_dt, "b_re": b_re, "b_im": b_im,
            "c_re": c_re, "c_im": c_im, "d_skip": d_skip, "w_glu": w_glu, "b_glu": b_glu,
            "w_out": w_out}


def reference(x, c, rel_bias, norm_g, w_ada, b_ada, w_in, q_gain, k_gain, a_re, a_im,
              log_dt, b_re, b_im, c_re, c_im, d_skip, w_glu, b_glu, w_out):
    split_points = np.cumsum(SPLITS)[:-1].tolist()
    cond = jax.nn.silu(c)
    for l in range(DEPTH):
        mod = cond @ w_ada[l] + b_ada[l]
        shift, scale, gate = jnp.split(mod, 3, axis=-1)
        h = rms_norm(x, norm_g[l]) * (1.0 + scale[:, None, :]) + shift[:, None, :]
        proj = h @ w_in[l]
        ssm_u, ssm_z, q, k, v, attn_z, q_idx, k_idx, w_idx = jnp.split(proj, split_points, axis=-1)
        y_ssm = s5_mixer(ssm_u, a_re[l], a_im[l], log_dt[l], b_re[l], b_im[l], c_re[l], c_im[l],
                         d_skip[l], w_glu[l], b_glu[l]) * jax.nn.silu(ssm_z)
        y_attn = dsa_mixer(q, k, v, q_idx, k_idx, w_idx, q_gain[l], k_gain[l], rel_bias) * jax.nn.silu(attn_z)
        y = jnp.concatenate([y_ssm, y_attn], axis=-1) @ w_out[l]
        x = x + gate[:, None, :] * y
    return x
```

```python
import math
import numpy as np
import concourse.bass as bass
import concourse.mybir as mybir
from concourse.bass_utils import run_bass_kernel_spmd
from contextlib import ExitStack

F32 = mybir.dt.float32
BF16 = mybir.dt.bfloat16
ALU = mybir.AluOpType
AF = mybir.ActivationFunctionType

S = 2048
D = 1024
NSEQ = 4
NT = 16
EPS = 1e-6
NEG = -30000.0
TOPK = 256
NBIS = 12
DEBUG_STAGE = None
NEAR_MAX = 128


class Buf:
    __slots__ = ("w", "r")

    def __init__(self):
        self.w = None
        self.r = {}


class Emitter:
    ENGS = ("pe", "act", "dve", "pool", "sp")

    def __init__(self, nc, es, n_dma_sems=8):
        self.nc = nc
        self.sem = {}
        self.cnt = {}
        self.prog = {e: [] for e in self.ENGS}
        self.waited = {e: {} for e in self.ENGS}
        for e in self.ENGS:
            self.sem[e] = es.enter_context(nc.semaphore("s_" + e))
            self.cnt[e] = 0
        self.dma_keys = []
        for i in range(n_dma_sems):
            k = "dma%d" % i
            self.sem[k] = es.enter_context(nc.semaphore("s_" + k))
            self.cnt[k] = 0
            self.dma_keys.append(k)
        self.dma_rr = 0

    def _deps(self, eng, reads, writes):
        deps = {}

        def add(tok):
            if tok is None:
                return
            k, c = tok
            if deps.get(k, 0) < c:
                deps[k] = c
        for b in reads:
            add(b.w)
        for b in writes:
            if b.w is not None and b.w[0] != eng:
                add(b.w)
            for k, c in b.r.items():
                if k != eng:
                    add((k, c))
        return deps

    def _emit_waits(self, eng, deps):
        w = self.waited[eng]
        for k, c in deps.items():
            if w.get(k, 0) >= c:
                continue
            w[k] = c
            val = c * 16 if k.startswith("dma") else c
            self.prog[eng].append(("wait", self.sem[k], val))

    def _post(self, key, tok, reads, writes):
        for b in reads:
            if b.r.get(key, 0) < tok[1]:
                b.r[key] = tok[1]
        for b in writes:
            b.w = tok
            b.r = {}

    def op(self, eng, fn, reads=(), writes=()):
        self._emit_waits(eng, self._deps(eng, reads, writes))
        self.cnt[eng] += 1
        tok = (eng, self.cnt[eng])
        self.prog[eng].append(("op", fn, self.sem[eng], 1))
        self._post(eng, tok, reads, writes)
        return tok

    def dma(self, out, in_, reads=(), writes=(), q="sp"):
        self._emit_waits(q, self._deps(q, reads, writes))
        k = self.dma_keys[self.dma_rr % len(self.dma_keys)]
        self.dma_rr += 1
        self.cnt[k] += 1
        tok = (k, self.cnt[k])
        self.prog[q].append(("op", (lambda e: e.dma_start(out=out, in_=in_)), self.sem[k], 16))
        self._post(k, tok, reads, writes)
        return tok

    def barrier(self):
        snap = {k: c for k, c in self.cnt.items() if c > 0}
        for e in self.ENGS:
            self._emit_waits(e, dict(snap))

    def replay(self):
        with self.nc.Block() as block:
            def mk(eng):
                def body(e):
                    for item in self.prog[eng]:
                        if item[0] == "wait":
                            e.wait_ge(item[1], item[2])
                        else:
                            item[1](e).then_inc(item[2], item[3])
                return body
            block.tensor(mk("pe"))
            block.scalar(mk("act"))
            block.vector(mk("dve"))
            block.gpsimd(mk("pool"))
            block.sync(mk("sp"))


class _Stop(Exception):
    pass


def build_program(nseq=NSEQ, stage=99):
    nc = bass.Bass("TRN2", target_bir_lowering=False)

    def din(name, shape, dt=F32):
        return nc.dram_tensor(name, list(shape), dt, kind="ExternalInput")

    x_t = din("x", [nseq, S, D]); x = x_t.ap()
    ct = din("ct", [128, 8, 4]).ap()
    wada = din("wada", [128, 8, 3072]).ap()
    bada4 = din("bada4", [4, 3072]).ap()
    ng = din("ng", [128, 8]).ap()
    winf = din("winf", [21, 128, 8, 128]).ap()
    wint = din("wint", [128, 8, 1032]).ap()
    wout = din("wout", [128, 8, 1024]).ap()
    wglu = din("wglu", [128, 4, 512]).ap()
    bglu = din("bglu", [128, 4]).ap()
    dskip = din("dskip", [128, 4]).ap()
    qg2 = din("qg2", [128, 1]).ap()
    kg2 = din("kg2", [128, 1]).ap()
    relb = din("relb", [32, 8]).ap()
    are_c = din("are_c", [128, 16]).ap(); aim_c = din("aim_c", [128, 16]).ap(); ldt_c = din("ldt_c", [128, 16]).ap()
    cre_c = din("cre_c", [128, 16, 16]).ap(); cim_c = din("cim_c", [128, 16, 16]).ap()
    are_r = din("are_r", [128, 256]).ap(); aim_r = din("aim_r", [128, 256]).ap(); ldt_r = din("ldt_r", [128, 256]).ap()
    bre_r = din("bre_r", [128, 256]).ap(); bim_r = din("bim_r", [128, 256]).ap()
    c_ident = din("c_ident", [128, 128]).ap(); c_jrev = din("c_jrev", [128, 128]).ap()
    c_bones = din("c_bones", [128, 128]).ap(); c_tri = din("c_tri", [128, 128]).ap()
    c_sel = din("c_sel", [4, 4, 128]).ap(); c_oh = din("c_oh", [32, 384]).ap()
    c_mrow = din("c_mrow", [128, 8]).ap(); c_mcol = din("c_mcol", [128, 2]).ap()
    out = nc.dram_tensor("out", [nseq, S, D], F32, kind="ExternalOutput").ap()
    tb_t = nc.dram_tensor("tb_d", [8, 384], F32)
    yst_d = nc.dram_tensor("yst_d", [4, 128, S], BF16).ap()
    az_d = nc.dram_tensor("az_d", [S, 512], BF16).ap()

    es = ExitStack()
    with es:
        em = Emitter(nc, es)

        def sb(name, shape, dt=F32):
            return es.enter_context(nc.sbuf_tensor(name, list(shape), dt))

        PS = [es.enter_context(nc.psum_tensor("ps%d" % i, [128, 512], F32)) for i in range(8)]
        bPS = [Buf() for _ in range(8)]

        def MM(o, l, r, st, sp, rd, wr):
            return em.op("pe", lambda e: e.matmul(o, lhsT=l, rhs=r, start=st, stop=sp), rd, wr)

        def TR(o, i, idn, rd, wr):
            return em.op("pe", lambda e: e.transpose(out=o, in_=i, identity=idn), rd, wr)

        def ACT(o, i, func, rd, wr, bias=None, scale=None, accum=None):
            kw = {}
            if bias is not None:
                kw["bias"] = bias
            if scale is not None:
                kw["scale"] = scale
            if accum is not None:
                kw["accum_out"] = accum
            return em.op("act", lambda e: e.activation(out=o, in_=i, func=func, **kw), rd, wr)

        def TS(eng, o, i, s1, s2, op0, op1, rd, wr, accum=None):
            if s2 is None:
                return em.op(eng, lambda e: e.tensor_scalar(out=o, in0=i, scalar1=s1, scalar2=None, op0=op0), rd, wr)
            if accum is None:
                return em.op(eng, lambda e: e.tensor_scalar(out=o, in0=i, scalar1=s1, scalar2=s2, op0=op0, op1=op1), rd, wr)
            return em.op(eng, lambda e: e.tensor_scalar(out=o, in0=i, scalar1=s1, scalar2=s2, op0=op0, op1=op1, accum_out=accum), rd, wr)

        def TT(eng, o, a, b, op, rd, wr):
            return em.op(eng, lambda e: e.tensor_tensor(out=o, in0=a, in1=b, op=op), rd, wr)

        def STT(eng, o, i0, sc, i1, op0, op1, rd, wr):
            return em.op(eng, lambda e: e.scalar_tensor_tensor(out=o, in0=i0, scalar=sc, in1=i1, op0=op0, op1=op1), rd, wr)

        def CP(eng, o, i, rd, wr):
            if eng == "act":
                return em.op("act", lambda e: e.copy(out=o, in_=i), rd, wr)
            return em.op(eng, lambda e: e.tensor_copy(out=o, in_=i), rd, wr)

        def MS(eng, o, v, wr):
            return em.op(eng, lambda e: e.memset(o, v), (), wr)

        NA = 36864
        ARENA = sb("arena", [128, NA], F32)

        def av(off_b, nbytes, dt=F32):
            sl = ARENA[:, off_b // 4:(off_b + nbytes) // 4]
            return sl.bitcast(BF16) if dt == BF16 else sl

        KB = 1024
        IDENT = sb("IDENT", [128, 128]); IDENTB = sb("IDENTB", [128, 128], BF16)
        JB = sb("JB", [128, 128], BF16); BONESB = sb("BONESB", [128, 128], BF16)
        TRI = sb("TRI", [128, 128]); SEL = sb("SEL", [4, 4, 128])
        MROW = sb("MROW", [128, 8]); MCOL = sb("MCOL", [128, 2]); NMCOL = sb("NMCOL", [128, 2])
        MODG = sb("MODG", [4, 1024]); SHSC = sb("SHSC", [128, 16, 4]); AMOD = sb("AMOD", [128, 8, 4])
        NG = sb("NG", [128, 8]); QG = sb("QG", [128, 1]); KG = sb("KG", [128, 1])
        BGLU = sb("BGLU", [128, 4]); DSKIP = sb("DSKIP", [128, 4])
        TREVB = sb("TREVB", [128, 8, 256], BF16)
        WOUTB = sb("WOUTB", [128, 8, 1024], BF16)
        WGLUB = sb("WGLUB", [128, 4, 512], BF16)
        WB = sb("WB", [128, 4, 4, 2, 128], BF16)
        WCP = sb("WCP", [128, 16, 2, 128], BF16)
        DSK = sb("DSK", [128, 4, 128], BF16)
        PR = sb("PR", [128, 17, 16]); PI = sb("PI", [128, 17, 16]); NPI = sb("NPI", [128, 17, 16])
        WI = sb("WI", [128, 16, 8])
        SMALL = sb("SMALL", [128, 64])
        SMALL2 = sb("SMALL2", [128, 32])
        STEPC = sb("STEPC", [128, 32])
        bC = Buf()

        ld = [(IDENT[:], c_ident), (TRI[:], c_tri), (SEL[:], c_sel), (MROW[:], c_mrow), (MCOL[:], c_mcol),
              (NG[:], ng), (QG[:], qg2), (KG[:], kg2), (BGLU[:], bglu), (DSKIP[:], dskip)]
        bL = Buf()
        for o_, i_ in ld:
            em.dma(o_, i_, writes=[bL])
        ST0 = av(0, 12 * KB)[:, 0:3072]; ST1 = av(12 * KB, 12 * KB)[:, 0:3072]
        bST = [Buf(), Buf()]
        TMPF = av(24 * KB, 4 * KB)
        bTMP = Buf()
        em.dma(TMPF[:, 0:128], c_jrev, writes=[bTMP])
        CP("dve", JB[:], TMPF[:, 0:128], [bTMP], [bC, bTMP])
        em.dma(TMPF[:, 128:256], c_bones, writes=[bTMP])
        CP("dve", BONESB[:], TMPF[:, 128:256], [bTMP], [bC, bTMP])
        CP("dve", IDENTB[:], IDENT[:], [bL], [bC])
        TS("dve", NMCOL[:], MCOL[:], -1.0, None, ALU.mult, None, [bL], [bC])
        TS("dve", QG[:], QG[:], 0.125, None, ALU.mult, None, [bL], [bL])

        CT = av(28 * KB, 128)[:, 0:32].rearrange("p (k b) -> p k b", k=8)
        COND = av(29 * KB, 128)[:, 0:32].rearrange("p (k b) -> p k b", k=8)
        bCT = Buf(); bCOND = Buf()
        em.dma(CT, ct, writes=[bCT])
        ACT(COND, CT, AF.Silu, [bCT], [bCOND])
        STs = [ST0, ST1]
        for k in range(8):
            em.dma(STs[k % 2], wada[:, k, :], writes=[bST[k % 2]])
            for blk in range(6):
                MM(PS[blk][0:4, 0:512], COND[:, k, :], STs[k % 2][:, blk * 512:(blk + 1) * 512], k == 0, k == 7,
                   [bCOND, bST[k % 2]], [bPS[blk]])
        MODROW = av(30 * KB, 12 * KB)[0:4, 0:3072]
        BADA = av(42 * KB, 12 * KB)[0:4, 0:3072]
        bMR = Buf(); bBA = Buf()
        em.dma(BADA, bada4, writes=[bBA])
        for blk in range(6):
            TT("dve", MODROW[:, blk * 512:(blk + 1) * 512], PS[blk][0:4, 0:512], BADA[:, blk * 512:(blk + 1) * 512], ALU.add,
               [bPS[blk], bBA], [bMR])
        CP("dve", MODG[:], MODROW[:, 2048:3072], [bMR], [bC])
        for c in range(16):
            TR(PS[6][:, c * 4:(c + 1) * 4], MODROW[0:4, c * 128:(c + 1) * 128], IDENT[0:4, 0:4], [bMR, bL], [bPS[6]])
        CP("dve", SHSC[:].rearrange("p a b -> p (a b)"), PS[6][:, 0:64], [bPS[6]], [bC])
        for b in range(4):
            TS("dve", AMOD[:, :, b], SHSC[:, 8:16, b], 1.0, None, ALU.add, None, [bC], [bC])
            TT("dve", AMOD[:, :, b], AMOD[:, :, b], NG[:], ALU.mult, [bC, bL], [bC])

        RB = av(54 * KB, 32)[0:32, 0:8]
        OHS = av(55 * KB, 1536)[0:32, 0:384]
        WROW = av(57 * KB, 1536)[0:8, 0:384]
        bRB = Buf(); bWR = Buf(); bTBD = Buf(); bTV = Buf()
        em.dma(RB, relb, writes=[bRB]); em.dma(OHS, c_oh, writes=[bRB])
        MM(PS[7][0:8, 0:384], RB, OHS, True, True, [bRB], [bPS[7]])
        CP("dve", WROW, PS[7][0:8, 0:384], [bPS[7]], [bWR])
        em.dma(tb_t.ap(), WROW, reads=[bWR], writes=[bTBD])
        TREVF = av(60 * KB, 8 * KB)[:, 0:2048].rearrange("p (h c) -> p h c", h=8)
        em.dma(TREVF, bass.AP(tb_t, 0, [[1, 128], [384, 8], [1, 256]]), reads=[bTBD], writes=[bTV])
        CP("dve", TREVB[:], TREVF, [bTV], [bC])

        for k in range(8):
            stg = TMPF
            em.dma(stg, wout[:, k, :], writes=[bTMP])
            CP("pool", WOUTB[:, k, :], stg, [bTMP], [bC, bTMP])
        for k in range(4):
            em.dma(TMPF[:, 0:512], wglu[:, k, :], writes=[bTMP])
            CP("pool", WGLUB[:, k, :], TMPF[:, 0:512], [bTMP], [bC, bTMP])

        TWO_PI = 2.0 * math.pi
        MAGIC = 12582912.0

        def s5_derive(n, ARE, AIM, LDT, base_b, tagbufs):
            names = ["DT", "TH", "T1", "ER", "Y", "K1", "R", "SN", "CS", "LR", "LI", "NR", "DEN", "CR", "CI", "T2", "T3"]
            t = {}
            for idx, nm in enumerate(names):
                t[nm] = av(base_b + idx * n * 4, n * 4)[:, 0:n]
            b = tagbufs
            ACT(t["DT"], LDT, AF.Exp, [b], [b])
            TT("dve", t["TH"], AIM, t["DT"], ALU.mult, [b], [b])
            TT("dve", t["T1"], ARE, t["DT"], ALU.mult, [b], [b])
            ACT(t["ER"], t["T1"], AF.Exp, [b], [b])

            def sincos(dst, shift):
                TS("dve", t["T2"], t["TH"], shift, None, ALU.add, None, [b], [b])
                TS("dve", t["Y"], t["T2"], 1.0 / TWO_PI, None, ALU.mult, None, [b], [b])
                TS("dve", t["K1"], t["Y"], MAGIC, None, ALU.add, None, [b], [b])
                TS("dve", t["K1"], t["K1"], MAGIC, None, ALU.subtract, None, [b], [b])
                STT("dve", t["R"], t["K1"], -TWO_PI, t["T2"], ALU.mult, ALU.add, [b], [b])
                TS("dve", t["R"], t["R"], -3.14159, 3.14159, ALU.max, ALU.min, [b], [b])
                ACT(dst, t["R"], AF.Sin, [b], [b])
            sincos(t["SN"], 0.0)
            sincos(t["CS"], math.pi / 2.0)
            TT("dve", t["LR"], t["ER"], t["CS"], ALU.mult, [b], [b])
            TT("dve", t["LI"], t["ER"], t["SN"], ALU.mult, [b], [b])
            TS("dve", t["NR"], t["LR"], -1.0, None, ALU.add, None, [b], [b])
            TT("dve", t["DEN"], ARE, ARE, ALU.mult, [b], [b])
            TT("dve", t["T1"], AIM, AIM, ALU.mult, [b], [b])
            TT("dve", t["DEN"], t["DEN"], t["T1"], ALU.add, [b], [b])
            em.op("dve", lambda e: e.reciprocal(out=t["DEN"], in_=t["DEN"]), [b], [b])
            TT("dve", t["T1"], t["NR"], ARE, ALU.mult, [b], [b])
            TT("dve", t["T3"], t["LI"], AIM, ALU.mult, [b], [b])
            TT("dve", t["T1"], t["T1"], t["T3"], ALU.add, [b], [b])
            TT("dve", t["CR"], t["T1"], t["DEN"], ALU.mult, [b], [b])
            TT("dve", t["T1"], t["LI"], ARE, ALU.mult, [b], [b])
            TT("dve", t["T3"], t["NR"], AIM, ALU.mult, [b], [b])
            TT("dve", t["T1"], t["T1"], t["T3"], ALU.subtract, [b], [b])
            TT("dve", t["CI"], t["T1"], t["DEN"], ALU.mult, [b], [b])
            return t

        def cmul(o_re, o_im, a_re, a_im, b_re, b_im, t1, t2, b):
            TT("dve", t1, a_re, b_re, ALU.mult, [b], [b])
            TT("dve", t2, a_im, b_im, ALU.mult, [b], [b])
            TT("dve", o_re_tmp_holder[0], t1, t2, ALU.subtract, [b], [b])
            TT("dve", t1, a_re, b_im, ALU.mult, [b], [b])
            TT("dve", t2, a_im, b_re, ALU.mult, [b], [b])
            TT("dve", o_im, t1, t2, ALU.add, [b], [b])
            CP("dve", o_re, o_re_tmp_holder[0], [b], [b])

        bS5 = Buf()
        PC = av(70 * KB, 3 * 64)
        em.dma(PC[:, 0:16], are_c, writes=[bS5]); em.dma(PC[:, 16:32], aim_c, writes=[bS5]); em.dma(PC[:, 32:48], ldt_c, writes=[bS5])
        tc_ = s5_derive(16, PC[:, 0:16], PC[:, 16:32], PC[:, 32:48], 71 * KB, bS5)
        CT1 = av(74 * KB, 64)[:, 0:16]; CT2 = av(74 * KB + 64, 64)[:, 0:16]; CT3 = av(74 * KB + 128, 64)[:, 0:16]
        o_re_tmp_holder = [CT3]
        CP("dve", PR[:, 1, :], tc_["LR"], [bS5], [bS5]); CP("dve", PI[:, 1, :], tc_["LI"], [bS5], [bS5])
        for k in range(2, 9):
            cmul(PR[:, k, :], PI[:, k, :], PR[:, k - 1, :], PI[:, k - 1, :], PR[:, 1, :], PI[:, 1, :], CT1, CT2, bS5)
        CP("dve", PR[:, 9, :], PR[:, 8, :], [bS5], [bS5]); CP("dve", PI[:, 9, :], PI[:, 8, :], [bS5], [bS5])
        for k in range(10, 17):
            cmul(PR[:, k, :], PI[:, k, :], PR[:, k - 1, :], PI[:, k - 1, :], PR[:, k - 1, :], PI[:, k - 1, :], CT1, CT2, bS5)
        MS("dve", PR[:, 0, :], 1.0, [bS5]); MS("dve", PI[:, 0, :], 0.0, [bS5])
        TS("dve", NPI[:].rearrange("p a b -> p (a b)"), PI[:].rearrange("p a b -> p (a b)"), -1.0, None, ALU.mult, None, [bS5], [bS5])
        CRE = av(75 * KB, 1024)[:, 0:256].rearrange("p (q c) -> p q c", q=16)
        CIM = av(76 * KB, 1024)[:, 0:256].rearrange("p (q c) -> p q c", q=16)
        em.dma(CRE, cre_c, writes=[bS5]); em.dma(CIM, cim_c, writes=[bS5])
        MS("dve", WCP[:].rearrange("p a b c -> p (a b c)"), 0.0, [bS5])
        WCPv = WCP[:].rearrange("p (qh qq) r c -> p qh qq r c", qq=4)
        CREv = CRE.rearrange("p (qh qq) c -> p qh qq c", qq=4)
        CIMv = CIM.rearrange("p (qh qq) c -> p qh qq c", qq=4)
        for qq in range(4):
            for g2 in range(2):
                c0 = 32 * qq + 16 * g2
                TS("dve", WCPv[:, :, qq, 0, c0:c0 + 16], CREv[:, :, qq, :], MCOL[:, g2:g2 + 1], None, ALU.mult, None, [bS5, bL], [bS5])
                TS("dve", WCPv[:, :, qq, 1, c0:c0 + 16], CIMv[:, :, qq, :], NMCOL[:, g2:g2 + 1], None, ALU.mult, None, [bS5, bC], [bS5])
        PRW = av(77 * KB, 5 * KB)
        em.dma(PRW[:, 0:256], are_r, writes=[bS5]); em.dma(PRW[:, 256:512], aim_r, writes=[bS5]); em.dma(PRW[:, 512:768], ldt_r, writes=[bS5])
        em.dma(PRW[:, 768:1024], bre_r, writes=[bS5]); em.dma(PRW[:, 1024:1280], bim_r, writes=[bS5])
        tr_ = s5_derive(256, PRW[:, 0:256], PRW[:, 256:512], PRW[:, 512:768], 82 * KB, bS5)
        BBR = av(100 * KB, 1024)[:, 0:256]; BBI = av(101 * KB, 1024)[:, 0:256]
        RT1 = av(102 * KB, 1024)[:, 0:256]; RT2 = av(103 * KB, 1024)[:, 0:256]; RT3 = av(104 * KB, 1024)[:, 0:256]
        o_re_tmp_holder[0] = RT3
        cmul(BBR, BBI, tr_["CR"], tr_["CI"], PRW[:, 768:1024], PRW[:, 1024:1280], RT1, RT2, bS5)
        for qq in range(4):
            for ri, src in enumerate((BBR, BBI)):
                for g2 in range(2):
                    TS("dve", WB[:, qq, :, ri, g2 * 64:(g2 + 1) * 64], src.rearrange("p (k c) -> p k c", k=4),
                       MROW[:, qq * 2 + g2:qq * 2 + g2 + 1], None, ALU.mult, None, [bS5, bL], [bS5])
        for kt in range(4):
            TS("dve", DSK[:, kt, :], IDENT[:], DSKIP[:, kt:kt + 1], None, ALU.mult, None, [bL], [bS5])
        for kq in range(NBIS + 1):
            MS("dve", STEPC[:, kq:kq + 1], -0.25 * (0.5 ** kq), [bC])
        em.barrier()

        O_HT = 0
        O_UT = 32 * KB; O_SZT = 48 * KB; O_X = 64 * KB; O_XB = 96 * KB; O_ZG = 112 * KB; O_WSA = 128 * KB
        O_QT = 32 * KB; O_KT = 48 * KB; O_QIT = 64 * KB; O_KIT = 80 * KB; O_VA = 84 * KB; O_WSB = 101 * KB; O_C2 = 101 * KB

        HT = av(O_HT, 32 * KB, BF16).rearrange("p (k t) -> p k t", k=8)
        UT = av(O_UT, 16 * KB, BF16).rearrange("p (k t) -> p k t", k=4)
        SZT = av(O_SZT, 16 * KB, BF16).rearrange("p (k t) -> p k t", k=4)
        ZG = av(O_ZG, 16 * KB, BF16).rearrange("p (k t) -> p k t", k=4)
        XX = [av(O_X + i * 16 * KB, 16 * KB).rearrange("p (r t) -> p r t", r=2) for i in range(2)]
        XBB = [av(O_XB + i * 8 * KB, 8 * KB, BF16).rearrange("p (r t) -> p r t", r=2) for i in range(2)]
        QT = av(O_QT, 16 * KB, BF16).rearrange("p (k t) -> p k t", k=4)
        KT = av(O_KT, 16 * KB, BF16).rearrange("p (k t) -> p k t", k=4)
        QIT = av(O_QIT, 16 * KB, BF16).rearrange("p (k t) -> p k t", k=4)
        KIT = av(O_KIT, 4 * KB, BF16)
        VA = av(O_VA, 16640, BF16).rearrange("p (j h d) -> p j h d", j=16, h=8)

        for b in range(nseq):
          try:
              if stage == 0:
                  raise _Stop()
              XT = [av(O_WSA + i * 4 * KB, 4 * KB) for i in range(2)]
              XN = av(O_WSA + 8 * KB, 4 * KB)
              JNK = av(O_ZG, 4 * KB)
              bXT = [Buf(), Buf()]; bXN = Buf(); bJ = Buf(); bSS = Buf()
              bHT = [Buf() for _ in range(NT)]
              SS = SMALL[:, 0:1]; RST = SMALL[:, 1:2]
              for tt in range(NT):
                  xt = XT[tt % 2]
                  em.dma(xt, x[b, tt * 128:(tt + 1) * 128, :], writes=[bXT[tt % 2]])
                  ACT(JNK, xt, AF.Square, [bXT[tt % 2], bSS], [bJ, bSS], accum=SS)
                  TS("dve", RST, SS, 1.0 / D, EPS, ALU.mult, ALU.add, [bSS], [bSS])
                  ACT(RST, RST, AF.Sqrt, [bSS], [bSS])
                  em.op("dve", lambda e: e.reciprocal(out=RST, in_=RST), [bSS], [bSS])
                  TS("dve", XN, xt, RST, None, ALU.mult, None, [bXT[tt % 2], bSS], [bXN])
                  for k in range(8):
                      bank = 4 + (k // 4)
                      TR(PS[bank][:, (k % 4) * 128:(k % 4 + 1) * 128], XN[:, k * 128:(k + 1) * 128], IDENT[:], [bXN], [bPS[bank]])
                  for k in range(8):
                      bank = 4 + (k // 4)
                      src = PS[bank][:, (k % 4) * 128:(k % 4 + 1) * 128]
                      dst = HT[:, k, tt * 128:(tt + 1) * 128]
                      if k % 2 == 0:
                          TS("dve", dst, src, AMOD[:, k, b:b + 1], SHSC[:, k, b:b + 1], ALU.mult, ALU.add, [bPS[bank]], [bHT[tt]])
                      else:
                          ACT(dst, src, AF.Identity, [bPS[bank]], [bHT[tt]], bias=SHSC[:, k, b:b + 1], scale=AMOD[:, k, b:b + 1])

              if stage == 1:
                  raise _Stop()
              WSF = [av(O_WSA + i * 4 * KB, 4 * KB).rearrange("p (k c) -> p k c", k=8) for i in range(2)]
              WSH = [av(O_WSA + 8 * KB + i * 2 * KB, 2 * KB, BF16).rearrange("p (k c) -> p k c", k=8) for i in range(2)]
              bWSF = [Buf(), Buf()]; bWSH = [Buf(), Buf()]
              em.barrier()

              def proj_fm(mt, evac, wsf=WSF, wsh=WSH, bwsf=bWSF, bwsh=bWSH, banks=(0, 1, 2, 3)):
                  i2 = mt % 2
                  em.dma(wsf[i2], winf[mt], writes=[bwsf[i2]])
                  CP("pool", wsh[i2], wsf[i2], [bwsf[i2]], [bwsh[i2]])
                  for tb in range(4):
                      bank = banks[tb % len(banks)]
                      for k in range(8):
                          MM(PS[bank][:, :], wsh[i2][:, k, :], HT[:, k, tb * 512:(tb + 1) * 512], k == 0, k == 7,
                             [bwsh[i2]] + bHT[tb * 4:(tb + 1) * 4], [bPS[bank]])
                      evac(mt, tb, bank)

              bUT = [Buf() for _ in range(4)]; bSZT = [Buf() for _ in range(4)]

              def evac_A(mt, tb, bank):
                  if mt < 4:
                      CP("dve", UT[:, mt, tb * 512:(tb + 1) * 512], PS[bank][:, :], [bPS[bank]], [bUT[mt]])
                  else:
                      ACT(SZT[:, mt - 4, tb * 512:(tb + 1) * 512], PS[bank][:, :], AF.Silu, [bPS[bank]], [bSZT[mt - 4]])
              for mt in range(8):
                  proj_fm(mt, evac_A)

              if stage == 2:
                  raise _Stop()
              bXh = [[Buf(), Buf()], [Buf(), Buf()]]
              bXBh = [[Buf(), Buf()], [Buf(), Buf()]]
              bZG = [Buf() for _ in range(4)]
              CZ = [[av(O_WSA + 4 * KB + (e2 * 2 + i) * 2 * KB, 2 * KB).rearrange("p (r m) -> p r m", r=2) for i in range(2)] for e2 in range(2)]
              bCZ = [[Buf(), Buf()], [Buf(), Buf()]]
              T1f = [av(O_WSA + e2 * 2 * KB, 2 * KB)[:, 0:510].rearrange("p (r m) -> p r m", r=2) for e2 in range(2)]
              bT1f = [Buf(), Buf()]
              em.barrier()
              GT = [av(O_XB, 2 * KB), av(O_XB + 2 * KB, 2 * KB), av(O_XB + 4 * KB, 2 * KB)]
              for kt in range(4):
                  for tb in range(4):
                      MM(PS[tb][:, :], DSK[:, kt, :], UT[:, kt, tb * 512:(tb + 1) * 512], True, False, [bUT[kt], bS5], [bPS[tb]])
                  for qq in range(4):
                      q = 4 * kt + qq
                      xi = q % 2
                      X = XX[xi]
                      XBq = XBB[xi]
                      se = "dve"
                      bxh = bXh[xi]; bxbh = bXBh[xi]
                      for ri in range(2):
                          for tb in range(4):
                              bank = 4 + ((ri * 4 + tb) % 4)
                              MM(PS[bank][:, :], WB[64 * (qq // 2):64 * (qq // 2) + 64, qq, kt, ri, :], UT[64 * (qq // 2):64 * (qq // 2) + 64, kt, tb * 512:(tb + 1) * 512],
                                 True, True, [bUT[kt], bS5], [bPS[bank]])
                              ACT(X[:, ri, tb * 512:(tb + 1) * 512], PS[bank][:, :], AF.Copy, [bPS[bank]], [bxh[tb // 2]])
                      bx = bxh
                      bxb = bxbh
                      X8 = X.rearrange("p r (m a) -> p r m a", a=8)
                      X42 = X.rearrange("p r (m a c) -> p r m a c", a=4, c=2)
                      X24 = X.rearrange("p r (m a c) -> p r m a c", a=2, c=4)

                      def cupd3(d_all, s_all, d_re, d_im, s_re, s_im, pk):
                          pr = PR[:, pk, q:q + 1]; pi = PI[:, pk, q:q + 1]; npi = NPI[:, pk, q:q + 1]
                          STT(se, d_all, s_all, pr, d_all, ALU.mult, ALU.add, bx, bx)
                          STT(se, d_re, s_im, npi, d_re, ALU.mult, ALU.add, bx, bx)
                          STT(se, d_im, s_re, pi, d_im, ALU.mult, ALU.add, bx, bx)
                      cupd3(X42[:, :, :, :, 1], X42[:, :, :, :, 0], X42[:, 0, :, :, 1], X42[:, 1, :, :, 1],
                            X42[:, 0, :, :, 0], X42[:, 1, :, :, 0], 1)
                      for j in range(2):
                          cupd3(X24[:, :, :, :, 2 + j], X24[:, :, :, :, 1], X24[:, 0, :, :, 2 + j], X24[:, 1, :, :, 2 + j],
                                X24[:, 0, :, :, 1], X24[:, 1, :, :, 1], j + 1)
                      for j in range(4):
                          cupd3(X8[:, :, :, 4 + j], X8[:, :, :, 3], X8[:, 0, :, 4 + j], X8[:, 1, :, 4 + j],
                                X8[:, 0, :, 3], X8[:, 1, :, 3], j + 1)
                      cz = CZ[xi]; bcz = bCZ[xi]
                      cur = 0
                      CP(se, cz[0][:, :, :], X8[:, :, :, 7], bx, [bcz[0]])
                      for j in range(8):
                          d = 1 << j
                          a_, n_ = cz[cur], cz[1 - cur]
                          ba, bn = bcz[cur], bcz[1 - cur]
                          pk = 9 + j
                          pr = PR[:, pk, q:q + 1]; pi = PI[:, pk, q:q + 1]; npi = NPI[:, pk, q:q + 1]
                          CP("act", n_[:, :, 0:d], a_[:, :, 0:d], [ba], [bn])
                          STT(se, n_[:, :, d:], a_[:, :, 0:256 - d], pr, a_[:, :, d:], ALU.mult, ALU.add, [ba], [bn])
                          STT(se, n_[:, 0, d:], a_[:, 1, 0:256 - d], npi, n_[:, 0, d:], ALU.mult, ALU.add, [ba, bn], [bn])
                          STT(se, n_[:, 1, d:], a_[:, 0, 0:256 - d], pi, n_[:, 1, d:], ALU.mult, ALU.add, [ba, bn], [bn])
                          cur = 1 - cur
                      Zc = cz[cur]; bz = bcz[cur]
                      XB8 = XBq.rearrange("p r (m a) -> p r m a", a=8)
                      CP("act", XB8[:, :, 0, :], X8[:, :, 0, :], bx, bxb)
                      t1 = T1f[xi]; bt = bT1f[xi]
                      for a in range(8):
                          pk = a + 1
                          pr = PR[:, pk, q:q + 1]; pi = PI[:, pk, q:q + 1]; npi = NPI[:, pk, q:q + 1]
                          STT(se, t1, Zc[:, :, 0:255], pr, X8[:, :, 1:256, a], ALU.mult, ALU.add, [bz] + bx, [bt])
                          STT(se, XB8[:, 0, 1:256, a], Zc[:, 1, 0:255], npi, t1[:, 0, :], ALU.mult, ALU.add, [bz, bt], bxb)
                          STT(se, XB8[:, 1, 1:256, a], Zc[:, 0, 0:255], pi, t1[:, 1, :], ALU.mult, ALU.add, [bz, bt], bxb)
                      for ri in range(2):
                          for tb in range(4):
                              MM(PS[tb][:, :], WCP[:, q, ri, :], XBq[:, ri, tb * 512:(tb + 1) * 512], False, (qq == 3 and ri == 1),
                                 [bxbh[tb // 2], bS5], [bPS[tb]])
                  gbs = bXBh[0]
                  for tb in range(4):
                      y = PS[tb][:, :]
                      ACT(GT[0][:, 0:512], y, AF.Square, [bPS[tb]] + gbs, gbs)
                      TS("dve", GT[0][:, 0:512], GT[0][:, 0:512], 0.044715, 1.0, ALU.mult, ALU.add, gbs, gbs)
                      TT("dve", GT[1][:, 0:512], GT[0][:, 0:512], y, ALU.mult, gbs + [bPS[tb]], gbs)
                      ACT(GT[2][:, 0:512], GT[1][:, 0:512], AF.Sigmoid, gbs, gbs, scale=1.5957691216057308)
                      TT("dve", ZG[:, kt, tb * 512:(tb + 1) * 512], GT[2][:, 0:512], y, ALU.mult, gbs + [bPS[tb]], [bZG[kt]])
              em.barrier()
              if stage == 3:
                  raise _Stop()
              YO = [av(O_XB + i * 1 * KB, 1 * KB, BF16) for i in range(2)]
              bYO = [Buf(), Buf()]
              G1 = av(O_XB + 2 * KB, 2 * KB); G2 = av(O_XB + 4 * KB, 2 * KB); bG1 = Buf()
              bYD = Buf()
              cnt = 0
              for nt in range(4):
                  for tb in range(4):
                      bank = 4 + (cnt % 4)
                      for kc in range(4):
                          MM(PS[bank][:, :], WGLUB[:, kc, nt * 128:(nt + 1) * 128], ZG[:, kc, tb * 512:(tb + 1) * 512], kc == 0, kc == 3,
                             bZG + [bC], [bPS[bank]])
                      ACT(G1[:, 0:512], PS[bank][:, :], AF.Sigmoid, [bPS[bank]], [bG1], bias=BGLU[:, nt:nt + 1])
                      TT("dve", G2[:, 0:512], G1[:, 0:512], ZG[:, nt, tb * 512:(tb + 1) * 512], ALU.mult, [bG1] + bZG, [bG1])
                      yo = YO[cnt % 2]
                      TT("dve", yo[:, 0:512], G2[:, 0:512], SZT[:, nt, tb * 512:(tb + 1) * 512], ALU.mult, [bG1, bSZT[nt]], [bYO[cnt % 2]])
                      em.dma(yst_d[nt, :, tb * 512:(tb + 1) * 512], yo[:, 0:512], reads=[bYO[cnt % 2]], writes=[bYD])
                      cnt += 1
              em.barrier()

              if stage == 4:
                  raise _Stop()
              WSF2 = [av(O_WSB + i * 4 * KB, 4 * KB).rearrange("p (k c) -> p k c", k=8) for i in range(2)]
              WSH2 = [av(O_WSB + 8 * KB + i * 2 * KB, 2 * KB, BF16).rearrange("p (k c) -> p k c", k=8) for i in range(2)]
              bWSF2 = [Buf(), Buf()]; bWSH2 = [Buf(), Buf()]
              SQ = av(O_WSB + 12 * KB, 1 * KB, BF16); RS = av(O_WSB + 13 * KB, 2 * KB); bSQ = Buf(); bRS = Buf()
              bQT = Buf(); bKT = Buf(); bQIT = Buf(); bKIT = Buf()

              def evac_B(mt, tb, bank):
                  src = PS[bank][:, :]
                  cs = slice(tb * 512, (tb + 1) * 512)
                  if mt < 16:
                      dst = QT[:, mt - 8, cs] if mt < 12 else KT[:, mt - 12, cs]
                      gn = QG if mt < 12 else KG
                      bb = bQT if mt < 12 else bKT
                      ACT(SQ[:, 0:512], src, AF.Square, [bPS[bank]], [bSQ])
                      MM(PS[5][:, :], BONESB[:], SQ[:, 0:512], True, True, [bSQ, bC], [bPS[5]])
                      TS("dve", RS[:, 0:512], PS[5][:, :], 1.0 / 64.0, EPS, ALU.mult, ALU.add, [bPS[5]], [bRS])
                      ACT(RS[:, 0:512], RS[:, 0:512], AF.Sqrt, [bRS], [bRS])
                      em.op("dve", lambda e: e.reciprocal(out=RS[:, 0:512], in_=RS[:, 0:512]), [bRS], [bRS])
                      STT("dve", dst, src, gn[:, 0:1], RS[:, 0:512], ALU.mult, ALU.mult, [bPS[bank], bRS, bL], [bb])
                  elif mt < 20:
                      CP("dve", QIT[:, mt - 16, cs], src, [bPS[bank]], [bQIT])
                  else:
                      ACT(KIT[:, cs], src, AF.Copy, [bPS[bank]], [bKIT])
              for mt in range(8, 21):
                  proj_fm(mt, evac_B, WSF2, WSH2, bWSF2, bWSH2, banks=(0, 1, 2, 3))
              WTS = av(O_WSB, 4 * KB)
              WTB = av(O_WSB + 4 * KB, 8 * KB, BF16).rearrange("p (k c) -> p k c", k=8)
              bWTS = Buf(); bWTB = Buf(); bVA = Buf(); bAZ = Buf(); bWI = Buf(); bAZD = Buf()
              em.barrier()
              MS("dve", VA[:, :, :, 64:65], 1.0, [bVA])
              AZO = [av(O_WSB + 12 * KB + i * KB, 1 * KB, BF16) for i in range(2)]; bAZO = [Buf(), Buf()]
              for part in range(3):
                  ncol = 512 if part < 2 else 8
                  c0 = part * 512
                  for k in range(8):
                      em.dma(WTS[:, 0:ncol], wint[:, k, c0:c0 + ncol], writes=[bWTS])
                      CP("pool", WTB[:, k, 0:ncol], WTS[:, 0:ncol], [bWTS], [bWTB, bWTS])
                  for tt in range(NT):
                      bank = tt % 4
                      for k in range(8):
                          MM(PS[bank][:, 0:ncol], HT[:, k, tt * 128:(tt + 1) * 128], WTB[:, k, 0:ncol], k == 0, k == 7,
                             [bWTB, bHT[tt]], [bPS[bank]])
                      if part == 0:
                          CP("dve", VA[:, tt, :, 0:64], PS[bank][:, :].rearrange("p (h d) -> p h d", h=8), [bPS[bank]], [bVA])
                      elif part == 1:
                          ACT(AZO[tt % 2][:, 0:512], PS[bank][:, :], AF.Silu, [bPS[bank]], [bAZO[tt % 2]])
                          em.dma(az_d[tt * 128:(tt + 1) * 128, :], AZO[tt % 2][:, 0:512], reads=[bAZO[tt % 2]], writes=[bAZD])
                      else:
                          TS("dve", WI[:, tt, :], PS[bank][:, 0:8], 8.0 ** -0.5, None, ALU.mult, None, [bPS[bank]], [bWI])
              em.barrier()

              if stage == 5:
                  raise _Stop()
              SCs = [av(0, 8 * KB), av(O_C2 + 16 * KB, 8 * KB)]
              NM = av(8 * KB, 4 * KB, BF16)
              NMT = [av(12 * KB + i * 4 * KB, 4 * KB, BF16).rearrange("p (j t) -> p j t", j=16) for i in range(2)]
              PT = [av(20 * KB + i * KB, 1 * KB, BF16).rearrange("p (h t) -> p h t", h=4) for i in range(4)]
              TMPH = [av(24 * KB + i * 2 * KB, 2 * KB) for i in range(2)]
              XR_ = av(28 * KB, 4 * KB)
              OT = av(O_C2, 4 * KB)
              GATEB = av(O_C2 + 4 * KB, 4 * KB)
              CAT = av(O_C2 + 8 * KB, 2 * KB, BF16).rearrange("p (k t) -> p k t", k=8)
              AZT = av(O_C2 + 10 * KB, 1 * KB, BF16)
              YA = av(O_C2 + 11 * KB, 2 * KB)
              YAB = av(O_C2 + 13 * KB, 1 * KB, BF16)
              RD = SMALL[:, 8:16]
              bSCs = [Buf(), Buf()]; bBIs = [Buf(), Buf()]; bNM = Buf(); bNMT = [Buf(), Buf()]; bPT = [Buf() for _ in range(4)]; bTMPH = [Buf(), Buf()]
              bXR = Buf(); bOT = Buf(); bGB = Buf(); bCAT = Buf(); bAZT = Buf(); bYA = Buf(); bBI = Buf(); bRD = Buf()
              for hh in range(2):
                  MM(PS[6 + hh][:, :], SEL[0:4, b, :], MODG[0:4, hh * 512:(hh + 1) * 512], True, True, [bC, bL], [bPS[6 + hh]])
                  CP("dve", GATEB[:, hh * 512:(hh + 1) * 512], PS[6 + hh][:, :], [bPS[6 + hh]], [bGB])
              LO = SMALL[:, 16:17]; HI = SMALL[:, 17:18]; TAU = SMALL[:, 18:19]; CNTc = SMALL[:, 19:20]; TF = SMALL[:, 20:21]
              STEPS = SMALL[:, 24:24 + NBIS + 1]
              NTAU = SMALL[:, 21:22]; SSUM = SMALL[:, 22:23]; TSG = SMALL[:, 23:24]
              STEPN = SMALL[:, 24:24 + NBIS + 1]
              ptc = [0]
              PSB6 = PS[6].bitcast(BF16); PSB7 = PS[7].bitcast(BF16)

              SM2 = [SMALL[:, 16:16 + 24], SMALL2[:, 0:24]]

              def scal(i):
                  sm = SM2[i % 2]
                  return dict(LO=sm[:, 0:1], HI=sm[:, 1:2], TAU=sm[:, 2:3], TF=sm[:, 3:4], NTAU=sm[:, 4:5], SSUM=sm[:, 5:6],
                              TSG=sm[:, 6:7], STEPN=sm[:, 8:8 + NBIS + 1])

              def fidx(i):
                  nk = (i + 1) * 128
                  ngr = (nk + 511) // 512
                  SC = SCs[i % 2]; bSC = bSCs[i % 2]; bBI = bBIs[i % 2]; v = scal(i)
                  for g in range(ngr):
                      k0 = g * 512; kn = min(512, nk - k0)
                      for h in range(8):
                          bank = h % 2
                          base = 64 * (h % 2)
                          MM(PS[bank][:, 0:kn], QIT[base:base + 64, h // 2, i * 128:(i + 1) * 128], KIT[base:base + 64, k0:k0 + kn],
                             True, True, [bQIT, bKIT], [bPS[bank]])
                          if h == 0:
                              TS("dve", SC[:, k0:k0 + kn], PS[bank][:, 0:kn], 0.0, WI[:, i, h:h + 1], ALU.max, ALU.mult,
                                 [bPS[bank], bWI], [bSC])
                          else:
                              th = TMPH[h % 2]
                              TS("dve", th[:, 0:kn], PS[bank][:, 0:kn], 0.0, WI[:, i, h:h + 1], ALU.max, ALU.mult,
                                 [bPS[bank], bWI], [bTMPH[h % 2]])
                              TT("pool", SC[:, k0:k0 + kn], SC[:, k0:k0 + kn], th[:, 0:kn], ALU.add, [bTMPH[h % 2], bSC], [bSC])
                          yield
                  HI, LO, TF, NTAU, STEPN = v["HI"], v["LO"], v["TF"], v["NTAU"], v["STEPN"]
                  em.op("dve", lambda e: e.tensor_reduce(out=HI, in_=SC[:, 0:nk], axis=mybir.AxisListType.X, op=ALU.max), [bSC], [bBI])
                  em.op("dve", lambda e: e.tensor_reduce(out=LO, in_=SC[:, 0:nk], axis=mybir.AxisListType.X, op=ALU.min), [bSC], [bBI])
                  TT("dve", SC[:, i * 128:(i + 1) * 128], SC[:, i * 128:(i + 1) * 128], TRI[:], ALU.add, [bSC, bL], [bSC])
                  TT("dve", TF, HI, LO, ALU.subtract, [bBI], [bBI])
                  STT("dve", LO, TF, -0.01, LO, ALU.mult, ALU.add, [bBI], [bBI])
                  TT("dve", TF, HI, LO, ALU.subtract, [bBI], [bBI])
                  STT("dve", NTAU, TF, -0.5, LO, ALU.mult, ALU.subtract, [bBI], [bBI])
                  TS("dve", STEPN, STEPC[:, 0:NBIS + 1], TF, None, ALU.mult, None, [bBI, bC], [bBI])
                  yield

              def n_fidx(i):
                  return (((i + 1) * 128 + 511) // 512) * 8 + 1

              def fbis(i):
                  nk = (i + 1) * 128
                  SC = SCs[i % 2]; bSC = bSCs[i % 2]; bBI = bBIs[i % 2]; v = scal(i)
                  NTAU, SSUM, TSG, STEPN, TAU = v["NTAU"], v["SSUM"], v["TSG"], v["STEPN"], v["TAU"]
                  for kq in range(NBIS):
                      ACT(NM[:, 0:nk], SC[:, 0:nk], AF.Sign, [bSC, bBI], [bNM, bBI], bias=NTAU, scale=1.0, accum=SSUM)
                      ACT(TSG, SSUM, AF.Sign, [bBI], [bBI], bias=float(nk) - (2.0 * TOPK - 0.5), scale=1.0)
                      ACT(NTAU, TSG, AF.Identity, [bBI], [bBI], bias=NTAU, scale=STEPN[:, kq:kq + 1])
                      yield
                  STT("dve", TAU, NTAU, -1.0, STEPN[:, NBIS:NBIS + 1], ALU.mult, ALU.add, [bBI], [bBI])
                  em.op("dve", lambda e: e.tensor_scalar(out=NM[:, 0:nk], in0=SC[:, 0:nk], scalar1=TAU, scalar2=NEG,
                                                         op0=ALU.is_lt, op1=ALU.mult), [bSC, bBI], [bNM])
                  nmt = NMT[i % 2]; bnmt = bNMT[i % 2]
                  for j0 in range(0, i + 1, 8):
                      jn = min(8, i + 1 - j0)
                      for jj in range(jn):
                          j = j0 + jj
                          TR(PSB6[:, jj * 128:(jj + 1) * 128], NM[:, j * 128:(j + 1) * 128], IDENTB[:], [bNM, bC], [bPS[6]])
                      CP("dve", nmt[:, j0:j0 + jn, :], PSB6[:, 0:jn * 128].rearrange("p (j t) -> p j t", j=jn), [bPS[6]], [bnmt])
                  yield

              def n_fbis(i):
                  return NBIS + 1

              def attn(i):
                  nmt = NMT[i % 2]; bnmt = bNMT[i % 2]
                  for h in range(8):
                      base = 64 * (h % 2); ch = h // 2; ob = 4 + h // 4; hh = h % 4
                      for j0 in range(0, i + 1, 4):
                          jn = min(4, i + 1 - j0)
                          bank = 2 + (ptc[0] % 2)
                          pt = PT[ptc[0] % 4]; bpt = bPT[ptc[0] % 4]; ptc[0] += 1
                          for jj in range(jn):
                              j = j0 + jj
                              dlt = (i - j) * 128
                              reg = PS[bank][:, jj * 128:(jj + 1) * 128]
                              MM(reg, IDENTB[:], nmt[:, j, :], True, False, [bnmt, bC], [bPS[bank]])
                              if dlt <= NEAR_MAX:
                                  MM(reg, JB[:], TREVB[:, h, dlt:dlt + 128], False, False, [bC], [bPS[bank]])
                              MM(reg, KT[base:base + 64, ch, j * 128:(j + 1) * 128],
                                 QT[base:base + 64, ch, i * 128:(i + 1) * 128], False, True, [bQT, bKT], [bPS[bank]])
                          ACT(pt[:, 0:jn, :], PS[bank][:, 0:jn * 128].rearrange("p (h t) -> p h t", h=jn), AF.Exp, [bPS[bank]], [bpt])
                          for jj in range(jn):
                              j = j0 + jj
                              MM(PS[ob][:, hh * 65:(hh + 1) * 65], pt[:, jj, :], VA[:, j, h, :], j == 0, j == i, [bpt, bVA], [bPS[ob]])
                          yield

              def n_attn(i):
                  return 8 * ((i + 4) // 4)

              def back(i):
                  em.dma(AZT[:, 0:512], az_d[i * 128:(i + 1) * 128, :], reads=[bAZD], writes=[bAZT])
                  em.dma(CAT[:, 0:4, :], yst_d[:, :, i * 128:(i + 1) * 128].rearrange("k p t -> p k t"), reads=[bYD], writes=[bCAT])
                  em.dma(XR_, x[b, i * 128:(i + 1) * 128, :], writes=[bXR])
                  for g in range(2):
                      O3 = PS[4 + g][:, 0:260].rearrange("p (h d) -> p h d", h=4)
                      em.op("dve", lambda e, O3=O3, g=g: e.reciprocal(out=RD[:, 4 * g:4 * g + 4], in_=O3[:, :, 64]), [bPS[4 + g]], [bRD])
                      TT("dve", YA[:, g * 256:(g + 1) * 256].rearrange("p (h d) -> p h d", h=4), O3[:, :, 0:64],
                         RD[:, 4 * g:4 * g + 4].unsqueeze(2).to_broadcast([128, 4, 64]), ALU.mult, [bPS[4 + g], bRD], [bYA])
                  TT("dve", YAB[:, 0:512], YA[:, 0:512], AZT[:, 0:512], ALU.mult, [bYA, bAZT], [bYA])
                  for c in range(4):
                      TR(PSB7[:, c * 128:(c + 1) * 128], YAB[:, c * 128:(c + 1) * 128], IDENTB[:], [bYA, bC], [bPS[7]])
                  CP("dve", CAT[:, 4:8, :], PSB7[:, 0:512].rearrange("p (c t) -> p c t", c=4), [bPS[7]], [bCAT])
                  for hh in range(2):
                      bank = 7 - hh
                      for k in range(8):
                          MM(PS[bank][:, :], CAT[:, k, :], WOUTB[:, k, hh * 512:(hh + 1) * 512], k == 0, k == 7, [bCAT, bC], [bPS[bank]])
                      TT("dve", OT[:, hh * 512:(hh + 1) * 512], PS[bank][:, :], GATEB[:, hh * 512:(hh + 1) * 512], ALU.mult,
                         [bPS[bank], bGB], [bOT])
                      TT("dve", OT[:, hh * 512:(hh + 1) * 512], OT[:, hh * 512:(hh + 1) * 512], XR_[:, hh * 512:(hh + 1) * 512], ALU.add,
                         [bOT, bXR], [bOT])
                  em.dma(out[b, i * 128:(i + 1) * 128, :], OT, reads=[bOT], writes=[bOT])

              def merge(streams):
                  prog = [0] * len(streams)
                  while True:
                      best = None
                      for idx, (g, n) in enumerate(streams):
                          if prog[idx] < n and (best is None or prog[idx] * streams[best][1] < prog[best] * n):
                              best = idx
                      if best is None:
                          break
                      next(streams[best][0], None)
                      prog[best] += 1
                  for g, n in streams:
                      for _ in g:
                          pass

              merge([(fidx(0), n_fidx(0))])
              merge([(fidx(1), n_fidx(1)), (fbis(0), n_fbis(0))])
              for i in range(NT):
                  st = [(attn(i), n_attn(i))]
                  if i + 2 < NT:
                      st.append((fidx(i + 2), n_fidx(i + 2)))
                  if i + 1 < NT:
                      st.append((fbis(i + 1), n_fbis(i + 1)))
                  merge(st)
                  back(i)
              em.barrier()
          except _Stop:
            break
        em.barrier()
        em.replay()
    return nc


def t5_bucket(dist):
    max_exact = 16
    d = np.maximum(dist, 1).astype(np.float32)
    large = max_exact + (np.log(d / max_exact) / np.float32(math.log(128 / max_exact)) * (32 - max_exact)).astype(np.int32)
    large = np.minimum(large, 31)
    return np.where(dist < max_exact, dist, large)


def host_consts():
    c = {}
    c["c_ident"] = np.eye(128, dtype=np.float32)
    c["c_jrev"] = np.ascontiguousarray(np.eye(128, dtype=np.float32)[::-1])
    bo = np.zeros((128, 128), np.float32); bo[:64, :64] = 1; bo[64:, 64:] = 1
    c["c_bones"] = bo
    t = np.arange(128)
    c["c_tri"] = np.where(t[None, :] <= t[:, None], 0.0, -1e9).astype(np.float32)
    sel = np.zeros((4, 4, 128), np.float32)
    for b in range(4):
        sel[b, b, :] = 1
    c["c_sel"] = sel
    oh = np.zeros((32, 384), np.float32)
    for i in range(127, 384):
        d = i - 127
        oh[t5_bucket(np.array([d]))[0], i] += 1.0
        oh[31, i] -= 1.0
    c["c_oh"] = oh
    p = np.arange(128)
    mrow = np.zeros((128, 8), np.float32); mrow[p, (p // 32) * 2 + ((p // 16) % 2)] = 1
    mcol = np.zeros((128, 2), np.float32); mcol[p, p // 64] = 1
    c["c_mrow"] = mrow; c["c_mcol"] = mcol
    return c


def host_layout(inp):
    f = lambda a: np.ascontiguousarray(a, dtype=np.float32)
    m = {}
    m["wada"] = f(inp["w_ada"][0].reshape(8, 128, 3072).transpose(1, 0, 2))
    m["bada4"] = f(np.broadcast_to(inp["b_ada"][0][None, :], (4, 3072)))
    m["ng"] = f(inp["norm_g"][0].reshape(8, 128).T)
    w = inp["w_in"][0]
    cols_fm = np.concatenate([np.arange(0, 512), np.arange(512, 1024), np.arange(1024, 1536), np.arange(1536, 2048),
                              np.arange(3072, 3584), np.arange(3584, 3648), np.arange(3584, 3648)])
    wf = w[:, cols_fm]
    m["winf"] = f(wf.reshape(8, 128, 21, 128).transpose(2, 1, 0, 3))
    cols_tm = np.concatenate([np.arange(2048, 2560), np.arange(2560, 3072), np.arange(3648, 3656)])
    m["wint"] = f(w[:, cols_tm].reshape(8, 128, 1032).transpose(1, 0, 2))
    m["wout"] = f(inp["w_out"][0].reshape(8, 128, 1024).transpose(1, 0, 2))
    m["wglu"] = f(inp["w_glu"][0].reshape(4, 128, 512).transpose(1, 0, 2))
    m["bglu"] = f(inp["b_glu"][0].reshape(4, 128).T)
    m["dskip"] = f(inp["d_skip"][0].reshape(4, 128).T)
    m["qg2"] = f(np.tile(inp["q_gain"][0], 2).reshape(128, 1))
    m["kg2"] = f(np.tile(inp["k_gain"][0], 2).reshape(128, 1))
    m["relb"] = f(inp["rel_bias"])
    are, aim, ldt = inp["a_re"][0], inp["a_im"][0], inp["log_dt"][0]
    col = lambda a: f(a.reshape(16, 2, 64).transpose(1, 2, 0).reshape(128, 16))
    m["are_c"] = col(are); m["aim_c"] = col(aim); m["ldt_c"] = col(np.repeat(ldt[:, None], 64, 1))
    ccol = lambda a: f(a.reshape(16, 2, 16, 64).transpose(1, 3, 0, 2).reshape(128, 16, 16))
    m["cre_c"] = ccol(inp["c_re"][0]); m["cim_c"] = ccol(inp["c_im"][0])
    row = lambda a: f(np.repeat(a.reshape(4, 8, 64).transpose(1, 0, 2)[:, None, :, :], 16, 1).reshape(128, 256))
    m["are_r"] = row(are); m["aim_r"] = row(aim); m["ldt_r"] = row(np.repeat(ldt[:, None], 64, 1))
    brow = lambda a: f(a.reshape(4, 8, 64, 16).transpose(1, 3, 0, 2).reshape(128, 256))
    m["bre_r"] = brow(inp["b_re"][0]); m["bim_r"] = brow(inp["b_im"][0])
    m.update(host_consts())
    return m


_NC_CACHE = {}


def kernel(**inputs):
    inp = {k: np.asarray(v) for k, v in inputs.items()}
    n_cores = 8
    shared = host_layout(inp)
    x = np.ascontiguousarray(inp["x"], dtype=np.float32)
    c = np.asarray(inp["c"], dtype=np.float32)
    in_maps = []
    for r in range(n_cores):
        m = dict(shared)
        m["x"] = x[r * NSEQ:(r + 1) * NSEQ]
        m["ct"] = np.ascontiguousarray(c[r * NSEQ:(r + 1) * NSEQ].T.reshape(8, 128, NSEQ).transpose(1, 0, 2))
        in_maps.append(m)
    if "nc" not in _NC_CACHE:
        _NC_CACHE["nc"] = build_program()
    res = run_bass_kernel_spmd(_NC_CACHE["nc"], in_maps, core_ids=list(range(n_cores)))
    outs = [np.asarray(res.results[r]["out"]) for r in range(n_cores)]
    return np.concatenate(outs, axis=0).astype(np.float32)
```

```python
import math
import numpy as np
import concourse.bass as bass
import concourse.mybir as mybir
from concourse.bass_utils import run_bass_kernel_spmd
from contextlib import ExitStack

F32 = mybir.dt.float32
BF16 = mybir.dt.bfloat16
ALU = mybir.AluOpType
AF = mybir.ActivationFunctionType

S = 2048
D = 1024
NSEQ = 4
NT = 16
EPS = 1e-6
NEG = -30000.0
TOPK = 256
NBIS = 12
DEBUG_STAGE = None
NEAR_MAX = 128


class Buf:
    __slots__ = ("w", "r")

    def __init__(self):
        self.w = None
        self.r = {}


class Emitter:
    ENGS = ("pe", "act", "dve", "pool", "sp")

    def __init__(self, nc, es, n_dma_sems=8):
        self.nc = nc
        self.sem = {}
        self.cnt = {}
        self.prog = {e: [] for e in self.ENGS}
        self.waited = {e: {} for e in self.ENGS}
        for e in self.ENGS:
            self.sem[e] = es.enter_context(nc.semaphore("s_" + e))
            self.cnt[e] = 0
        self.dma_keys = []
        for i in range(n_dma_sems):
            k = "dma%d" % i
            self.sem[k] = es.enter_context(nc.semaphore("s_" + k))
            self.cnt[k] = 0
            self.dma_keys.append(k)
        self.dma_rr = 0

    def _deps(self, eng, reads, writes):
        deps = {}

        def add(tok):
            if tok is None:
                return
            k, c = tok
            if deps.get(k, 0) < c:
                deps[k] = c
        for b in reads:
            add(b.w)
        for b in writes:
            if b.w is not None and b.w[0] != eng:
                add(b.w)
            for k, c in b.r.items():
                if k != eng:
                    add((k, c))
        return deps

    def _emit_waits(self, eng, deps):
        w = self.waited[eng]
        for k, c in deps.items():
            if w.get(k, 0) >= c:
                continue
            w[k] = c
            val = c * 16 if k.startswith("dma") else c
            self.prog[eng].append(("wait", self.sem[k], val))

    def _post(self, key, tok, reads, writes):
        for b in reads:
            if b.r.get(key, 0) < tok[1]:
                b.r[key] = tok[1]
        for b in writes:
            b.w = tok
            b.r = {}

    def op(self, eng, fn, reads=(), writes=()):
        self._emit_waits(eng, self._deps(eng, reads, writes))
        self.cnt[eng] += 1
        tok = (eng, self.cnt[eng])
        self.prog[eng].append(("op", fn, self.sem[eng], 1))
        self._post(eng, tok, reads, writes)
        return tok

    def dma(self, out, in_, reads=(), writes=(), q="sp"):
        self._emit_waits(q, self._deps(q, reads, writes))
        k = self.dma_keys[self.dma_rr % len(self.dma_keys)]
        self.dma_rr += 1
        self.cnt[k] += 1
        tok = (k, self.cnt[k])
        self.prog[q].append(("op", (lambda e: e.dma_start(out=out, in_=in_)), self.sem[k], 16))
        self._post(k, tok, reads, writes)
        return tok

    def barrier(self):
        snap = {k: c for k, c in self.cnt.items() if c > 0}
        for e in self.ENGS:
            self._emit_waits(e, dict(snap))

    def replay(self):
        with self.nc.Block() as block:
            def mk(eng):
                def body(e):
                    for item in self.prog[eng]:
                        if item[0] == "wait":
                            e.wait_ge(item[1], item[2])
                        else:
                            item[1](e).then_inc(item[2], item[3])
                return body
            block.tensor(mk("pe"))
            block.scalar(mk("act"))
            block.vector(mk("dve"))
            block.gpsimd(mk("pool"))
            block.sync(mk("sp"))


class _Stop(Exception):
    pass


def build_program(nseq=NSEQ, stage=99):
    nc = bass.Bass("TRN2", target_bir_lowering=False)

    def din(name, shape, dt=F32):
        return nc.dram_tensor(name, list(shape), dt, kind="ExternalInput")

    x_t = din("x", [nseq, S, D]); x = x_t.ap()
    ct = din("ct", [128, 8, 4]).ap()
    wada = din("wada", [128, 8, 3072]).ap()
    bada4 = din("bada4", [4, 3072]).ap()
    ng = din("ng", [128, 8]).ap()
    winf = din("winf", [21, 128, 8, 128]).ap()
    wint = din("wint", [128, 8, 1032]).ap()
    wout = din("wout", [128, 8, 1024]).ap()
    wglu = din("wglu", [128, 4, 512]).ap()
    bglu = din("bglu", [128, 4]).ap()
    dskip = din("dskip", [128, 4]).ap()
    qg2 = din("qg2", [128, 1]).ap()
    kg2 = din("kg2", [128, 1]).ap()
    relb = din("relb", [32, 8]).ap()
    are_c = din("are_c", [128, 16]).ap(); aim_c = din("aim_c", [128, 16]).ap(); ldt_c = din("ldt_c", [128, 16]).ap()
    cre_c = din("cre_c", [128, 16, 16]).ap(); cim_c = din("cim_c", [128, 16, 16]).ap()
    are_r = din("are_r", [128, 256]).ap(); aim_r = din("aim_r", [128, 256]).ap(); ldt_r = din("ldt_r", [128, 256]).ap()
    bre_r = din("bre_r", [128, 256]).ap(); bim_r = din("bim_r", [128, 256]).ap()
    bre_c = din("bre_c", [128, 16, 16]).ap(); bim_c = din("bim_c", [128, 16, 16]).ap()
    dsk16 = din("dsk16", [16, 32]).ap()
    c_ident = din("c_ident", [128, 128]).ap(); c_jrev = din("c_jrev", [128, 128]).ap()
    c_bones = din("c_bones", [128, 128]).ap(); c_tri = din("c_tri", [128, 128]).ap()
    c_sel = din("c_sel", [4, 4, 128]).ap(); c_oh = din("c_oh", [32, 384]).ap()
    c_mrow = din("c_mrow", [128, 8]).ap(); c_mcol = din("c_mcol", [128, 2]).ap()
    out = nc.dram_tensor("out", [nseq, S, D], F32, kind="ExternalOutput").ap()
    dbg = nc.dram_tensor("dbg", [128, 1024], F32, kind="ExternalOutput").ap() if DEBUG_STAGE is not None else None
    tb_t = nc.dram_tensor("tb_d", [8, 384], F32)
    yst_d = nc.dram_tensor("yst_d", [4, 128, S], BF16).ap()
    az_d = nc.dram_tensor("az_d", [S, 512], BF16).ap()

    es = ExitStack()
    with es:
        em = Emitter(nc, es)

        def sb(name, shape, dt=F32):
            return es.enter_context(nc.sbuf_tensor(name, list(shape), dt))

        PS = [es.enter_context(nc.psum_tensor("ps%d" % i, [128, 512], F32)) for i in range(8)]
        bPS = [Buf() for _ in range(8)]

        def MM(o, l, r, st, sp, rd, wr):
            return em.op("pe", lambda e: e.matmul(o, lhsT=l, rhs=r, start=st, stop=sp), rd, wr)

        def TR(o, i, idn, rd, wr):
            return em.op("pe", lambda e: e.transpose(out=o, in_=i, identity=idn), rd, wr)

        def ACT(o, i, func, rd, wr, bias=None, scale=None, accum=None):
            kw = {}
            if bias is not None:
                kw["bias"] = bias
            if scale is not None:
                kw["scale"] = scale
            if accum is not None:
                kw["accum_out"] = accum
            return em.op("act", lambda e: e.activation(out=o, in_=i, func=func, **kw), rd, wr)

        def TS(eng, o, i, s1, s2, op0, op1, rd, wr, accum=None):
            if s2 is None:
                return em.op(eng, lambda e: e.tensor_scalar(out=o, in0=i, scalar1=s1, scalar2=None, op0=op0), rd, wr)
            if accum is None:
                return em.op(eng, lambda e: e.tensor_scalar(out=o, in0=i, scalar1=s1, scalar2=s2, op0=op0, op1=op1), rd, wr)
            return em.op(eng, lambda e: e.tensor_scalar(out=o, in0=i, scalar1=s1, scalar2=s2, op0=op0, op1=op1, accum_out=accum), rd, wr)

        def TT(eng, o, a, b, op, rd, wr):
            return em.op(eng, lambda e: e.tensor_tensor(out=o, in0=a, in1=b, op=op), rd, wr)

        def STT(eng, o, i0, sc, i1, op0, op1, rd, wr):
            return em.op(eng, lambda e: e.scalar_tensor_tensor(out=o, in0=i0, scalar=sc, in1=i1, op0=op0, op1=op1), rd, wr)

        def CP(eng, o, i, rd, wr):
            if eng == "act":
                return em.op("act", lambda e: e.copy(out=o, in_=i), rd, wr)
            return em.op(eng, lambda e: e.tensor_copy(out=o, in_=i), rd, wr)

        def MS(eng, o, v, wr):
            return em.op(eng, lambda e: e.memset(o, v), (), wr)

        NA = 36864
        ARENA = sb("arena", [128, NA], F32)

        def av(off_b, nbytes, dt=F32):
            sl = ARENA[:, off_b // 4:(off_b + nbytes) // 4]
            return sl.bitcast(BF16) if dt == BF16 else sl

        KB = 1024
        IDENT = sb("IDENT", [128, 128]); IDENTB = sb("IDENTB", [128, 128], BF16)
        JB = sb("JB", [128, 128], BF16); BONESB = sb("BONESB", [128, 128], BF16)
        TRI = sb("TRI", [128, 128]); SEL = sb("SEL", [4, 4, 128])
        MROW = sb("MROW", [128, 8]); MCOL = sb("MCOL", [128, 2]); NMCOL = sb("NMCOL", [128, 2])
        MODG = sb("MODG", [4, 1024]); SHSC = sb("SHSC", [128, 16, 4]); AMOD = sb("AMOD", [128, 8, 4])
        NG = sb("NG", [128, 8]); QG = sb("QG", [128, 1]); KG = sb("KG", [128, 1])
        BGLU = sb("BGLU", [128, 4]); DSKIP = sb("DSKIP", [128, 4])
        TREVB = sb("TREVB", [128, 8, 256], BF16)
        WOUTB = sb("WOUTB", [128, 8, 1024], BF16)
        WGLUB = sb("WGLUB", [128, 4, 512], BF16)
        WB = sb("WB", [128, 4, 4, 2, 128], BF16)
        WK = sb("WK", [128, 4, 4, 128], BF16)
        WCA = sb("WCA", [128, 16, 4, 2, 32], BF16)
        WCA3 = sb("WCA3", [128, 4, 4, 2, 64], BF16)
        PR = sb("PR", [128, 17, 16]); PI = sb("PI", [128, 17, 16]); NPI = sb("NPI", [128, 17, 16])
        WI = sb("WI", [128, 16, 8])
        SMALL = sb("SMALL", [128, 64])
        SMALL2 = sb("SMALL2", [128, 32])
        STEPC = sb("STEPC", [128, 32])
        bC = Buf()

        ld = [(IDENT[:], c_ident), (TRI[:], c_tri), (SEL[:], c_sel), (MROW[:], c_mrow), (MCOL[:], c_mcol),
              (NG[:], ng), (QG[:], qg2), (KG[:], kg2), (BGLU[:], bglu), (DSKIP[:], dskip)]
        bL = Buf()
        for o_, i_ in ld:
            em.dma(o_, i_, writes=[bL])
        ST0 = av(0, 12 * KB)[:, 0:3072]; ST1 = av(12 * KB, 12 * KB)[:, 0:3072]
        bST = [Buf(), Buf()]
        TMPF = av(24 * KB, 4 * KB)
        bTMP = Buf()
        em.dma(TMPF[:, 0:128], c_jrev, writes=[bTMP])
        CP("dve", JB[:], TMPF[:, 0:128], [bTMP], [bC, bTMP])
        em.dma(TMPF[:, 128:256], c_bones, writes=[bTMP])
        CP("dve", BONESB[:], TMPF[:, 128:256], [bTMP], [bC, bTMP])
        CP("dve", IDENTB[:], IDENT[:], [bL], [bC])
        TS("dve", NMCOL[:], MCOL[:], -1.0, None, ALU.mult, None, [bL], [bC])
        TS("dve", QG[:], QG[:], 0.125, None, ALU.mult, None, [bL], [bL])

        CT = av(28 * KB, 128)[:, 0:32].rearrange("p (k b) -> p k b", k=8)
        COND = av(29 * KB, 128)[:, 0:32].rearrange("p (k b) -> p k b", k=8)
        bCT = Buf(); bCOND = Buf()
        em.dma(CT, ct, writes=[bCT])
        ACT(COND, CT, AF.Silu, [bCT], [bCOND])
        STs = [ST0, ST1]
        for k in range(8):
            em.dma(STs[k % 2], wada[:, k, :], writes=[bST[k % 2]])
            for blk in range(6):
                MM(PS[blk][0:4, 0:512], COND[:, k, :], STs[k % 2][:, blk * 512:(blk + 1) * 512], k == 0, k == 7,
                   [bCOND, bST[k % 2]], [bPS[blk]])
        MODROW = av(30 * KB, 12 * KB)[0:4, 0:3072]
        BADA = av(42 * KB, 12 * KB)[0:4, 0:3072]
        bMR = Buf(); bBA = Buf()
        em.dma(BADA, bada4, writes=[bBA])
        for blk in range(6):
            TT("dve", MODROW[:, blk * 512:(blk + 1) * 512], PS[blk][0:4, 0:512], BADA[:, blk * 512:(blk + 1) * 512], ALU.add,
               [bPS[blk], bBA], [bMR])
        CP("dve", MODG[:], MODROW[:, 2048:3072], [bMR], [bC])
        for c in range(16):
            TR(PS[6][:, c * 4:(c + 1) * 4], MODROW[0:4, c * 128:(c + 1) * 128], IDENT[0:4, 0:4], [bMR, bL], [bPS[6]])
        CP("dve", SHSC[:].rearrange("p a b -> p (a b)"), PS[6][:, 0:64], [bPS[6]], [bC])
        for b in range(4):
            TS("dve", AMOD[:, :, b], SHSC[:, 8:16, b], 1.0, None, ALU.add, None, [bC], [bC])
            TT("dve", AMOD[:, :, b], AMOD[:, :, b], NG[:], ALU.mult, [bC, bL], [bC])

        RB = av(54 * KB, 32)[0:32, 0:8]
        OHS = av(55 * KB, 1536)[0:32, 0:384]
        WROW = av(57 * KB, 1536)[0:8, 0:384]
        bRB = Buf(); bWR = Buf(); bTBD = Buf(); bTV = Buf()
        em.dma(RB, relb, writes=[bRB]); em.dma(OHS, c_oh, writes=[bRB])
        MM(PS[7][0:8, 0:384], RB, OHS, True, True, [bRB], [bPS[7]])
        CP("dve", WROW, PS[7][0:8, 0:384], [bPS[7]], [bWR])
        em.dma(tb_t.ap(), WROW, reads=[bWR], writes=[bTBD])
        TREVF = av(60 * KB, 8 * KB)[:, 0:2048].rearrange("p (h c) -> p h c", h=8)
        em.dma(TREVF, bass.AP(tb_t, 0, [[1, 128], [384, 8], [1, 256]]), reads=[bTBD], writes=[bTV])
        CP("dve", TREVB[:], TREVF, [bTV], [bC])

        for k in range(8):
            stg = TMPF
            em.dma(stg, wout[:, k, :], writes=[bTMP])
            CP("pool", WOUTB[:, k, :], stg, [bTMP], [bC, bTMP])
        for k in range(4):
            em.dma(TMPF[:, 0:512], wglu[:, k, :], writes=[bTMP])
            CP("pool", WGLUB[:, k, :], TMPF[:, 0:512], [bTMP], [bC, bTMP])

        TWO_PI = 2.0 * math.pi
        MAGIC = 12582912.0

        def s5_derive(n, ARE, AIM, LDT, base_b, tagbufs):
            names = ["DT", "TH", "T1", "ER", "Y", "K1", "R", "SN", "CS", "LR", "LI", "NR", "DEN", "CR", "CI", "T2", "T3"]
            t = {}
            for idx, nm in enumerate(names):
                t[nm] = av(base_b + idx * n * 4, n * 4)[:, 0:n]
            b = tagbufs
            ACT(t["DT"], LDT, AF.Exp, [b], [b])
            TT("dve", t["TH"], AIM, t["DT"], ALU.mult, [b], [b])
            TT("dve", t["T1"], ARE, t["DT"], ALU.mult, [b], [b])
            ACT(t["ER"], t["T1"], AF.Exp, [b], [b])

            def sincos(dst, shift):
                TS("dve", t["T2"], t["TH"], shift, None, ALU.add, None, [b], [b])
                TS("dve", t["Y"], t["T2"], 1.0 / TWO_PI, None, ALU.mult, None, [b], [b])
                TS("dve", t["K1"], t["Y"], MAGIC, None, ALU.add, None, [b], [b])
                TS("dve", t["K1"], t["K1"], MAGIC, None, ALU.subtract, None, [b], [b])
                STT("dve", t["R"], t["K1"], -TWO_PI, t["T2"], ALU.mult, ALU.add, [b], [b])
                TS("dve", t["R"], t["R"], -3.14159, 3.14159, ALU.max, ALU.min, [b], [b])
                ACT(dst, t["R"], AF.Sin, [b], [b])
            sincos(t["SN"], 0.0)
            sincos(t["CS"], math.pi / 2.0)
            TT("dve", t["LR"], t["ER"], t["CS"], ALU.mult, [b], [b])
            TT("dve", t["LI"], t["ER"], t["SN"], ALU.mult, [b], [b])
            TS("dve", t["NR"], t["LR"], -1.0, None, ALU.add, None, [b], [b])
            TT("dve", t["DEN"], ARE, ARE, ALU.mult, [b], [b])
            TT("dve", t["T1"], AIM, AIM, ALU.mult, [b], [b])
            TT("dve", t["DEN"], t["DEN"], t["T1"], ALU.add, [b], [b])
            em.op("dve", lambda e: e.reciprocal(out=t["DEN"], in_=t["DEN"]), [b], [b])
            TT("dve", t["T1"], t["NR"], ARE, ALU.mult, [b], [b])
            TT("dve", t["T3"], t["LI"], AIM, ALU.mult, [b], [b])
            TT("dve", t["T1"], t["T1"], t["T3"], ALU.add, [b], [b])
            TT("dve", t["CR"], t["T1"], t["DEN"], ALU.mult, [b], [b])
            TT("dve", t["T1"], t["LI"], ARE, ALU.mult, [b], [b])
            TT("dve", t["T3"], t["NR"], AIM, ALU.mult, [b], [b])
            TT("dve", t["T1"], t["T1"], t["T3"], ALU.subtract, [b], [b])
            TT("dve", t["CI"], t["T1"], t["DEN"], ALU.mult, [b], [b])
            return t

        def cmul(o_re, o_im, a_re, a_im, b_re, b_im, t1, t2, b):
            TT("dve", t1, a_re, b_re, ALU.mult, [b], [b])
            TT("dve", t2, a_im, b_im, ALU.mult, [b], [b])
            TT("dve", o_re_tmp_holder[0], t1, t2, ALU.subtract, [b], [b])
            TT("dve", t1, a_re, b_im, ALU.mult, [b], [b])
            TT("dve", t2, a_im, b_re, ALU.mult, [b], [b])
            TT("dve", o_im, t1, t2, ALU.add, [b], [b])
            CP("dve", o_re, o_re_tmp_holder[0], [b], [b])

        bS5 = Buf()
        PC = av(70 * KB, 3 * 64)
        em.dma(PC[:, 0:16], are_c, writes=[bS5]); em.dma(PC[:, 16:32], aim_c, writes=[bS5]); em.dma(PC[:, 32:48], ldt_c, writes=[bS5])
        tc_ = s5_derive(16, PC[:, 0:16], PC[:, 16:32], PC[:, 32:48], 71 * KB, bS5)
        CT1 = av(74 * KB, 64)[:, 0:16]; CT2 = av(74 * KB + 64, 64)[:, 0:16]; CT3 = av(74 * KB + 128, 64)[:, 0:16]
        o_re_tmp_holder = [CT3]
        CP("dve", PR[:, 1, :], tc_["LR"], [bS5], [bS5]); CP("dve", PI[:, 1, :], tc_["LI"], [bS5], [bS5])
        for k in range(2, 5):
            cmul(PR[:, k, :], PI[:, k, :], PR[:, k - 1, :], PI[:, k - 1, :], PR[:, 1, :], PI[:, 1, :], CT1, CT2, bS5)
        CP("dve", PR[:, 5, :], PR[:, 4, :], [bS5], [bS5]); CP("dve", PI[:, 5, :], PI[:, 4, :], [bS5], [bS5])
        for k in range(6, 14):
            cmul(PR[:, k, :], PI[:, k, :], PR[:, k - 1, :], PI[:, k - 1, :], PR[:, k - 1, :], PI[:, k - 1, :], CT1, CT2, bS5)
        for k in range(14, 17):
            MS("dve", PR[:, k, :], 0.0, [bS5]); MS("dve", PI[:, k, :], 0.0, [bS5])
        MS("dve", PR[:, 0, :], 1.0, [bS5]); MS("dve", PI[:, 0, :], 0.0, [bS5])
        TS("dve", NPI[:].rearrange("p a b -> p (a b)"), PI[:].rearrange("p a b -> p (a b)"), -1.0, None, ALU.mult, None, [bS5], [bS5])
        CRE = av(75 * KB, 1024)[:, 0:256].rearrange("p (q c) -> p q c", q=16)
        CIM = av(76 * KB, 1024)[:, 0:256].rearrange("p (q c) -> p q c", q=16)
        em.dma(CRE, cre_c, writes=[bS5]); em.dma(CIM, cim_c, writes=[bS5])
        PRW = av(77 * KB, 5 * KB)
        em.dma(PRW[:, 0:256], are_r, writes=[bS5]); em.dma(PRW[:, 256:512], aim_r, writes=[bS5]); em.dma(PRW[:, 512:768], ldt_r, writes=[bS5])
        em.dma(PRW[:, 768:1024], bre_r, writes=[bS5]); em.dma(PRW[:, 1024:1280], bim_r, writes=[bS5])
        tr_ = s5_derive(256, PRW[:, 0:256], PRW[:, 256:512], PRW[:, 512:768], 82 * KB, bS5)
        BBR = av(100 * KB, 1024)[:, 0:256]; BBI = av(101 * KB, 1024)[:, 0:256]
        RT1 = av(102 * KB, 1024)[:, 0:256]; RT2 = av(103 * KB, 1024)[:, 0:256]; RT3 = av(104 * KB, 1024)[:, 0:256]
        o_re_tmp_holder[0] = RT3
        cmul(BBR, BBI, tr_["CR"], tr_["CI"], PRW[:, 768:1024], PRW[:, 1024:1280], RT1, RT2, bS5)
        for qq in range(4):
            for ri, src in enumerate((BBR, BBI)):
                for g2 in range(2):
                    TS("dve", WB[:, qq, :, ri, g2 * 64:(g2 + 1) * 64], src.rearrange("p (k c) -> p k c", k=4),
                       MROW[:, qq * 2 + g2:qq * 2 + g2 + 1], None, ALU.mult, None, [bS5, bL], [bS5])
        em.barrier()
        def v3(off):
            return av(off, 1024)[:, 0:256].rearrange("p (q c) -> p q c", q=16)
        BCR = v3(106 * KB); BCI = v3(107 * KB); BBCR = v3(108 * KB); BBCI = v3(109 * KB); TA = v3(110 * KB); TB_ = v3(111 * KB)
        BTR = v3(129 * KB); BTI = v3(130 * KB); NCIM = v3(128 * KB); GAR = v3(131 * KB); GAI = v3(132 * KB)
        em.dma(BCR, bre_c, writes=[bS5]); em.dma(BCI, bim_c, writes=[bS5])

        def bc(ap2d):
            return ap2d.unsqueeze(2).to_broadcast([128, 16, 16])

        def cmul_b(o_re, o_im, a_re, a_im, s_re, s_im):
            TT("dve", TA, a_re, bc(s_re), ALU.mult, [bS5], [bS5])
            TT("dve", TB_, a_im, bc(s_im), ALU.mult, [bS5], [bS5])
            TT("dve", o_re, TA, TB_, ALU.subtract, [bS5], [bS5])
            TT("dve", TA, a_re, bc(s_im), ALU.mult, [bS5], [bS5])
            TT("dve", TB_, a_im, bc(s_re), ALU.mult, [bS5], [bS5])
            TT("dve", o_im, TA, TB_, ALU.add, [bS5], [bS5])
        cmul_b(BBCR, BBCI, BCR, BCI, tc_["CR"], tc_["CI"])
        TS("dve", NCIM, CIM, -1.0, None, ALU.mult, None, [bS5], [bS5])
        BM = {}
        for tau in range(4):
            if tau == 0:
                sr, si = BBCR, BBCI
            else:
                cmul_b(BTR, BTI, BBCR, BBCI, PR[:, tau, :], PI[:, tau, :])
                sr, si = BTR, BTI
            for g2 in range(2):
                for ri, src in enumerate((sr, si)):
                    t_ = v3(112 * KB + ((tau * 2 + g2) * 2 + ri) * KB)
                    TS("dve", t_, src, MCOL[:, g2:g2 + 1], None, ALU.mult, None, [bS5, bL], [bS5])
                    BM[(tau, g2, ri)] = t_
        for g in range(32):
            q_ = g // 2; g2 = g % 2
            for tau in range(4):
                col = (g * 4 + tau) * 16
                blk = PS[col // 512][0:16, col % 512:col % 512 + 16]
                MM(blk, BM[(tau, g2, 0)][:, q_, :], CRE[:, q_, :], True, False, [bS5], [bPS[col // 512]])
                MM(blk, BM[(tau, g2, 1)][:, q_, :], NCIM[:, q_, :], False, True, [bS5], [bPS[col // 512]])
        KST = av(8 * KB, 8 * KB)[0:16, 0:2048]
        bKS = Buf()
        for bk in range(4):
            CP("dve", KST[:, bk * 512:(bk + 1) * 512], PS[bk][0:16, :], [bPS[bk]], [bKS])
        KSTv = KST.rearrange("p (g t c) -> p g t c", g=32, t=4)
        DSK16 = av(18 * KB, 128)[0:16, 0:32]
        TMPK = av(16 * KB, 2 * KB)[0:16, 0:512].rearrange("p (g c) -> p g c", g=32)
        em.dma(DSK16, dsk16, writes=[bKS])
        TT("dve", TMPK, IDENT[0:16, 0:16].unsqueeze(1).to_broadcast([16, 32, 16]), DSK16.unsqueeze(2).to_broadcast([16, 32, 16]),
           ALU.mult, [bKS, bL], [bKS])
        TT("dve", KSTv[:, :, 0, :], KSTv[:, :, 0, :], TMPK, ALU.add, [bKS], [bKS])
        WKF = av(0, 8 * KB).rearrange("p (k t c) -> p k t c", k=4, t=4)
        bWKF = Buf()
        MS("dve", av(0, 8 * KB), 0.0, [bWKF])
        for g in range(32):
            g8 = g % 8
            em.dma(WKF[16 * g8:16 * g8 + 16, g // 8, :, 16 * g8:16 * g8 + 16], KSTv[0:16, g, :, :], reads=[bKS], writes=[bWKF])
        CP("dve", WK[:], WKF, [bWKF], [bS5])
        for a in range(4):
            cmul_b(GAR, GAI, CRE, CIM, PR[:, a + 1, :], PI[:, a + 1, :])
            for g2 in range(2):
                TS("dve", WCA[:, :, a, 0, 16 * g2:16 * g2 + 16], GAR, MCOL[:, g2:g2 + 1], None, ALU.mult, None, [bS5, bL], [bS5])
                TS("dve", WCA[:, :, a, 1, 16 * g2:16 * g2 + 16], GAI, NMCOL[:, g2:g2 + 1], None, ALU.mult, None, [bS5, bC], [bS5])
        MS("dve", WCA3[:].rearrange("p k a r c -> p (k a r c)"), 0.0, [bS5])
        for kt in range(4):
            CP("dve", WCA3[:, kt, :, :, 32:64], WCA[:, 4 * kt + 3, :, :, :], [bS5], [bS5])
        for kq in range(NBIS + 1):
            MS("dve", STEPC[:, kq:kq + 1], -0.25 * (0.5 ** kq), [bC])
        em.barrier()

        O_HT = 0
        O_UT = 32 * KB; O_SZT = 48 * KB; O_X = 64 * KB; O_XB = 96 * KB; O_ZB = 112 * KB; O_ZG = 120 * KB; O_GT = 136 * KB; O_WSA = 96 * KB
        O_QT = 32 * KB; O_KT = 48 * KB; O_QIT = 64 * KB; O_KIT = 80 * KB; O_VA = 84 * KB; O_WSB = 101 * KB; O_C2 = 101 * KB

        HT = av(O_HT, 32 * KB, BF16).rearrange("p (k t) -> p k t", k=8)
        UT = av(O_UT, 16 * KB, BF16).rearrange("p (k t) -> p k t", k=4)
        SZT = av(O_SZT, 16 * KB, BF16).rearrange("p (k t) -> p k t", k=4)
        ZG = av(O_ZG, 16 * KB, BF16).rearrange("p (k t) -> p k t", k=4)
        XX = [av(O_X + i * 16 * KB, 16 * KB).rearrange("p (r t) -> p r t", r=2) for i in range(2)]
        HP = [av(O_XB + j2 * 6 * KB, 6 * KB).rearrange("p (r m) -> p r m", r=2) for j2 in range(2)]
        ZB = av(O_ZB, 8 * KB, BF16).rearrange("p (q r m) -> p q r m", q=4, r=2)
        QT = av(O_QT, 16 * KB, BF16).rearrange("p (k t) -> p k t", k=4)
        KT = av(O_KT, 16 * KB, BF16).rearrange("p (k t) -> p k t", k=4)
        QIT = av(O_QIT, 16 * KB, BF16).rearrange("p (k t) -> p k t", k=4)
        KIT = av(O_KIT, 4 * KB, BF16)
        VA = av(O_VA, 16640, BF16).rearrange("p (j h d) -> p j h d", j=16, h=8)

        for b in range(nseq):
          try:
              if stage == 0:
                  raise _Stop()
              XT = [av(O_WSA + i * 4 * KB, 4 * KB) for i in range(2)]
              XN = av(O_WSA + 8 * KB, 4 * KB)
              JNK = av(O_ZG, 4 * KB)
              bXT = [Buf(), Buf()]; bXN = Buf(); bJ = Buf(); bSS = Buf()
              bHT = [Buf() for _ in range(NT)]
              SS = SMALL[:, 0:1]; RST = SMALL[:, 1:2]
              for tt in range(NT):
                  xt = XT[tt % 2]
                  em.dma(xt, x[b, tt * 128:(tt + 1) * 128, :], writes=[bXT[tt % 2]])
                  ACT(JNK, xt, AF.Square, [bXT[tt % 2], bSS], [bJ, bSS], accum=SS)
                  TS("dve", RST, SS, 1.0 / D, EPS, ALU.mult, ALU.add, [bSS], [bSS])
                  ACT(RST, RST, AF.Sqrt, [bSS], [bSS])
                  em.op("dve", lambda e: e.reciprocal(out=RST, in_=RST), [bSS], [bSS])
                  TS("dve", XN, xt, RST, None, ALU.mult, None, [bXT[tt % 2], bSS], [bXN])
                  for k in range(8):
                      bank = 4 + (k // 4)
                      TR(PS[bank][:, (k % 4) * 128:(k % 4 + 1) * 128], XN[:, k * 128:(k + 1) * 128], IDENT[:], [bXN], [bPS[bank]])
                  for k in range(8):
                      bank = 4 + (k // 4)
                      src = PS[bank][:, (k % 4) * 128:(k % 4 + 1) * 128]
                      dst = HT[:, k, tt * 128:(tt + 1) * 128]
                      if k % 2 == 0:
                          TS("dve", dst, src, AMOD[:, k, b:b + 1], SHSC[:, k, b:b + 1], ALU.mult, ALU.add, [bPS[bank]], [bHT[tt]])
                      else:
                          ACT(dst, src, AF.Identity, [bPS[bank]], [bHT[tt]], bias=SHSC[:, k, b:b + 1], scale=AMOD[:, k, b:b + 1])

              if stage == 1:
                  raise _Stop()
              WSF = [av(O_WSA + i * 4 * KB, 4 * KB).rearrange("p (k c) -> p k c", k=8) for i in range(2)]
              WSH = [av(O_WSA + 8 * KB + i * 2 * KB, 2 * KB, BF16).rearrange("p (k c) -> p k c", k=8) for i in range(2)]
              bWSF = [Buf(), Buf()]; bWSH = [Buf(), Buf()]
              em.barrier()

              def proj_fm(mt, evac, wsf=WSF, wsh=WSH, bwsf=bWSF, bwsh=bWSH, banks=(0, 1, 2, 3)):
                  i2 = mt % 2
                  em.dma(wsf[i2], winf[mt], writes=[bwsf[i2]])
                  CP("pool", wsh[i2], wsf[i2], [bwsf[i2]], [bwsh[i2]])
                  for tb in range(4):
                      bank = banks[tb % len(banks)]
                      for k in range(8):
                          MM(PS[bank][:, :], wsh[i2][:, k, :], HT[:, k, tb * 512:(tb + 1) * 512], k == 0, k == 7,
                             [bwsh[i2]] + bHT[tb * 4:(tb + 1) * 4], [bPS[bank]])
                      evac(mt, tb, bank)

              bUT = [Buf() for _ in range(4)]; bSZT = [Buf() for _ in range(4)]

              def evac_A(mt, tb, bank):
                  if mt < 4:
                      CP("dve", UT[:, mt, tb * 512:(tb + 1) * 512], PS[bank][:, :], [bPS[bank]], [bUT[mt]])
                  else:
                      ACT(SZT[:, mt - 4, tb * 512:(tb + 1) * 512], PS[bank][:, :], AF.Silu, [bPS[bank]], [bSZT[mt - 4]])
              for mt in range(8):
                  proj_fm(mt, evac_A)

              if stage == 2:
                  raise _Stop()
              bXq = [Buf(), Buf()]
              bHq = [[Buf(), Buf()], [Buf(), Buf()]]
              bZB = [Buf() for _ in range(4)]
              bZG = [Buf() for _ in range(4)]
              bGT = Buf()
              em.barrier()
              GT = [av(O_GT, 2 * KB), av(O_GT + 2 * KB, 2 * KB), av(O_GT + 4 * KB, 2 * KB)]
              for j2 in range(2):
                  MS("pool", HP[j2][:, :, 0:256], 0.0, [bHq[0][j2]])
              def emit_bu(q):
                  kt, qq = divmod(q, 4)
                  X = XX[q % 2]; bx = bXq[q % 2]
                  for ri in range(2):
                      for tb in range(4):
                          bank = 4 + ((ri * 4 + tb) % 4)
                          MM(PS[bank][:, :], WB[64 * (qq // 2):64 * (qq // 2) + 64, qq, kt, ri, :], UT[64 * (qq // 2):64 * (qq // 2) + 64, kt, tb * 512:(tb + 1) * 512],
                             True, True, [bUT[kt], bS5], [bPS[bank]])
                          ACT(X[:, ri, tb * 512:(tb + 1) * 512], PS[bank][:, :], AF.Copy, [bPS[bank]], [bx])

              def emit_taps(kt):
                  UT4 = UT[:, kt, :].rearrange("p (m a) -> p m a", a=4)
                  for a in range(4):
                      for tau in range(a + 1):
                          MM(PS[a][:, :], WK[:, kt, tau, :], UT4[:, :, a - tau], tau == 0, False, [bUT[kt], bS5], [bPS[a]])

              def emit_scan(q):
                  kt, qq = divmod(q, 4)
                  X = XX[q % 2]; bx = bXq[q % 2]
                  X4 = X.rearrange("p r (m a) -> p r m a", a=4)
                  pr1 = PR[:, 1, q:q + 1]; pi1 = PI[:, 1, q:q + 1]; npi1 = NPI[:, 1, q:q + 1]
                  H = HP; bh = bHq[0]
                  src = X4[:, :, :, 0]; bsrc = [bx]
                  cur = 0
                  for a in range(1, 4):
                      dst = H[cur][:, :, 256:768]; bd = bh[cur]
                      STT("dve", dst, src, pr1, X4[:, :, :, a], ALU.mult, ALU.add, bsrc + [bx], [bd])
                      STT("dve", dst[:, 0, :], src[:, 1, :], npi1, dst[:, 0, :], ALU.mult, ALU.add, bsrc + [bd], [bd])
                      STT("dve", dst[:, 1, :], src[:, 0, :], pi1, dst[:, 1, :], ALU.mult, ALU.add, bsrc + [bd], [bd])
                      src = dst; bsrc = [bd]
                      cur = 1 - cur
                  cur = 1 - cur
                  for j in range(9):
                      d = 1 << j
                      a_, n_ = H[cur], H[1 - cur]
                      ba, bn = bh[cur], bh[1 - cur]
                      pk = 5 + j
                      pr = PR[:, pk, q:q + 1]; pi = PI[:, pk, q:q + 1]; npi = NPI[:, pk, q:q + 1]
                      STT("dve", n_[:, :, 256:768], a_[:, :, 256 - d:768 - d], pr, a_[:, :, 256:768], ALU.mult, ALU.add, [ba], [bn])
                      STT("dve", n_[:, 0, 256:768], a_[:, 1, 256 - d:768 - d], npi, n_[:, 0, 256:768], ALU.mult, ALU.add, [ba, bn], [bn])
                      STT("dve", n_[:, 1, 256:768], a_[:, 0, 256 - d:768 - d], pi, n_[:, 1, 256:768], ALU.mult, ALU.add, [ba, bn], [bn])
                      cur = 1 - cur
                  CP("act", ZB[:, qq, :, :], H[cur][:, :, 256:768], [bh[cur]], [bZB[qq]])

              def emit_carry(q):
                  kt, qq = divmod(q, 4)
                  for a in range(4):
                      for ri in range(2):
                          last = (qq == 3 and ri == 1)
                          if qq < 3:
                              MM(PS[a][32 * qq:32 * qq + 32, 1:512], WCA[:, q, a, ri, :], ZB[:, qq, ri, 0:511], False, last,
                                 [bZB[qq], bS5], [bPS[a]])
                          else:
                              MM(PS[a][64:128, 1:512], WCA3[:, kt, a, ri, :], ZB[:, qq, ri, 0:511], False, last,
                                 [bZB[qq], bS5], [bPS[a]])

              def emit_gelu(kt):
                  ZG4 = ZG[:, kt, :].rearrange("p (m a) -> p m a", a=4)
                  for a in range(4):
                      y = PS[a][:, :]
                      ACT(GT[0][:, 0:512], y, AF.Square, [bPS[a], bGT], [bGT])
                      TS("dve", GT[0][:, 0:512], GT[0][:, 0:512], 0.044715, 1.0, ALU.mult, ALU.add, [bGT], [bGT])
                      TT("dve", GT[1][:, 0:512], GT[0][:, 0:512], y, ALU.mult, [bGT, bPS[a]], [bGT])
                      ACT(GT[2][:, 0:512], GT[1][:, 0:512], AF.Sigmoid, [bGT], [bGT], scale=1.5957691216057308)
                      TT("dve", ZG4[:, :, a], GT[2][:, 0:512], y, ALU.mult, [bGT, bPS[a]], [bZG[kt]])

              emit_bu(0)
              for q in range(16):
                  kt, qq = divmod(q, 4)
                  if q + 1 < 16:
                      emit_bu(q + 1)
                  if qq == 0:
                      emit_taps(kt)
                  emit_scan(q)
                  emit_carry(q)
                  if qq == 3:
                      emit_gelu(kt)
              em.barrier()
              if stage == 3:
                  raise _Stop()
              YO = [av(O_XB + i * 1 * KB, 1 * KB, BF16) for i in range(2)]
              bYO = [Buf(), Buf()]
              G1 = av(O_XB + 2 * KB, 2 * KB); G2 = av(O_XB + 4 * KB, 2 * KB); bG1 = Buf()
              bYD = Buf()
              cnt = 0
              for nt in range(4):
                  for tb in range(4):
                      bank = 4 + (cnt % 4)
                      for kc in range(4):
                          MM(PS[bank][:, :], WGLUB[:, kc, nt * 128:(nt + 1) * 128], ZG[:, kc, tb * 512:(tb + 1) * 512], kc == 0, kc == 3,
                             bZG + [bC], [bPS[bank]])
                      ACT(G1[:, 0:512], PS[bank][:, :], AF.Sigmoid, [bPS[bank]], [bG1], bias=BGLU[:, nt:nt + 1])
                      TT("dve", G2[:, 0:512], G1[:, 0:512], ZG[:, nt, tb * 512:(tb + 1) * 512], ALU.mult, [bG1] + bZG, [bG1])
                      yo = YO[cnt % 2]
                      TT("dve", yo[:, 0:512], G2[:, 0:512], SZT[:, nt, tb * 512:(tb + 1) * 512], ALU.mult, [bG1, bSZT[nt]], [bYO[cnt % 2]])
                      em.dma(yst_d[nt, :, tb * 512:(tb + 1) * 512], yo[:, 0:512], reads=[bYO[cnt % 2]], writes=[bYD])
                      cnt += 1
              em.barrier()

              if stage == 4:
                  raise _Stop()
              WSF2 = [av(O_WSB + i * 4 * KB, 4 * KB).rearrange("p (k c) -> p k c", k=8) for i in range(2)]
              WSH2 = [av(O_WSB + 8 * KB + i * 2 * KB, 2 * KB, BF16).rearrange("p (k c) -> p k c", k=8) for i in range(2)]
              bWSF2 = [Buf(), Buf()]; bWSH2 = [Buf(), Buf()]
              SQ = av(O_WSB + 12 * KB, 1 * KB, BF16); RS = av(O_WSB + 13 * KB, 2 * KB); bSQ = Buf(); bRS = Buf()
              bQT = Buf(); bKT = Buf(); bQIT = Buf(); bKIT = Buf()

              def evac_B(mt, tb, bank):
                  src = PS[bank][:, :]
                  cs = slice(tb * 512, (tb + 1) * 512)
                  if mt < 16:
                      dst = QT[:, mt - 8, cs] if mt < 12 else KT[:, mt - 12, cs]
                      gn = QG if mt < 12 else KG
                      bb = bQT if mt < 12 else bKT
                      ACT(SQ[:, 0:512], src, AF.Square, [bPS[bank]], [bSQ])
                      MM(PS[5][:, :], BONESB[:], SQ[:, 0:512], True, True, [bSQ, bC], [bPS[5]])
                      TS("dve", RS[:, 0:512], PS[5][:, :], 1.0 / 64.0, EPS, ALU.mult, ALU.add, [bPS[5]], [bRS])
                      ACT(RS[:, 0:512], RS[:, 0:512], AF.Sqrt, [bRS], [bRS])
                      em.op("dve", lambda e: e.reciprocal(out=RS[:, 0:512], in_=RS[:, 0:512]), [bRS], [bRS])
                      STT("dve", dst, src, gn[:, 0:1], RS[:, 0:512], ALU.mult, ALU.mult, [bPS[bank], bRS, bL], [bb])
                  elif mt < 20:
                      CP("dve", QIT[:, mt - 16, cs], src, [bPS[bank]], [bQIT])
                  else:
                      ACT(KIT[:, cs], src, AF.Copy, [bPS[bank]], [bKIT])
              for mt in range(8, 21):
                  proj_fm(mt, evac_B, WSF2, WSH2, bWSF2, bWSH2, banks=(0, 1, 2, 3))
              WTS = av(O_WSB, 4 * KB)
              WTB = av(O_WSB + 4 * KB, 8 * KB, BF16).rearrange("p (k c) -> p k c", k=8)
              bWTS = Buf(); bWTB = Buf(); bVA = Buf(); bAZ = Buf(); bWI = Buf(); bAZD = Buf()
              em.barrier()
              MS("dve", VA[:, :, :, 64:65], 1.0, [bVA])
              AZO = [av(O_WSB + 12 * KB + i * KB, 1 * KB, BF16) for i in range(2)]; bAZO = [Buf(), Buf()]
              for part in range(3):
                  ncol = 512 if part < 2 else 8
                  c0 = part * 512
                  for k in range(8):
                      em.dma(WTS[:, 0:ncol], wint[:, k, c0:c0 + ncol], writes=[bWTS])
                      CP("pool", WTB[:, k, 0:ncol], WTS[:, 0:ncol], [bWTS], [bWTB, bWTS])
                  for tt in range(NT):
                      bank = tt % 4
                      for k in range(8):
                          MM(PS[bank][:, 0:ncol], HT[:, k, tt * 128:(tt + 1) * 128], WTB[:, k, 0:ncol], k == 0, k == 7,
                             [bWTB, bHT[tt]], [bPS[bank]])
                      if part == 0:
                          CP("dve", VA[:, tt, :, 0:64], PS[bank][:, :].rearrange("p (h d) -> p h d", h=8), [bPS[bank]], [bVA])
                      elif part == 1:
                          ACT(AZO[tt % 2][:, 0:512], PS[bank][:, :], AF.Silu, [bPS[bank]], [bAZO[tt % 2]])
                          em.dma(az_d[tt * 128:(tt + 1) * 128, :], AZO[tt % 2][:, 0:512], reads=[bAZO[tt % 2]], writes=[bAZD])
                      else:
                          TS("dve", WI[:, tt, :], PS[bank][:, 0:8], 8.0 ** -0.5, None, ALU.mult, None, [bPS[bank]], [bWI])
              em.barrier()

              if stage == 5:
                  raise _Stop()
              SCs = [av(0, 8 * KB), av(O_C2 + 16 * KB, 8 * KB)]
              NM = av(8 * KB, 4 * KB, BF16)
              NMT = [av(12 * KB + i * 4 * KB, 4 * KB, BF16).rearrange("p (j t) -> p j t", j=16) for i in range(2)]
              PT = [av(20 * KB + i * KB, 1 * KB, BF16).rearrange("p (h t) -> p h t", h=4) for i in range(4)]
              TMPH = [av(24 * KB + i * 2 * KB, 2 * KB) for i in range(2)]
              XR_ = av(28 * KB, 4 * KB)
              OT = av(O_C2, 4 * KB)
              GATEB = av(O_C2 + 4 * KB, 4 * KB)
              CAT = av(O_C2 + 8 * KB, 2 * KB, BF16).rearrange("p (k t) -> p k t", k=8)
              AZT = av(O_C2 + 10 * KB, 1 * KB, BF16)
              YA = av(O_C2 + 11 * KB, 2 * KB)
              YAB = av(O_C2 + 13 * KB, 1 * KB, BF16)
              RD = SMALL[:, 8:16]
              bSCs = [Buf(), Buf()]; bBIs = [Buf(), Buf()]; bNM = Buf(); bNMT = [Buf(), Buf()]; bPT = [Buf() for _ in range(4)]; bTMPH = [Buf(), Buf()]
              bXR = Buf(); bOT = Buf(); bGB = Buf(); bCAT = Buf(); bAZT = Buf(); bYA = Buf(); bBI = Buf(); bRD = Buf()
              for hh in range(2):
                  MM(PS[6 + hh][:, :], SEL[0:4, b, :], MODG[0:4, hh * 512:(hh + 1) * 512], True, True, [bC, bL], [bPS[6 + hh]])
                  CP("dve", GATEB[:, hh * 512:(hh + 1) * 512], PS[6 + hh][:, :], [bPS[6 + hh]], [bGB])
              LO = SMALL[:, 16:17]; HI = SMALL[:, 17:18]; TAU = SMALL[:, 18:19]; CNTc = SMALL[:, 19:20]; TF = SMALL[:, 20:21]
              STEPS = SMALL[:, 24:24 + NBIS + 1]
              NTAU = SMALL[:, 21:22]; SSUM = SMALL[:, 22:23]; TSG = SMALL[:, 23:24]
              STEPN = SMALL[:, 24:24 + NBIS + 1]
              ptc = [0]
              PSB6 = PS[6].bitcast(BF16); PSB7 = PS[7].bitcast(BF16)

              SM2 = [SMALL[:, 16:16 + 24], SMALL2[:, 0:24]]

              def scal(i):
                  sm = SM2[i % 2]
                  return dict(LO=sm[:, 0:1], HI=sm[:, 1:2], TAU=sm[:, 2:3], TF=sm[:, 3:4], NTAU=sm[:, 4:5], SSUM=sm[:, 5:6],
                              TSG=sm[:, 6:7], STEPN=sm[:, 8:8 + NBIS + 1])

              def fidx(i):
                  nk = (i + 1) * 128
                  ngr = (nk + 511) // 512
                  SC = SCs[i % 2]; bSC = bSCs[i % 2]; bBI = bBIs[i % 2]; v = scal(i)
                  for g in range(ngr):
                      k0 = g * 512; kn = min(512, nk - k0)
                      for h in range(8):
                          bank = h % 2
                          base = 64 * (h % 2)
                          MM(PS[bank][:, 0:kn], QIT[base:base + 64, h // 2, i * 128:(i + 1) * 128], KIT[base:base + 64, k0:k0 + kn],
                             True, True, [bQIT, bKIT], [bPS[bank]])
                          if h == 0:
                              TS("dve", SC[:, k0:k0 + kn], PS[bank][:, 0:kn], 0.0, WI[:, i, h:h + 1], ALU.max, ALU.mult,
                                 [bPS[bank], bWI], [bSC])
                          else:
                              th = TMPH[h % 2]
                              TS("dve", th[:, 0:kn], PS[bank][:, 0:kn], 0.0, WI[:, i, h:h + 1], ALU.max, ALU.mult,
                                 [bPS[bank], bWI], [bTMPH[h % 2]])
                              TT("pool", SC[:, k0:k0 + kn], SC[:, k0:k0 + kn], th[:, 0:kn], ALU.add, [bTMPH[h % 2], bSC], [bSC])
                          yield
                  HI, LO, TF, NTAU, STEPN = v["HI"], v["LO"], v["TF"], v["NTAU"], v["STEPN"]
                  em.op("dve", lambda e: e.tensor_reduce(out=HI, in_=SC[:, 0:nk], axis=mybir.AxisListType.X, op=ALU.max), [bSC], [bBI])
                  em.op("dve", lambda e: e.tensor_reduce(out=LO, in_=SC[:, 0:nk], axis=mybir.AxisListType.X, op=ALU.min), [bSC], [bBI])
                  TT("dve", SC[:, i * 128:(i + 1) * 128], SC[:, i * 128:(i + 1) * 128], TRI[:], ALU.add, [bSC, bL], [bSC])
                  TT("dve", TF, HI, LO, ALU.subtract, [bBI], [bBI])
                  STT("dve", LO, TF, -0.01, LO, ALU.mult, ALU.add, [bBI], [bBI])
                  TT("dve", TF, HI, LO, ALU.subtract, [bBI], [bBI])
                  STT("dve", NTAU, TF, -0.5, LO, ALU.mult, ALU.subtract, [bBI], [bBI])
                  TS("dve", STEPN, STEPC[:, 0:NBIS + 1], TF, None, ALU.mult, None, [bBI, bC], [bBI])
                  yield

              def n_fidx(i):
                  return (((i + 1) * 128 + 511) // 512) * 8 + 1

              def fbis(i):
                  nk = (i + 1) * 128
                  SC = SCs[i % 2]; bSC = bSCs[i % 2]; bBI = bBIs[i % 2]; v = scal(i)
                  NTAU, SSUM, TSG, STEPN, TAU = v["NTAU"], v["SSUM"], v["TSG"], v["STEPN"], v["TAU"]
                  for kq in range(NBIS):
                      ACT(NM[:, 0:nk], SC[:, 0:nk], AF.Sign, [bSC, bBI], [bNM, bBI], bias=NTAU, scale=1.0, accum=SSUM)
                      ACT(TSG, SSUM, AF.Sign, [bBI], [bBI], bias=float(nk) - (2.0 * TOPK - 0.5), scale=1.0)
                      ACT(NTAU, TSG, AF.Identity, [bBI], [bBI], bias=NTAU, scale=STEPN[:, kq:kq + 1])
                      yield
                  STT("dve", TAU, NTAU, -1.0, STEPN[:, NBIS:NBIS + 1], ALU.mult, ALU.add, [bBI], [bBI])
                  em.op("dve", lambda e: e.tensor_scalar(out=NM[:, 0:nk], in0=SC[:, 0:nk], scalar1=TAU, scalar2=NEG,
                                                         op0=ALU.is_lt, op1=ALU.mult), [bSC, bBI], [bNM])
                  nmt = NMT[i % 2]; bnmt = bNMT[i % 2]
                  for j0 in range(0, i + 1, 8):
                      jn = min(8, i + 1 - j0)
                      for jj in range(jn):
                          j = j0 + jj
                          TR(PSB6[:, jj * 128:(jj + 1) * 128], NM[:, j * 128:(j + 1) * 128], IDENTB[:], [bNM, bC], [bPS[6]])
                      CP("dve", nmt[:, j0:j0 + jn, :], PSB6[:, 0:jn * 128].rearrange("p (j t) -> p j t", j=jn), [bPS[6]], [bnmt])
                  yield

              def n_fbis(i):
                  return NBIS + 1

              def attn(i):
                  nmt = NMT[i % 2]; bnmt = bNMT[i % 2]
                  for h in range(8):
                      base = 64 * (h % 2); ch = h // 2; ob = 4 + h // 4; hh = h % 4
                      for j0 in range(0, i + 1, 4):
                          jn = min(4, i + 1 - j0)
                          bank = 2 + (ptc[0] % 2)
                          pt = PT[ptc[0] % 4]; bpt = bPT[ptc[0] % 4]; ptc[0] += 1
                          for jj in range(jn):
                              j = j0 + jj
                              dlt = (i - j) * 128
                              reg = PS[bank][:, jj * 128:(jj + 1) * 128]
                              MM(reg, IDENTB[:], nmt[:, j, :], True, False, [bnmt, bC], [bPS[bank]])
                              if dlt <= NEAR_MAX:
                                  MM(reg, JB[:], TREVB[:, h, dlt:dlt + 128], False, False, [bC], [bPS[bank]])
                              MM(reg, KT[base:base + 64, ch, j * 128:(j + 1) * 128],
                                 QT[base:base + 64, ch, i * 128:(i + 1) * 128], False, True, [bQT, bKT], [bPS[bank]])
                          ACT(pt[:, 0:jn, :], PS[bank][:, 0:jn * 128].rearrange("p (h t) -> p h t", h=jn), AF.Exp, [bPS[bank]], [bpt])
                          for jj in range(jn):
                              j = j0 + jj
                              MM(PS[ob][:, hh * 65:(hh + 1) * 65], pt[:, jj, :], VA[:, j, h, :], j == 0, j == i, [bpt, bVA], [bPS[ob]])
                          yield

              def n_attn(i):
                  return 8 * ((i + 4) // 4)

              def back(i):
                  em.dma(AZT[:, 0:512], az_d[i * 128:(i + 1) * 128, :], reads=[bAZD], writes=[bAZT])
                  em.dma(CAT[:, 0:4, :], yst_d[:, :, i * 128:(i + 1) * 128].rearrange("k p t -> p k t"), reads=[bYD], writes=[bCAT])
                  em.dma(XR_, x[b, i * 128:(i + 1) * 128, :], writes=[bXR])
                  for g in range(2):
                      O3 = PS[4 + g][:, 0:260].rearrange("p (h d) -> p h d", h=4)
                      em.op("dve", lambda e, O3=O3, g=g: e.reciprocal(out=RD[:, 4 * g:4 * g + 4], in_=O3[:, :, 64]), [bPS[4 + g]], [bRD])
                      TT("dve", YA[:, g * 256:(g + 1) * 256].rearrange("p (h d) -> p h d", h=4), O3[:, :, 0:64],
                         RD[:, 4 * g:4 * g + 4].unsqueeze(2).to_broadcast([128, 4, 64]), ALU.mult, [bPS[4 + g], bRD], [bYA])
                  TT("dve", YAB[:, 0:512], YA[:, 0:512], AZT[:, 0:512], ALU.mult, [bYA, bAZT], [bYA])
                  if DEBUG_STAGE is not None and i == DEBUG_STAGE and b == 0:
                      em.dma(dbg[:, 0:512], YA[:, 0:512], reads=[bYA])
                      DBT = av(O_C2 + 26 * KB, 2 * KB)
                      CP("dve", DBT[:, 0:512].rearrange("p (k t) -> p k t", k=4), CAT[:, 0:4, :], [bCAT], [bYA])
                      em.dma(dbg[:, 512:1024], DBT[:, 0:512], reads=[bYA])
                  for c in range(4):
                      TR(PSB7[:, c * 128:(c + 1) * 128], YAB[:, c * 128:(c + 1) * 128], IDENTB[:], [bYA, bC], [bPS[7]])
                  CP("dve", CAT[:, 4:8, :], PSB7[:, 0:512].rearrange("p (c t) -> p c t", c=4), [bPS[7]], [bCAT])
                  for hh in range(2):
                      bank = 7 - hh
                      for k in range(8):
                          MM(PS[bank][:, :], CAT[:, k, :], WOUTB[:, k, hh * 512:(hh + 1) * 512], k == 0, k == 7, [bCAT, bC], [bPS[bank]])
                      TT("dve", OT[:, hh * 512:(hh + 1) * 512], PS[bank][:, :], GATEB[:, hh * 512:(hh + 1) * 512], ALU.mult,
                         [bPS[bank], bGB], [bOT])
                      TT("dve", OT[:, hh * 512:(hh + 1) * 512], OT[:, hh * 512:(hh + 1) * 512], XR_[:, hh * 512:(hh + 1) * 512], ALU.add,
                         [bOT, bXR], [bOT])
                  em.dma(out[b, i * 128:(i + 1) * 128, :], OT, reads=[bOT], writes=[bOT])

              def merge(streams):
                  prog = [0] * len(streams)
                  while True:
                      best = None
                      for idx, (g, n) in enumerate(streams):
                          if prog[idx] < n and (best is None or prog[idx] * streams[best][1] < prog[best] * n):
                              best = idx
                      if best is None:
                          break
                      next(streams[best][0], None)
                      prog[best] += 1
                  for g, n in streams:
                      for _ in g:
                          pass

              merge([(fidx(0), n_fidx(0))])
              merge([(fidx(1), n_fidx(1)), (fbis(0), n_fbis(0))])
              for i in range(NT):
                  st = [(attn(i), n_attn(i))]
                  if i + 2 < NT:
                      st.append((fidx(i + 2), n_fidx(i + 2)))
                  if i + 1 < NT:
                      st.append((fbis(i + 1), n_fbis(i + 1)))
                  merge(st)
                  back(i)
              em.barrier()
          except _Stop:
            break
        em.barrier()
        em.replay()
    return nc


def t5_bucket(dist):
    max_exact = 16
    d = np.maximum(dist, 1).astype(np.float32)
    large = max_exact + (np.log(d / max_exact) / np.float32(math.log(128 / max_exact)) * (32 - max_exact)).astype(np.int32)
    large = np.minimum(large, 31)
    return np.where(dist < max_exact, dist, large)


def host_consts():
    c = {}
    c["c_ident"] = np.eye(128, dtype=np.float32)
    c["c_jrev"] = np.ascontiguousarray(np.eye(128, dtype=np.float32)[::-1])
    bo = np.zeros((128, 128), np.float32); bo[:64, :64] = 1; bo[64:, 64:] = 1
    c["c_bones"] = bo
    t = np.arange(128)
    c["c_tri"] = np.where(t[None, :] <= t[:, None], 0.0, -1e9).astype(np.float32)
    sel = np.zeros((4, 4, 128), np.float32)
    for b in range(4):
        sel[b, b, :] = 1
    c["c_sel"] = sel
    oh = np.zeros((32, 384), np.float32)
    for i in range(127, 384):
        d = i - 127
        oh[t5_bucket(np.array([d]))[0], i] += 1.0
        oh[31, i] -= 1.0
    c["c_oh"] = oh
    p = np.arange(128)
    mrow = np.zeros((128, 8), np.float32); mrow[p, (p // 32) * 2 + ((p // 16) % 2)] = 1
    mcol = np.zeros((128, 2), np.float32); mcol[p, p // 64] = 1
    c["c_mrow"] = mrow; c["c_mcol"] = mcol
    return c


def host_layout(inp):
    f = lambda a: np.ascontiguousarray(a, dtype=np.float32)
    m = {}
    m["wada"] = f(inp["w_ada"][0].reshape(8, 128, 3072).transpose(1, 0, 2))
    m["bada4"] = f(np.broadcast_to(inp["b_ada"][0][None, :], (4, 3072)))
    m["ng"] = f(inp["norm_g"][0].reshape(8, 128).T)
    w = inp["w_in"][0]
    cols_fm = np.concatenate([np.arange(0, 512), np.arange(512, 1024), np.arange(1024, 1536), np.arange(1536, 2048),
                              np.arange(3072, 3584), np.arange(3584, 3648), np.arange(3584, 3648)])
    wf = w[:, cols_fm]
    m["winf"] = f(wf.reshape(8, 128, 21, 128).transpose(2, 1, 0, 3))
    cols_tm = np.concatenate([np.arange(2048, 2560), np.arange(2560, 3072), np.arange(3648, 3656)])
    m["wint"] = f(w[:, cols_tm].reshape(8, 128, 1032).transpose(1, 0, 2))
    m["wout"] = f(inp["w_out"][0].reshape(8, 128, 1024).transpose(1, 0, 2))
    m["wglu"] = f(inp["w_glu"][0].reshape(4, 128, 512).transpose(1, 0, 2))
    m["bglu"] = f(inp["b_glu"][0].reshape(4, 128).T)
    m["dskip"] = f(inp["d_skip"][0].reshape(4, 128).T)
    m["qg2"] = f(np.tile(inp["q_gain"][0], 2).reshape(128, 1))
    m["kg2"] = f(np.tile(inp["k_gain"][0], 2).reshape(128, 1))
    m["relb"] = f(inp["rel_bias"])
    are, aim, ldt = inp["a_re"][0], inp["a_im"][0], inp["log_dt"][0]
    col = lambda a: f(a.reshape(16, 2, 64).transpose(1, 2, 0).reshape(128, 16))
    m["are_c"] = col(are); m["aim_c"] = col(aim); m["ldt_c"] = col(np.repeat(ldt[:, None], 64, 1))
    ccol = lambda a: f(a.reshape(16, 2, 16, 64).transpose(1, 3, 0, 2).reshape(128, 16, 16))
    m["cre_c"] = ccol(inp["c_re"][0]); m["cim_c"] = ccol(inp["c_im"][0])
    row = lambda a: f(np.repeat(a.reshape(4, 8, 64).transpose(1, 0, 2)[:, None, :, :], 16, 1).reshape(128, 256))
    m["are_r"] = row(are); m["aim_r"] = row(aim); m["ldt_r"] = row(np.repeat(ldt[:, None], 64, 1))
    brow = lambda a: f(a.reshape(4, 8, 64, 16).transpose(1, 3, 0, 2).reshape(128, 256))
    m["bre_r"] = brow(inp["b_re"][0]); m["bim_r"] = brow(inp["b_im"][0])
    bcol = lambda a: f(a.reshape(16, 2, 64, 16).transpose(1, 2, 0, 3).reshape(128, 16, 16))
    m["bre_c"] = bcol(inp["b_re"][0]); m["bim_c"] = bcol(inp["b_im"][0])
    m["dsk16"] = f(inp["d_skip"][0].reshape(32, 16).T)
    m.update(host_consts())
    return m


_NC_CACHE = {}


def kernel(**inputs):
    inp = {k: np.asarray(v) for k, v in inputs.items()}
    n_cores = 8
    shared = host_layout(inp)
    x = np.ascontiguousarray(inp["x"], dtype=np.float32)
    c = np.asarray(inp["c"], dtype=np.float32)
    in_maps = []
    for r in range(n_cores):
        m = dict(shared)
        m["x"] = x[r * NSEQ:(r + 1) * NSEQ]
        m["ct"] = np.ascontiguousarray(c[r * NSEQ:(r + 1) * NSEQ].T.reshape(8, 128, NSEQ).transpose(1, 0, 2))
        in_maps.append(m)
    if "nc" not in _NC_CACHE:
        _NC_CACHE["nc"] = build_program()
    res = run_bass_kernel_spmd(_NC_CACHE["nc"], in_maps, core_ids=list(range(n_cores)))
    outs = [np.asarray(res.results[r]["out"]) for r in range(n_cores)]
    return np.concatenate(outs, axis=0).astype(np.float32)
```

```python
import math
import numpy as np
import concourse.bass as bass
import concourse.mybir as mybir
from concourse.bass_utils import run_bass_kernel_spmd
from contextlib import ExitStack

F32 = mybir.dt.float32
BF16 = mybir.dt.bfloat16
ALU = mybir.AluOpType
AF = mybir.ActivationFunctionType

S = 2048
D = 1024
NSEQ = 4
NT = 16
EPS = 1e-6
NEG = -30000.0
TOPK = 256
NBIS = 12
DEBUG_STAGE = None
NEAR_MAX = 128
PHASEC_MODE = "all"


class Buf:
    __slots__ = ("w", "r")

    def __init__(self):
        self.w = None
        self.r = {}


class Emitter:
    ENGS = ("pe", "act", "dve", "pool", "sp")

    def __init__(self, nc, es, n_dma_sems=8):
        self.nc = nc
        self.sem = {}
        self.cnt = {}
        self.prog = {e: [] for e in self.ENGS}
        self.waited = {e: {} for e in self.ENGS}
        for e in self.ENGS:
            self.sem[e] = es.enter_context(nc.semaphore("s_" + e))
            self.cnt[e] = 0
        self.dma_keys = []
        for i in range(n_dma_sems):
            k = "dma%d" % i
            self.sem[k] = es.enter_context(nc.semaphore("s_" + k))
            self.cnt[k] = 0
            self.dma_keys.append(k)
        self.dma_rr = 0

    def _deps(self, eng, reads, writes):
        deps = {}

        def add(tok):
            if tok is None:
                return
            k, c = tok
            if deps.get(k, 0) < c:
                deps[k] = c
        for b in reads:
            add(b.w)
        for b in writes:
            if b.w is not None and b.w[0] != eng:
                add(b.w)
            for k, c in b.r.items():
                if k != eng:
                    add((k, c))
        return deps

    def _emit_waits(self, eng, deps):
        w = self.waited[eng]
        for k, c in deps.items():
            if w.get(k, 0) >= c:
                continue
            w[k] = c
            val = c * 16 if k.startswith("dma") else c
            self.prog[eng].append(("wait", self.sem[k], val))

    def _post(self, key, tok, reads, writes):
        for b in reads:
            if b.r.get(key, 0) < tok[1]:
                b.r[key] = tok[1]
        for b in writes:
            b.w = tok
            b.r = {}

    def op(self, eng, fn, reads=(), writes=()):
        self._emit_waits(eng, self._deps(eng, reads, writes))
        self.cnt[eng] += 1
        tok = (eng, self.cnt[eng])
        self.prog[eng].append(("op", fn, self.sem[eng], 1))
        self._post(eng, tok, reads, writes)
        return tok

    def dma(self, out, in_, reads=(), writes=(), q="sp"):
        self._emit_waits(q, self._deps(q, reads, writes))
        k = self.dma_keys[self.dma_rr % len(self.dma_keys)]
        self.dma_rr += 1
        self.cnt[k] += 1
        tok = (k, self.cnt[k])
        self.prog[q].append(("op", (lambda e: e.dma_start(out=out, in_=in_)), self.sem[k], 16))
        self._post(k, tok, reads, writes)
        return tok

    def barrier(self):
        snap = {k: c for k, c in self.cnt.items() if c > 0}
        for e in self.ENGS:
            self._emit_waits(e, dict(snap))

    def replay(self):
        with self.nc.Block() as block:
            def mk(eng):
                def body(e):
                    for item in self.prog[eng]:
                        if item[0] == "wait":
                            e.wait_ge(item[1], item[2])
                        else:
                            item[1](e).then_inc(item[2], item[3])
                return body
            block.tensor(mk("pe"))
            block.scalar(mk("act"))
            block.vector(mk("dve"))
            block.gpsimd(mk("pool"))
            block.sync(mk("sp"))


class _Stop(Exception):
    pass


def build_program(nseq=NSEQ, stage=99):
    nc = bass.Bass("TRN2", target_bir_lowering=False)

    def din(name, shape, dt=F32):
        return nc.dram_tensor(name, list(shape), dt, kind="ExternalInput")

    x_t = din("x", [nseq, S, D]); x = x_t.ap()
    ct = din("ct", [128, 8, 4]).ap()
    wada = din("wada", [128, 8, 3072]).ap()
    bada4 = din("bada4", [4, 3072]).ap()
    ng = din("ng", [128, 8]).ap()
    winf = din("winf", [21, 128, 8, 128]).ap()
    wint = din("wint", [128, 8, 1032]).ap()
    wout = din("wout", [128, 8, 1024]).ap()
    wglu = din("wglu", [128, 4, 512]).ap()
    bglu = din("bglu", [128, 4]).ap()
    dskip = din("dskip", [128, 4]).ap()
    qg2 = din("qg2", [128, 1]).ap()
    kg2 = din("kg2", [128, 1]).ap()
    relb = din("relb", [32, 8]).ap()
    are_c = din("are_c", [128, 16]).ap(); aim_c = din("aim_c", [128, 16]).ap(); ldt_c = din("ldt_c", [128, 16]).ap()
    cre_c = din("cre_c", [128, 16, 16]).ap(); cim_c = din("cim_c", [128, 16, 16]).ap()
    are_r = din("are_r", [128, 256]).ap(); aim_r = din("aim_r", [128, 256]).ap(); ldt_r = din("ldt_r", [128, 256]).ap()
    bre_r = din("bre_r", [128, 256]).ap(); bim_r = din("bim_r", [128, 256]).ap()
    bre_c = din("bre_c", [128, 16, 16]).ap(); bim_c = din("bim_c", [128, 16, 16]).ap()
    dsk16 = din("dsk16", [16, 32]).ap()
    c_ident = din("c_ident", [128, 128]).ap(); c_jrev = din("c_jrev", [128, 128]).ap()
    c_bones = din("c_bones", [128, 128]).ap(); c_tri = din("c_tri", [128, 128]).ap()
    c_sel = din("c_sel", [4, 4, 128]).ap(); c_oh = din("c_oh", [32, 384]).ap()
    c_mrow = din("c_mrow", [128, 8]).ap(); c_mcol = din("c_mcol", [128, 2]).ap()
    out = nc.dram_tensor("out", [nseq, S, D], F32, kind="ExternalOutput").ap()
    dbg = nc.dram_tensor("dbg", [128, 1024], F32, kind="ExternalOutput").ap() if DEBUG_STAGE is not None else None
    tb_t = nc.dram_tensor("tb_d", [8, 384], F32)
    yst_d = nc.dram_tensor("yst_d", [4, 128, S], BF16).ap()
    az_d = nc.dram_tensor("az_d", [S, 512], BF16).ap()

    es = ExitStack()
    with es:
        em = Emitter(nc, es)

        def sb(name, shape, dt=F32):
            return es.enter_context(nc.sbuf_tensor(name, list(shape), dt))

        PS = [es.enter_context(nc.psum_tensor("ps%d" % i, [128, 512], F32)) for i in range(8)]
        bPS = [Buf() for _ in range(8)]

        def MM(o, l, r, st, sp, rd, wr):
            return em.op("pe", lambda e: e.matmul(o, lhsT=l, rhs=r, start=st, stop=sp), rd, wr)

        def TR(o, i, idn, rd, wr):
            return em.op("pe", lambda e: e.transpose(out=o, in_=i, identity=idn), rd, wr)

        def ACT(o, i, func, rd, wr, bias=None, scale=None, accum=None):
            kw = {}
            if bias is not None:
                kw["bias"] = bias
            if scale is not None:
                kw["scale"] = scale
            if accum is not None:
                kw["accum_out"] = accum
            return em.op("act", lambda e: e.activation(out=o, in_=i, func=func, **kw), rd, wr)

        def TS(eng, o, i, s1, s2, op0, op1, rd, wr, accum=None):
            if s2 is None:
                return em.op(eng, lambda e: e.tensor_scalar(out=o, in0=i, scalar1=s1, scalar2=None, op0=op0), rd, wr)
            if accum is None:
                return em.op(eng, lambda e: e.tensor_scalar(out=o, in0=i, scalar1=s1, scalar2=s2, op0=op0, op1=op1), rd, wr)
            return em.op(eng, lambda e: e.tensor_scalar(out=o, in0=i, scalar1=s1, scalar2=s2, op0=op0, op1=op1, accum_out=accum), rd, wr)

        def TT(eng, o, a, b, op, rd, wr):
            return em.op(eng, lambda e: e.tensor_tensor(out=o, in0=a, in1=b, op=op), rd, wr)

        def STT(eng, o, i0, sc, i1, op0, op1, rd, wr):
            return em.op(eng, lambda e: e.scalar_tensor_tensor(out=o, in0=i0, scalar=sc, in1=i1, op0=op0, op1=op1), rd, wr)

        def CP(eng, o, i, rd, wr):
            if eng == "act":
                return em.op("act", lambda e: e.copy(out=o, in_=i), rd, wr)
            return em.op(eng, lambda e: e.tensor_copy(out=o, in_=i), rd, wr)

        def MS(eng, o, v, wr):
            return em.op(eng, lambda e: e.memset(o, v), (), wr)

        NA = 36864
        ARENA = sb("arena", [128, NA], F32)

        def av(off_b, nbytes, dt=F32):
            sl = ARENA[:, off_b // 4:(off_b + nbytes) // 4]
            return sl.bitcast(BF16) if dt == BF16 else sl

        KB = 1024
        IDENT = sb("IDENT", [128, 128]); IDENTB = sb("IDENTB", [128, 128], BF16)
        JB = sb("JB", [128, 128], BF16); BONESB = sb("BONESB", [128, 128], BF16)
        TRI = sb("TRI", [128, 128]); SEL = sb("SEL", [4, 4, 128])
        MROW = sb("MROW", [128, 8]); MCOL = sb("MCOL", [128, 2]); NMCOL = sb("NMCOL", [128, 2])
        MODG = sb("MODG", [4, 1024]); SHSC = sb("SHSC", [128, 16, 4]); AMOD = sb("AMOD", [128, 8, 4])
        NG = sb("NG", [128, 8]); QG = sb("QG", [128, 1]); KG = sb("KG", [128, 1])
        BGLU = sb("BGLU", [128, 4]); DSKIP = sb("DSKIP", [128, 4])
        TREVB = sb("TREVB", [128, 8, 256], BF16)
        WOUTB = sb("WOUTB", [128, 8, 1024], BF16)
        WGLUB = sb("WGLUB", [128, 4, 512], BF16)
        WB = sb("WB", [128, 4, 4, 2, 128], BF16)
        WK = sb("WK", [128, 4, 4, 128], BF16)
        WCA = sb("WCA", [128, 16, 4, 2, 32], BF16)
        WCA3 = sb("WCA3", [128, 4, 4, 2, 64], BF16)
        PR = sb("PR", [128, 17, 16]); PI = sb("PI", [128, 17, 16]); NPI = sb("NPI", [128, 17, 16])
        WI = sb("WI", [128, 16, 8])
        SMALL = sb("SMALL", [128, 64])
        SMALL2 = sb("SMALL2", [128, 32])
        STEPC = sb("STEPC", [128, 32])
        EPSC = sb("EPSC", [128, 1])
        bC = Buf()

        ld = [(IDENT[:], c_ident), (TRI[:], c_tri), (SEL[:], c_sel), (MROW[:], c_mrow), (MCOL[:], c_mcol),
              (NG[:], ng), (QG[:], qg2), (KG[:], kg2), (BGLU[:], bglu), (DSKIP[:], dskip)]
        bL = Buf()
        for o_, i_ in ld:
            em.dma(o_, i_, writes=[bL])
        ST0 = av(0, 12 * KB)[:, 0:3072]; ST1 = av(12 * KB, 12 * KB)[:, 0:3072]
        bST = [Buf(), Buf()]
        TMPF = av(24 * KB, 4 * KB)
        bTMP = Buf()
        em.dma(TMPF[:, 0:128], c_jrev, writes=[bTMP])
        CP("dve", JB[:], TMPF[:, 0:128], [bTMP], [bC, bTMP])
        em.dma(TMPF[:, 128:256], c_bones, writes=[bTMP])
        CP("dve", BONESB[:], TMPF[:, 128:256], [bTMP], [bC, bTMP])
        CP("dve", IDENTB[:], IDENT[:], [bL], [bC])
        TS("dve", NMCOL[:], MCOL[:], -1.0, None, ALU.mult, None, [bL], [bC])
        TS("dve", QG[:], QG[:], 0.125, None, ALU.mult, None, [bL], [bL])

        CT = av(28 * KB, 128)[:, 0:32].rearrange("p (k b) -> p k b", k=8)
        COND = av(29 * KB, 128)[:, 0:32].rearrange("p (k b) -> p k b", k=8)
        bCT = Buf(); bCOND = Buf()
        em.dma(CT, ct, writes=[bCT])
        ACT(COND, CT, AF.Silu, [bCT], [bCOND])
        STs = [ST0, ST1]
        for k in range(8):
            em.dma(STs[k % 2], wada[:, k, :], writes=[bST[k % 2]])
            for blk in range(6):
                MM(PS[blk][0:4, 0:512], COND[:, k, :], STs[k % 2][:, blk * 512:(blk + 1) * 512], k == 0, k == 7,
                   [bCOND, bST[k % 2]], [bPS[blk]])
        MODROW = av(30 * KB, 12 * KB)[0:4, 0:3072]
        BADA = av(42 * KB, 12 * KB)[0:4, 0:3072]
        bMR = Buf(); bBA = Buf()
        em.dma(BADA, bada4, writes=[bBA])
        for blk in range(6):
            TT("dve", MODROW[:, blk * 512:(blk + 1) * 512], PS[blk][0:4, 0:512], BADA[:, blk * 512:(blk + 1) * 512], ALU.add,
               [bPS[blk], bBA], [bMR])
        CP("dve", MODG[:], MODROW[:, 2048:3072], [bMR], [bC])
        for c in range(16):
            TR(PS[6][:, c * 4:(c + 1) * 4], MODROW[0:4, c * 128:(c + 1) * 128], IDENT[0:4, 0:4], [bMR, bL], [bPS[6]])
        CP("dve", SHSC[:].rearrange("p a b -> p (a b)"), PS[6][:, 0:64], [bPS[6]], [bC])
        for b in range(4):
            TS("dve", AMOD[:, :, b], SHSC[:, 8:16, b], 1.0, None, ALU.add, None, [bC], [bC])
            TT("dve", AMOD[:, :, b], AMOD[:, :, b], NG[:], ALU.mult, [bC, bL], [bC])

        RB = av(54 * KB, 32)[0:32, 0:8]
        OHS = av(55 * KB, 1536)[0:32, 0:384]
        WROW = av(57 * KB, 1536)[0:8, 0:384]
        bRB = Buf(); bWR = Buf(); bTBD = Buf(); bTV = Buf()
        em.dma(RB, relb, writes=[bRB]); em.dma(OHS, c_oh, writes=[bRB])
        MM(PS[7][0:8, 0:384], RB, OHS, True, True, [bRB], [bPS[7]])
        CP("dve", WROW, PS[7][0:8, 0:384], [bPS[7]], [bWR])
        em.dma(tb_t.ap(), WROW, reads=[bWR], writes=[bTBD])
        TREVF = av(60 * KB, 8 * KB)[:, 0:2048].rearrange("p (h c) -> p h c", h=8)
        em.dma(TREVF, bass.AP(tb_t, 0, [[1, 128], [384, 8], [1, 256]]), reads=[bTBD], writes=[bTV])
        CP("dve", TREVB[:], TREVF, [bTV], [bC])

        for k in range(8):
            stg = TMPF
            em.dma(stg, wout[:, k, :], writes=[bTMP])
            CP("pool", WOUTB[:, k, :], stg, [bTMP], [bC, bTMP])
        for k in range(4):
            em.dma(TMPF[:, 0:512], wglu[:, k, :], writes=[bTMP])
            CP("pool", WGLUB[:, k, :], TMPF[:, 0:512], [bTMP], [bC, bTMP])

        TWO_PI = 2.0 * math.pi
        MAGIC = 12582912.0

        def s5_derive(n, ARE, AIM, LDT, base_b, tagbufs):
            names = ["DT", "TH", "T1", "ER", "Y", "K1", "R", "SN", "CS", "LR", "LI", "NR", "DEN", "CR", "CI", "T2", "T3"]
            t = {}
            for idx, nm in enumerate(names):
                t[nm] = av(base_b + idx * n * 4, n * 4)[:, 0:n]
            b = tagbufs
            ACT(t["DT"], LDT, AF.Exp, [b], [b])
            TT("dve", t["TH"], AIM, t["DT"], ALU.mult, [b], [b])
            TT("dve", t["T1"], ARE, t["DT"], ALU.mult, [b], [b])
            ACT(t["ER"], t["T1"], AF.Exp, [b], [b])

            def sincos(dst, shift):
                TS("dve", t["T2"], t["TH"], shift, None, ALU.add, None, [b], [b])
                TS("dve", t["Y"], t["T2"], 1.0 / TWO_PI, None, ALU.mult, None, [b], [b])
                TS("dve", t["K1"], t["Y"], MAGIC, None, ALU.add, None, [b], [b])
                TS("dve", t["K1"], t["K1"], MAGIC, None, ALU.subtract, None, [b], [b])
                STT("dve", t["R"], t["K1"], -TWO_PI, t["T2"], ALU.mult, ALU.add, [b], [b])
                TS("dve", t["R"], t["R"], -3.14159, 3.14159, ALU.max, ALU.min, [b], [b])
                ACT(dst, t["R"], AF.Sin, [b], [b])
            sincos(t["SN"], 0.0)
            sincos(t["CS"], math.pi / 2.0)
            TT("dve", t["LR"], t["ER"], t["CS"], ALU.mult, [b], [b])
            TT("dve", t["LI"], t["ER"], t["SN"], ALU.mult, [b], [b])
            TS("dve", t["NR"], t["LR"], -1.0, None, ALU.add, None, [b], [b])
            TT("dve", t["DEN"], ARE, ARE, ALU.mult, [b], [b])
            TT("dve", t["T1"], AIM, AIM, ALU.mult, [b], [b])
            TT("dve", t["DEN"], t["DEN"], t["T1"], ALU.add, [b], [b])
            em.op("dve", lambda e: e.reciprocal(out=t["DEN"], in_=t["DEN"]), [b], [b])
            TT("dve", t["T1"], t["NR"], ARE, ALU.mult, [b], [b])
            TT("dve", t["T3"], t["LI"], AIM, ALU.mult, [b], [b])
            TT("dve", t["T1"], t["T1"], t["T3"], ALU.add, [b], [b])
            TT("dve", t["CR"], t["T1"], t["DEN"], ALU.mult, [b], [b])
            TT("dve", t["T1"], t["LI"], ARE, ALU.mult, [b], [b])
            TT("dve", t["T3"], t["NR"], AIM, ALU.mult, [b], [b])
            TT("dve", t["T1"], t["T1"], t["T3"], ALU.subtract, [b], [b])
            TT("dve", t["CI"], t["T1"], t["DEN"], ALU.mult, [b], [b])
            return t

        def cmul(o_re, o_im, a_re, a_im, b_re, b_im, t1, t2, b):
            TT("dve", t1, a_re, b_re, ALU.mult, [b], [b])
            TT("dve", t2, a_im, b_im, ALU.mult, [b], [b])
            TT("dve", o_re_tmp_holder[0], t1, t2, ALU.subtract, [b], [b])
            TT("dve", t1, a_re, b_im, ALU.mult, [b], [b])
            TT("dve", t2, a_im, b_re, ALU.mult, [b], [b])
            TT("dve", o_im, t1, t2, ALU.add, [b], [b])
            CP("dve", o_re, o_re_tmp_holder[0], [b], [b])

        bS5 = Buf()
        PC = av(70 * KB, 3 * 64)
        em.dma(PC[:, 0:16], are_c, writes=[bS5]); em.dma(PC[:, 16:32], aim_c, writes=[bS5]); em.dma(PC[:, 32:48], ldt_c, writes=[bS5])
        tc_ = s5_derive(16, PC[:, 0:16], PC[:, 16:32], PC[:, 32:48], 71 * KB, bS5)
        CT1 = av(74 * KB, 64)[:, 0:16]; CT2 = av(74 * KB + 64, 64)[:, 0:16]; CT3 = av(74 * KB + 128, 64)[:, 0:16]
        o_re_tmp_holder = [CT3]
        CP("dve", PR[:, 1, :], tc_["LR"], [bS5], [bS5]); CP("dve", PI[:, 1, :], tc_["LI"], [bS5], [bS5])
        for k in range(2, 5):
            cmul(PR[:, k, :], PI[:, k, :], PR[:, k - 1, :], PI[:, k - 1, :], PR[:, 1, :], PI[:, 1, :], CT1, CT2, bS5)
        CP("dve", PR[:, 5, :], PR[:, 4, :], [bS5], [bS5]); CP("dve", PI[:, 5, :], PI[:, 4, :], [bS5], [bS5])
        for k in range(6, 14):
            cmul(PR[:, k, :], PI[:, k, :], PR[:, k - 1, :], PI[:, k - 1, :], PR[:, k - 1, :], PI[:, k - 1, :], CT1, CT2, bS5)
        for k in range(14, 17):
            MS("dve", PR[:, k, :], 0.0, [bS5]); MS("dve", PI[:, k, :], 0.0, [bS5])
        MS("dve", PR[:, 0, :], 1.0, [bS5]); MS("dve", PI[:, 0, :], 0.0, [bS5])
        TS("dve", NPI[:].rearrange("p a b -> p (a b)"), PI[:].rearrange("p a b -> p (a b)"), -1.0, None, ALU.mult, None, [bS5], [bS5])
        CRE = av(75 * KB, 1024)[:, 0:256].rearrange("p (q c) -> p q c", q=16)
        CIM = av(76 * KB, 1024)[:, 0:256].rearrange("p (q c) -> p q c", q=16)
        em.dma(CRE, cre_c, writes=[bS5]); em.dma(CIM, cim_c, writes=[bS5])
        PRW = av(77 * KB, 5 * KB)
        em.dma(PRW[:, 0:256], are_r, writes=[bS5]); em.dma(PRW[:, 256:512], aim_r, writes=[bS5]); em.dma(PRW[:, 512:768], ldt_r, writes=[bS5])
        em.dma(PRW[:, 768:1024], bre_r, writes=[bS5]); em.dma(PRW[:, 1024:1280], bim_r, writes=[bS5])
        tr_ = s5_derive(256, PRW[:, 0:256], PRW[:, 256:512], PRW[:, 512:768], 82 * KB, bS5)
        BBR = av(100 * KB, 1024)[:, 0:256]; BBI = av(101 * KB, 1024)[:, 0:256]
        RT1 = av(102 * KB, 1024)[:, 0:256]; RT2 = av(103 * KB, 1024)[:, 0:256]; RT3 = av(104 * KB, 1024)[:, 0:256]
        o_re_tmp_holder[0] = RT3
        cmul(BBR, BBI, tr_["CR"], tr_["CI"], PRW[:, 768:1024], PRW[:, 1024:1280], RT1, RT2, bS5)
        for qq in range(4):
            for ri, src in enumerate((BBR, BBI)):
                for g2 in range(2):
                    TS("dve", WB[:, qq, :, ri, g2 * 64:(g2 + 1) * 64], src.rearrange("p (k c) -> p k c", k=4),
                       MROW[:, qq * 2 + g2:qq * 2 + g2 + 1], None, ALU.mult, None, [bS5, bL], [bS5])
        em.barrier()
        def v3(off):
            return av(off, 1024)[:, 0:256].rearrange("p (q c) -> p q c", q=16)
        BCR = v3(106 * KB); BCI = v3(107 * KB); BBCR = v3(108 * KB); BBCI = v3(109 * KB); TA = v3(110 * KB); TB_ = v3(111 * KB)
        BTR = v3(129 * KB); BTI = v3(130 * KB); NCIM = v3(128 * KB); GAR = v3(131 * KB); GAI = v3(132 * KB)
        em.dma(BCR, bre_c, writes=[bS5]); em.dma(BCI, bim_c, writes=[bS5])

        def bc(ap2d):
            return ap2d.unsqueeze(2).to_broadcast([128, 16, 16])

        def cmul_b(o_re, o_im, a_re, a_im, s_re, s_im):
            TT("dve", TA, a_re, bc(s_re), ALU.mult, [bS5], [bS5])
            TT("dve", TB_, a_im, bc(s_im), ALU.mult, [bS5], [bS5])
            TT("dve", o_re, TA, TB_, ALU.subtract, [bS5], [bS5])
            TT("dve", TA, a_re, bc(s_im), ALU.mult, [bS5], [bS5])
            TT("dve", TB_, a_im, bc(s_re), ALU.mult, [bS5], [bS5])
            TT("dve", o_im, TA, TB_, ALU.add, [bS5], [bS5])
        cmul_b(BBCR, BBCI, BCR, BCI, tc_["CR"], tc_["CI"])
        TS("dve", NCIM, CIM, -1.0, None, ALU.mult, None, [bS5], [bS5])
        BM = {}
        for tau in range(4):
            if tau == 0:
                sr, si = BBCR, BBCI
            else:
                cmul_b(BTR, BTI, BBCR, BBCI, PR[:, tau, :], PI[:, tau, :])
                sr, si = BTR, BTI
            for g2 in range(2):
                for ri, src in enumerate((sr, si)):
                    t_ = v3(112 * KB + ((tau * 2 + g2) * 2 + ri) * KB)
                    TS("dve", t_, src, MCOL[:, g2:g2 + 1], None, ALU.mult, None, [bS5, bL], [bS5])
                    BM[(tau, g2, ri)] = t_
        for g in range(32):
            q_ = g // 2; g2 = g % 2
            for tau in range(4):
                col = (g * 4 + tau) * 16
                blk = PS[col // 512][0:16, col % 512:col % 512 + 16]
                MM(blk, BM[(tau, g2, 0)][:, q_, :], CRE[:, q_, :], True, False, [bS5], [bPS[col // 512]])
                MM(blk, BM[(tau, g2, 1)][:, q_, :], NCIM[:, q_, :], False, True, [bS5], [bPS[col // 512]])
        KST = av(8 * KB, 8 * KB)[0:16, 0:2048]
        bKS = Buf()
        for bk in range(4):
            CP("dve", KST[:, bk * 512:(bk + 1) * 512], PS[bk][0:16, :], [bPS[bk]], [bKS])
        KSTv = KST.rearrange("p (g t c) -> p g t c", g=32, t=4)
        DSK16 = av(18 * KB, 128)[0:16, 0:32]
        TMPK = av(16 * KB, 2 * KB)[0:16, 0:512].rearrange("p (g c) -> p g c", g=32)
        em.dma(DSK16, dsk16, writes=[bKS])
        TT("dve", TMPK, IDENT[0:16, 0:16].unsqueeze(1).to_broadcast([16, 32, 16]), DSK16.unsqueeze(2).to_broadcast([16, 32, 16]),
           ALU.mult, [bKS, bL], [bKS])
        TT("dve", KSTv[:, :, 0, :], KSTv[:, :, 0, :], TMPK, ALU.add, [bKS], [bKS])
        WKF = av(0, 8 * KB).rearrange("p (k t c) -> p k t c", k=4, t=4)
        bWKF = Buf()
        MS("dve", av(0, 8 * KB), 0.0, [bWKF])
        for g in range(32):
            g8 = g % 8
            em.dma(WKF[16 * g8:16 * g8 + 16, g // 8, :, 16 * g8:16 * g8 + 16], KSTv[0:16, g, :, :], reads=[bKS], writes=[bWKF])
        CP("dve", WK[:], WKF, [bWKF], [bS5])
        for a in range(4):
            cmul_b(GAR, GAI, CRE, CIM, PR[:, a + 1, :], PI[:, a + 1, :])
            for g2 in range(2):
                TS("dve", WCA[:, :, a, 0, 16 * g2:16 * g2 + 16], GAR, MCOL[:, g2:g2 + 1], None, ALU.mult, None, [bS5, bL], [bS5])
                TS("dve", WCA[:, :, a, 1, 16 * g2:16 * g2 + 16], GAI, NMCOL[:, g2:g2 + 1], None, ALU.mult, None, [bS5, bC], [bS5])
        MS("dve", WCA3[:].rearrange("p k a r c -> p (k a r c)"), 0.0, [bS5])
        for kt in range(4):
            CP("dve", WCA3[:, kt, :, :, 32:64], WCA[:, 4 * kt + 3, :, :, :], [bS5], [bS5])
        MS("dve", EPSC[:, 0:1], EPS, [bC])
        for kq in range(NBIS + 1):
            MS("dve", STEPC[:, kq:kq + 1], -0.25 * (0.5 ** kq), [bC])
        em.barrier()

        O_HT = 0
        O_UT = 32 * KB; O_SZT = 48 * KB; O_X = 64 * KB; O_XB = 96 * KB; O_ZB = 112 * KB; O_ZG = 120 * KB; O_GT = 136 * KB; O_WSA = 96 * KB
        O_QT = 32 * KB; O_KT = 48 * KB; O_QIT = 64 * KB; O_KIT = 80 * KB; O_VA = 84 * KB; O_WSB = 101 * KB; O_C2 = 101 * KB

        HT = av(O_HT, 32 * KB, BF16).rearrange("p (k t) -> p k t", k=8)
        UT = av(O_UT, 16 * KB, BF16).rearrange("p (k t) -> p k t", k=4)
        SZT = av(O_SZT, 16 * KB, BF16).rearrange("p (k t) -> p k t", k=4)
        ZG = av(O_ZG, 16 * KB, BF16).rearrange("p (k t) -> p k t", k=4)
        XX = [av(O_X + i * 16 * KB, 16 * KB).rearrange("p (r t) -> p r t", r=2) for i in range(2)]
        HP = [av(O_XB + j2 * 6 * KB, 6 * KB).rearrange("p (r m) -> p r m", r=2) for j2 in range(2)]
        ZB = av(O_ZB, 8 * KB, BF16).rearrange("p (q r m) -> p q r m", q=4, r=2)
        QT = av(O_QT, 16 * KB, BF16).rearrange("p (k t) -> p k t", k=4)
        KT = av(O_KT, 16 * KB, BF16).rearrange("p (k t) -> p k t", k=4)
        QIT = av(O_QIT, 16 * KB, BF16).rearrange("p (k t) -> p k t", k=4)
        KIT = av(O_KIT, 4 * KB, BF16)
        VA = av(O_VA, 16640, BF16).rearrange("p (j h d) -> p j h d", j=16, h=8)

        for b in range(nseq):
          try:
              if stage == 0:
                  raise _Stop()
              XT = [av(O_WSA + i * 4 * KB, 4 * KB) for i in range(2)]
              XN = av(O_WSA + 8 * KB, 4 * KB)
              JNK = av(O_ZG, 4 * KB)
              bXT = [Buf(), Buf()]; bXN = Buf(); bJ = Buf(); bSS = Buf()
              bHT = [Buf() for _ in range(NT)]
              SS = SMALL[:, 0:1]; RST = SMALL[:, 1:2]
              for tt in range(NT):
                  xt = XT[tt % 2]
                  em.dma(xt, x[b, tt * 128:(tt + 1) * 128, :], writes=[bXT[tt % 2]])
                  ACT(JNK, xt, AF.Square, [bXT[tt % 2], bSS], [bJ, bSS], accum=SS)
                  TS("dve", RST, SS, 1.0 / D, EPS, ALU.mult, ALU.add, [bSS], [bSS])
                  ACT(RST, RST, AF.Sqrt, [bSS], [bSS])
                  em.op("dve", lambda e: e.reciprocal(out=RST, in_=RST), [bSS], [bSS])
                  TS("dve", XN, xt, RST, None, ALU.mult, None, [bXT[tt % 2], bSS], [bXN])
                  for k in range(8):
                      bank = 4 + (k // 4)
                      TR(PS[bank][:, (k % 4) * 128:(k % 4 + 1) * 128], XN[:, k * 128:(k + 1) * 128], IDENT[:], [bXN], [bPS[bank]])
                  for k in range(8):
                      bank = 4 + (k // 4)
                      src = PS[bank][:, (k % 4) * 128:(k % 4 + 1) * 128]
                      dst = HT[:, k, tt * 128:(tt + 1) * 128]
                      if k % 2 == 0:
                          TS("dve", dst, src, AMOD[:, k, b:b + 1], SHSC[:, k, b:b + 1], ALU.mult, ALU.add, [bPS[bank]], [bHT[tt]])
                      else:
                          ACT(dst, src, AF.Identity, [bPS[bank]], [bHT[tt]], bias=SHSC[:, k, b:b + 1], scale=AMOD[:, k, b:b + 1])

              if stage == 1:
                  raise _Stop()
              WSF = [av(O_WSA + i * 4 * KB, 4 * KB).rearrange("p (k c) -> p k c", k=8) for i in range(2)]
              WSH = [av(O_WSA + 8 * KB + i * 2 * KB, 2 * KB, BF16).rearrange("p (k c) -> p k c", k=8) for i in range(2)]
              bWSF = [Buf(), Buf()]; bWSH = [Buf(), Buf()]
              em.barrier()

              def proj_fm(mt, evac, wsf=WSF, wsh=WSH, bwsf=bWSF, bwsh=bWSH, banks=(0, 1, 2, 3)):
                  i2 = mt % 2
                  em.dma(wsf[i2], winf[mt], writes=[bwsf[i2]])
                  CP("pool", wsh[i2], wsf[i2], [bwsf[i2]], [bwsh[i2]])
                  for tb in range(4):
                      bank = banks[tb % len(banks)]
                      for k in range(8):
                          MM(PS[bank][:, :], wsh[i2][:, k, :], HT[:, k, tb * 512:(tb + 1) * 512], k == 0, k == 7,
                             [bwsh[i2]] + bHT[tb * 4:(tb + 1) * 4], [bPS[bank]])
                      evac(mt, tb, bank)

              bUT = [Buf() for _ in range(4)]; bSZT = [Buf() for _ in range(4)]

              def evac_A(mt, tb, bank):
                  if mt < 4:
                      CP("dve", UT[:, mt, tb * 512:(tb + 1) * 512], PS[bank][:, :], [bPS[bank]], [bUT[mt]])
                  else:
                      ACT(SZT[:, mt - 4, tb * 512:(tb + 1) * 512], PS[bank][:, :], AF.Silu, [bPS[bank]], [bSZT[mt - 4]])
              for mt in range(8):
                  proj_fm(mt, evac_A)

              if stage == 2:
                  raise _Stop()
              bXq = [Buf(), Buf()]
              bHq = [[Buf(), Buf()], [Buf(), Buf()]]
              bZB = [Buf() for _ in range(4)]
              bZG = [Buf() for _ in range(4)]
              bGT = Buf()
              em.barrier()
              GT = [av(O_GT, 2 * KB), av(O_GT + 2 * KB, 2 * KB), av(O_GT + 4 * KB, 2 * KB)]
              for j2 in range(2):
                  MS("pool", HP[j2][:, :, 0:256], 0.0, [bHq[0][j2]])
              def emit_bu(q):
                  kt, qq = divmod(q, 4)
                  X = XX[q % 2]; bx = bXq[q % 2]
                  for ri in range(2):
                      for tb in range(4):
                          bank = 4 + ((ri * 4 + tb) % 4)
                          MM(PS[bank][:, :], WB[64 * (qq // 2):64 * (qq // 2) + 64, qq, kt, ri, :], UT[64 * (qq // 2):64 * (qq // 2) + 64, kt, tb * 512:(tb + 1) * 512],
                             True, True, [bUT[kt], bS5], [bPS[bank]])
                          ACT(X[:, ri, tb * 512:(tb + 1) * 512], PS[bank][:, :], AF.Copy, [bPS[bank]], [bx])

              def emit_taps(kt):
                  UT4 = UT[:, kt, :].rearrange("p (m a) -> p m a", a=4)
                  for a in range(4):
                      for tau in range(a + 1):
                          MM(PS[a][:, :], WK[:, kt, tau, :], UT4[:, :, a - tau], tau == 0, False, [bUT[kt], bS5], [bPS[a]])

              def emit_scan(q):
                  kt, qq = divmod(q, 4)
                  X = XX[q % 2]; bx = bXq[q % 2]
                  X4 = X.rearrange("p r (m a) -> p r m a", a=4)
                  pr1 = PR[:, 1, q:q + 1]; pi1 = PI[:, 1, q:q + 1]; npi1 = NPI[:, 1, q:q + 1]
                  H = HP; bh = bHq[0]
                  src = X4[:, :, :, 0]; bsrc = [bx]
                  cur = 0
                  for a in range(1, 4):
                      dst = H[cur][:, :, 256:768]; bd = bh[cur]
                      STT("dve", dst, src, pr1, X4[:, :, :, a], ALU.mult, ALU.add, bsrc + [bx], [bd])
                      STT("dve", dst[:, 0, :], src[:, 1, :], npi1, dst[:, 0, :], ALU.mult, ALU.add, bsrc + [bd], [bd])
                      STT("dve", dst[:, 1, :], src[:, 0, :], pi1, dst[:, 1, :], ALU.mult, ALU.add, bsrc + [bd], [bd])
                      src = dst; bsrc = [bd]
                      cur = 1 - cur
                  cur = 1 - cur
                  for j in range(9):
                      d = 1 << j
                      a_, n_ = H[cur], H[1 - cur]
                      ba, bn = bh[cur], bh[1 - cur]
                      pk = 5 + j
                      pr = PR[:, pk, q:q + 1]; pi = PI[:, pk, q:q + 1]; npi = NPI[:, pk, q:q + 1]
                      STT("dve", n_[:, :, 256:768], a_[:, :, 256 - d:768 - d], pr, a_[:, :, 256:768], ALU.mult, ALU.add, [ba], [bn])
                      STT("dve", n_[:, 0, 256:768], a_[:, 1, 256 - d:768 - d], npi, n_[:, 0, 256:768], ALU.mult, ALU.add, [ba, bn], [bn])
                      STT("dve", n_[:, 1, 256:768], a_[:, 0, 256 - d:768 - d], pi, n_[:, 1, 256:768], ALU.mult, ALU.add, [ba, bn], [bn])
                      cur = 1 - cur
                  CP("act", ZB[:, qq, :, :], H[cur][:, :, 256:768], [bh[cur]], [bZB[qq]])

              def emit_carry(q):
                  kt, qq = divmod(q, 4)
                  for a in range(4):
                      for ri in range(2):
                          last = (qq == 3 and ri == 1)
                          if qq < 3:
                              MM(PS[a][32 * qq:32 * qq + 32, 1:512], WCA[:, q, a, ri, :], ZB[:, qq, ri, 0:511], False, last,
                                 [bZB[qq], bS5], [bPS[a]])
                          else:
                              MM(PS[a][64:128, 1:512], WCA3[:, kt, a, ri, :], ZB[:, qq, ri, 0:511], False, last,
                                 [bZB[qq], bS5], [bPS[a]])

              def emit_gelu(kt):
                  ZG4 = ZG[:, kt, :].rearrange("p (m a) -> p m a", a=4)
                  for a in range(4):
                      y = PS[a][:, :]
                      ACT(GT[0][:, 0:512], y, AF.Square, [bPS[a], bGT], [bGT])
                      TS("dve", GT[0][:, 0:512], GT[0][:, 0:512], 0.044715, 1.0, ALU.mult, ALU.add, [bGT], [bGT])
                      TT("dve", GT[1][:, 0:512], GT[0][:, 0:512], y, ALU.mult, [bGT, bPS[a]], [bGT])
                      ACT(GT[2][:, 0:512], GT[1][:, 0:512], AF.Sigmoid, [bGT], [bGT], scale=1.5957691216057308)
                      TT("dve", ZG4[:, :, a], GT[2][:, 0:512], y, ALU.mult, [bGT, bPS[a]], [bZG[kt]])

              emit_bu(0)
              for q in range(16):
                  kt, qq = divmod(q, 4)
                  if q + 1 < 16:
                      emit_bu(q + 1)
                  if qq == 0:
                      emit_taps(kt)
                  emit_scan(q)
                  emit_carry(q)
                  if qq == 3:
                      emit_gelu(kt)
              em.barrier()
              if stage == 3:
                  raise _Stop()
              YO = [av(O_XB + i * 1 * KB, 1 * KB, BF16) for i in range(2)]
              bYO = [Buf(), Buf()]
              G1 = av(O_XB + 2 * KB, 2 * KB); G2 = av(O_XB + 4 * KB, 2 * KB); bG1 = Buf()
              bYD = Buf()
              cnt = 0
              for nt in range(4):
                  for tb in range(4):
                      bank = 4 + (cnt % 4)
                      for kc in range(4):
                          MM(PS[bank][:, :], WGLUB[:, kc, nt * 128:(nt + 1) * 128], ZG[:, kc, tb * 512:(tb + 1) * 512], kc == 0, kc == 3,
                             bZG + [bC], [bPS[bank]])
                      ACT(G1[:, 0:512], PS[bank][:, :], AF.Sigmoid, [bPS[bank]], [bG1], bias=BGLU[:, nt:nt + 1])
                      TT("dve", G2[:, 0:512], G1[:, 0:512], ZG[:, nt, tb * 512:(tb + 1) * 512], ALU.mult, [bG1] + bZG, [bG1])
                      yo = YO[cnt % 2]
                      TT("dve", yo[:, 0:512], G2[:, 0:512], SZT[:, nt, tb * 512:(tb + 1) * 512], ALU.mult, [bG1, bSZT[nt]], [bYO[cnt % 2]])
                      em.dma(yst_d[nt, :, tb * 512:(tb + 1) * 512], yo[:, 0:512], reads=[bYO[cnt % 2]], writes=[bYD])
                      cnt += 1
              em.barrier()

              if stage == 4:
                  raise _Stop()
              WSF2 = [av(O_WSB + i * 4 * KB, 4 * KB).rearrange("p (k c) -> p k c", k=8) for i in range(2)]
              WSH2 = [av(O_WSB + 8 * KB + i * 2 * KB, 2 * KB, BF16).rearrange("p (k c) -> p k c", k=8) for i in range(2)]
              bWSF2 = [Buf(), Buf()]; bWSH2 = [Buf(), Buf()]
              SQs = [av(O_WSB + 12 * KB + i * KB, 1 * KB, BF16) for i in range(2)]
              RSs = [av(O_WSB + 14 * KB + i * 2 * KB, 2 * KB) for i in range(2)]
              bSQs = [Buf(), Buf()]; bRSs = [Buf(), Buf()]
              bQT = Buf(); bKT = Buf(); bQIT = Buf(); bKIT = Buf()
              pend = []
              nrm = [0]

              def flush_pend():
                  while pend:
                      pend.pop(0)()

              def evac_B(mt, tb, bank):
                  src = PS[bank][:, :]
                  cs = slice(tb * 512, (tb + 1) * 512)
                  if mt < 16:
                      dst = QT[:, mt - 8, cs] if mt < 12 else KT[:, mt - 12, cs]
                      gn = QG if mt < 12 else KG
                      bb = bQT if mt < 12 else bKT
                      n2 = nrm[0] % 2; nrm[0] += 1
                      SQ = SQs[n2]; RS = RSs[n2]; bSQ = bSQs[n2]; bRS = bRSs[n2]; sbank = 5 + n2
                      ACT(SQ[:, 0:512], src, AF.Square, [bPS[bank]], [bSQ])
                      flush_pend()

                      def rest():
                          MM(PS[sbank][:, :], BONESB[:], SQ[:, 0:512], True, True, [bSQ, bC], [bPS[sbank]])
                          ACT(RS[:, 0:512], PS[sbank][:, :], AF.Ln, [bPS[sbank]], [bRS], bias=EPSC[:, 0:1], scale=1.0 / 64.0)
                          ACT(RS[:, 0:512], RS[:, 0:512], AF.Exp, [bRS], [bRS], scale=-0.5)
                          STT("dve", dst, src, gn[:, 0:1], RS[:, 0:512], ALU.mult, ALU.mult, [bPS[bank], bRS, bL], [bb])
                      pend.append(rest)
                  elif mt < 20:
                      flush_pend()
                      CP("dve", QIT[:, mt - 16, cs], src, [bPS[bank]], [bQIT])
                  else:
                      ACT(KIT[:, cs], src, AF.Copy, [bPS[bank]], [bKIT])
              for mt in range(8, 21):
                  proj_fm(mt, evac_B, WSF2, WSH2, bWSF2, bWSH2, banks=(0, 1, 2, 3))
              flush_pend()
              WTS = av(O_WSB, 4 * KB)
              WTB = av(O_WSB + 4 * KB, 8 * KB, BF16).rearrange("p (k c) -> p k c", k=8)
              bWTS = Buf(); bWTB = Buf(); bVA = Buf(); bAZ = Buf(); bWI = Buf(); bAZD = Buf()
              em.barrier()
              MS("dve", VA[:, :, :, 64:65], 1.0, [bVA])
              AZO = [av(O_WSB + 12 * KB + i * KB, 1 * KB, BF16) for i in range(2)]; bAZO = [Buf(), Buf()]
              for part in range(3):
                  ncol = 512 if part < 2 else 8
                  c0 = part * 512
                  for k in range(8):
                      em.dma(WTS[:, 0:ncol], wint[:, k, c0:c0 + ncol], writes=[bWTS])
                      CP("pool", WTB[:, k, 0:ncol], WTS[:, 0:ncol], [bWTS], [bWTB, bWTS])
                  for tt in range(NT):
                      bank = tt % 4
                      for k in range(8):
                          MM(PS[bank][:, 0:ncol], HT[:, k, tt * 128:(tt + 1) * 128], WTB[:, k, 0:ncol], k == 0, k == 7,
                             [bWTB, bHT[tt]], [bPS[bank]])
                      if part == 0:
                          CP("dve", VA[:, tt, :, 0:64], PS[bank][:, :].rearrange("p (h d) -> p h d", h=8), [bPS[bank]], [bVA])
                      elif part == 1:
                          ACT(AZO[tt % 2][:, 0:512], PS[bank][:, :], AF.Silu, [bPS[bank]], [bAZO[tt % 2]])
                          em.dma(az_d[tt * 128:(tt + 1) * 128, :], AZO[tt % 2][:, 0:512], reads=[bAZO[tt % 2]], writes=[bAZD])
                      else:
                          TS("dve", WI[:, tt, :], PS[bank][:, 0:8], 8.0 ** -0.5, None, ALU.mult, None, [bPS[bank]], [bWI])
              em.barrier()

              if stage == 5:
                  raise _Stop()
              SCs = [av(0, 8 * KB), av(O_C2 + 16 * KB, 8 * KB)]
              NM = av(8 * KB, 4 * KB, BF16)
              NMT = [av(12 * KB + i * 4 * KB, 4 * KB, BF16).rearrange("p (j t) -> p j t", j=16) for i in range(2)]
              PT = [av(20 * KB + i * KB, 1 * KB, BF16).rearrange("p (h t) -> p h t", h=4) for i in range(4)]
              TMPH = [av(24 * KB + i * 2 * KB, 2 * KB) for i in range(2)]
              XR_ = av(28 * KB, 4 * KB)
              OT = av(O_C2, 4 * KB)
              GATEB = av(O_C2 + 4 * KB, 4 * KB)
              CAT = av(O_C2 + 8 * KB, 2 * KB, BF16).rearrange("p (k t) -> p k t", k=8)
              AZT = av(O_C2 + 10 * KB, 1 * KB, BF16)
              YA = av(O_C2 + 11 * KB, 2 * KB)
              YAB = av(O_C2 + 13 * KB, 1 * KB, BF16)
              RD = SMALL[:, 8:16]
              bSCs = [Buf(), Buf()]; bBIs = [Buf(), Buf()]; bNM = Buf(); bNMT = [Buf(), Buf()]; bPT = [Buf() for _ in range(4)]; bTMPH = [Buf(), Buf()]
              bXR = Buf(); bOT = Buf(); bGB = Buf(); bCAT = Buf(); bAZT = Buf(); bYA = Buf(); bBI = Buf(); bRD = Buf()
              for hh in range(2):
                  MM(PS[6 + hh][:, :], SEL[0:4, b, :], MODG[0:4, hh * 512:(hh + 1) * 512], True, True, [bC, bL], [bPS[6 + hh]])
                  CP("dve", GATEB[:, hh * 512:(hh + 1) * 512], PS[6 + hh][:, :], [bPS[6 + hh]], [bGB])
              LO = SMALL[:, 16:17]; HI = SMALL[:, 17:18]; TAU = SMALL[:, 18:19]; CNTc = SMALL[:, 19:20]; TF = SMALL[:, 20:21]
              STEPS = SMALL[:, 24:24 + NBIS + 1]
              NTAU = SMALL[:, 21:22]; SSUM = SMALL[:, 22:23]; TSG = SMALL[:, 23:24]
              STEPN = SMALL[:, 24:24 + NBIS + 1]
              ptc = [0]
              PSB6 = PS[6].bitcast(BF16); PSB7 = PS[7].bitcast(BF16)

              SM2 = [SMALL[:, 16:16 + 24], SMALL2[:, 0:24]]

              def scal(i):
                  sm = SM2[i % 2]
                  return dict(LO=sm[:, 0:1], HI=sm[:, 1:2], TAU=sm[:, 2:3], TF=sm[:, 3:4], NTAU=sm[:, 4:5], SSUM=sm[:, 5:6],
                              TSG=sm[:, 6:7], STEPN=sm[:, 8:8 + NBIS + 1])

              def fidx(i):
                  nk = (i + 1) * 128
                  ngr = (nk + 511) // 512
                  SC = SCs[i % 2]; bSC = bSCs[i % 2]; bBI = bBIs[i % 2]; v = scal(i)
                  for g in range(ngr):
                      k0 = g * 512; kn = min(512, nk - k0)
                      for h in range(8):
                          bank = h % 2
                          base = 64 * (h % 2)
                          MM(PS[bank][:, 0:kn], QIT[base:base + 64, h // 2, i * 128:(i + 1) * 128], KIT[base:base + 64, k0:k0 + kn],
                             True, True, [bQIT, bKIT], [bPS[bank]])
                          if h == 0:
                              TS("dve", SC[:, k0:k0 + kn], PS[bank][:, 0:kn], 0.0, WI[:, i, h:h + 1], ALU.max, ALU.mult,
                                 [bPS[bank], bWI], [bSC])
                          else:
                              th = TMPH[h % 2]
                              TS("dve", th[:, 0:kn], PS[bank][:, 0:kn], 0.0, WI[:, i, h:h + 1], ALU.max, ALU.mult,
                                 [bPS[bank], bWI], [bTMPH[h % 2]])
                              TT("pool", SC[:, k0:k0 + kn], SC[:, k0:k0 + kn], th[:, 0:kn], ALU.add, [bTMPH[h % 2], bSC], [bSC])
                          yield
                  HI, LO, TF, NTAU, STEPN = v["HI"], v["LO"], v["TF"], v["NTAU"], v["STEPN"]
                  em.op("dve", lambda e: e.tensor_reduce(out=HI, in_=SC[:, 0:nk], axis=mybir.AxisListType.X, op=ALU.max), [bSC], [bBI])
                  em.op("dve", lambda e: e.tensor_reduce(out=LO, in_=SC[:, 0:nk], axis=mybir.AxisListType.X, op=ALU.min), [bSC], [bBI])
                  TT("dve", SC[:, i * 128:(i + 1) * 128], SC[:, i * 128:(i + 1) * 128], TRI[:], ALU.add, [bSC, bL], [bSC])
                  TT("dve", TF, HI, LO, ALU.subtract, [bBI], [bBI])
                  STT("dve", LO, TF, -0.01, LO, ALU.mult, ALU.add, [bBI], [bBI])
                  TT("dve", TF, HI, LO, ALU.subtract, [bBI], [bBI])
                  STT("dve", NTAU, TF, -0.5, LO, ALU.mult, ALU.subtract, [bBI], [bBI])
                  TS("dve", STEPN, STEPC[:, 0:NBIS + 1], TF, None, ALU.mult, None, [bBI, bC], [bBI])
                  yield

              def n_fidx(i):
                  return (((i + 1) * 128 + 511) // 512) * 8 + 1

              def fbis(i):
                  nk = (i + 1) * 128
                  SC = SCs[i % 2]; bSC = bSCs[i % 2]; bBI = bBIs[i % 2]; v = scal(i)
                  NTAU, SSUM, TSG, STEPN, TAU = v["NTAU"], v["SSUM"], v["TSG"], v["STEPN"], v["TAU"]
                  for kq in range(NBIS):
                      ACT(NM[:, 0:nk], SC[:, 0:nk], AF.Sign, [bSC, bBI], [bNM, bBI], bias=NTAU, scale=1.0, accum=SSUM)
                      ACT(TSG, SSUM, AF.Sign, [bBI], [bBI], bias=float(nk) - (2.0 * TOPK - 0.5), scale=1.0)
                      ACT(NTAU, TSG, AF.Identity, [bBI], [bBI], bias=NTAU, scale=STEPN[:, kq:kq + 1])
                      yield
                  STT("dve", TAU, NTAU, -1.0, STEPN[:, NBIS:NBIS + 1], ALU.mult, ALU.add, [bBI], [bBI])
                  em.op("dve", lambda e: e.tensor_scalar(out=NM[:, 0:nk], in0=SC[:, 0:nk], scalar1=TAU, scalar2=NEG,
                                                         op0=ALU.is_lt, op1=ALU.mult), [bSC, bBI], [bNM])
                  nmt = NMT[i % 2]; bnmt = bNMT[i % 2]
                  for j0 in range(0, i + 1, 8):
                      jn = min(8, i + 1 - j0)
                      for jj in range(jn):
                          j = j0 + jj
                          TR(PSB6[:, jj * 128:(jj + 1) * 128], NM[:, j * 128:(j + 1) * 128], IDENTB[:], [bNM, bC], [bPS[6]])
                      CP("dve", nmt[:, j0:j0 + jn, :], PSB6[:, 0:jn * 128].rearrange("p (j t) -> p j t", j=jn), [bPS[6]], [bnmt])
                  yield

              def n_fbis(i):
                  return NBIS + 1

              def attn(i):
                  nmt = NMT[i % 2]; bnmt = bNMT[i % 2]
                  pending = []
                  for h in range(8):
                      base = 64 * (h % 2); ch = h // 2; ob = 4 + h // 4; hh = h % 4
                      for j0 in range(0, i + 1, 4):
                          jn = min(4, i + 1 - j0)
                          bank = 2 + (ptc[0] % 2)
                          pt = PT[ptc[0] % 4]; bpt = bPT[ptc[0] % 4]; ptc[0] += 1
                          for jj in range(jn):
                              j = j0 + jj
                              dlt = (i - j) * 128
                              reg = PS[bank][:, jj * 128:(jj + 1) * 128]
                              MM(reg, IDENTB[:], nmt[:, j, :], True, False, [bnmt, bC], [bPS[bank]])
                              if dlt <= NEAR_MAX:
                                  MM(reg, JB[:], TREVB[:, h, dlt:dlt + 128], False, False, [bC], [bPS[bank]])
                              MM(reg, KT[base:base + 64, ch, j * 128:(j + 1) * 128],
                                 QT[base:base + 64, ch, i * 128:(i + 1) * 128], False, True, [bQT, bKT], [bPS[bank]])
                          ACT(pt[:, 0:jn, :], PS[bank][:, 0:jn * 128].rearrange("p (h t) -> p h t", h=jn), AF.Exp, [bPS[bank]], [bpt])
                          while pending:
                              pending.pop(0)()

                          def pv(pt=pt, bpt=bpt, j0=j0, jn=jn, h=h, ob=ob, hh=hh):
                              for jj in range(jn):
                                  j = j0 + jj
                                  MM(PS[ob][:, hh * 65:(hh + 1) * 65], pt[:, jj, :], VA[:, j, h, :], j == 0, j == i, [bpt, bVA], [bPS[ob]])
                          pending.append(pv)
                          yield
                  while pending:
                      pending.pop(0)()
                  yield

              def n_attn(i):
                  return 8 * ((i + 4) // 4) + 1

              def back_a(i):
                  em.dma(AZT[:, 0:512], az_d[i * 128:(i + 1) * 128, :], reads=[bAZD], writes=[bAZT])
                  em.dma(CAT[:, 0:4, :], yst_d[:, :, i * 128:(i + 1) * 128].rearrange("k p t -> p k t"), reads=[bYD], writes=[bCAT])
                  em.dma(XR_, x[b, i * 128:(i + 1) * 128, :], reads=[bOT], writes=[bXR])
                  for g in range(2):
                      O3 = PS[4 + g][:, 0:260].rearrange("p (h d) -> p h d", h=4)
                      em.op("dve", lambda e, O3=O3, g=g: e.reciprocal(out=RD[:, 4 * g:4 * g + 4], in_=O3[:, :, 64]), [bPS[4 + g]], [bRD])
                      TT("dve", YA[:, g * 256:(g + 1) * 256].rearrange("p (h d) -> p h d", h=4), O3[:, :, 0:64],
                         RD[:, 4 * g:4 * g + 4].unsqueeze(2).to_broadcast([128, 4, 64]), ALU.mult, [bPS[4 + g], bRD], [bYA])
                  TT("dve", YAB[:, 0:512], YA[:, 0:512], AZT[:, 0:512], ALU.mult, [bYA, bAZT], [bYA])

              def back_b(i):
                  for c in range(4):
                      TR(PSB7[:, c * 128:(c + 1) * 128], YAB[:, c * 128:(c + 1) * 128], IDENTB[:], [bYA, bC], [bPS[7]])
                  CP("dve", CAT[:, 4:8, :], PSB7[:, 0:512].rearrange("p (c t) -> p c t", c=4), [bPS[7]], [bCAT])
                  yield
                  for hh in range(2):
                      bank = 7
                      for k in range(8):
                          MM(PS[bank][:, :], CAT[:, k, :], WOUTB[:, k, hh * 512:(hh + 1) * 512], k == 0, k == 7, [bCAT, bC], [bPS[bank]])
                      TT("dve", OT[:, hh * 512:(hh + 1) * 512], PS[bank][:, :], GATEB[:, hh * 512:(hh + 1) * 512], ALU.mult,
                         [bPS[bank], bGB], [bOT])
                      TT("dve", OT[:, hh * 512:(hh + 1) * 512], OT[:, hh * 512:(hh + 1) * 512], XR_[:, hh * 512:(hh + 1) * 512], ALU.add,
                         [bOT, bXR], [bOT])
                      if hh == 1:
                          em.dma(out[b, i * 128:(i + 1) * 128, :], OT, reads=[bOT], writes=[bOT])
                      yield

              def merge(streams):
                  prog = [0] * len(streams)
                  while True:
                      best = None
                      for idx, (g, n) in enumerate(streams):
                          if prog[idx] < n and (best is None or prog[idx] * streams[best][1] < prog[best] * n):
                              best = idx
                      if best is None:
                          break
                      next(streams[best][0], None)
                      prog[best] += 1
                  for g, n in streams:
                      for _ in g:
                          pass

              merge([(fidx(0), n_fidx(0))])
              merge([(fidx(1), n_fidx(1)), (fbis(0), n_fbis(0))])
              for i in range(NT):
                  st = []
                  if PHASEC_MODE in ("all", "attn"):
                      st.append((attn(i), n_attn(i)))
                  if PHASEC_MODE in ("all", "front"):
                      if i + 2 < NT:
                          st.append((fidx(i + 2), n_fidx(i + 2)))
                      if i + 1 < NT:
                          st.append((fbis(i + 1), n_fbis(i + 1)))
                  if PHASEC_MODE in ("all", "attn") and i > 0:
                      st.append((back_b(i - 1), 3))
                  merge(st)
                  if PHASEC_MODE in ("all", "attn"):
                      back_a(i)
              if PHASEC_MODE in ("all", "attn"):
                  merge([(back_b(NT - 1), 3)])
              em.barrier()
          except _Stop:
            break
        em.barrier()
        em.replay()
    return nc


def t5_bucket(dist):
    max_exact = 16
    d = np.maximum(dist, 1).astype(np.float32)
    large = max_exact + (np.log(d / max_exact) / np.float32(math.log(128 / max_exact)) * (32 - max_exact)).astype(np.int32)
    large = np.minimum(large, 31)
    return np.where(dist < max_exact, dist, large)


def host_consts():
    c = {}
    c["c_ident"] = np.eye(128, dtype=np.float32)
    c["c_jrev"] = np.ascontiguousarray(np.eye(128, dtype=np.float32)[::-1])
    bo = np.zeros((128, 128), np.float32); bo[:64, :64] = 1; bo[64:, 64:] = 1
    c["c_bones"] = bo
    t = np.arange(128)
    c["c_tri"] = np.where(t[None, :] <= t[:, None], 0.0, -1e9).astype(np.float32)
    sel = np.zeros((4, 4, 128), np.float32)
    for b in range(4):
        sel[b, b, :] = 1
    c["c_sel"] = sel
    oh = np.zeros((32, 384), np.float32)
    for i in range(127, 384):
        d = i - 127
        oh[t5_bucket(np.array([d]))[0], i] += 1.0
        oh[31, i] -= 1.0
    c["c_oh"] = oh
    p = np.arange(128)
    mrow = np.zeros((128, 8), np.float32); mrow[p, (p // 32) * 2 + ((p // 16) % 2)] = 1
    mcol = np.zeros((128, 2), np.float32); mcol[p, p // 64] = 1
    c["c_mrow"] = mrow; c["c_mcol"] = mcol
    return c


def host_layout(inp):
    f = lambda a: np.ascontiguousarray(a, dtype=np.float32)
    m = {}
    m["wada"] = f(inp["w_ada"][0].reshape(8, 128, 3072).transpose(1, 0, 2))
    m["bada4"] = f(np.broadcast_to(inp["b_ada"][0][None, :], (4, 3072)))
    m["ng"] = f(inp["norm_g"][0].reshape(8, 128).T)
    w = inp["w_in"][0]
    cols_fm = np.concatenate([np.arange(0, 512), np.arange(512, 1024), np.arange(1024, 1536), np.arange(1536, 2048),
                              np.arange(3072, 3584), np.arange(3584, 3648), np.arange(3584, 3648)])
    wf = w[:, cols_fm]
    m["winf"] = f(wf.reshape(8, 128, 21, 128).transpose(2, 1, 0, 3))
    cols_tm = np.concatenate([np.arange(2048, 2560), np.arange(2560, 3072), np.arange(3648, 3656)])
    m["wint"] = f(w[:, cols_tm].reshape(8, 128, 1032).transpose(1, 0, 2))
    m["wout"] = f(inp["w_out"][0].reshape(8, 128, 1024).transpose(1, 0, 2))
    m["wglu"] = f(inp["w_glu"][0].reshape(4, 128, 512).transpose(1, 0, 2))
    m["bglu"] = f(inp["b_glu"][0].reshape(4, 128).T)
    m["dskip"] = f(inp["d_skip"][0].reshape(4, 128).T)
    m["qg2"] = f(np.tile(inp["q_gain"][0], 2).reshape(128, 1))
    m["kg2"] = f(np.tile(inp["k_gain"][0], 2).reshape(128, 1))
    m["relb"] = f(inp["rel_bias"])
    are, aim, ldt = inp["a_re"][0], inp["a_im"][0], inp["log_dt"][0]
    col = lambda a: f(a.reshape(16, 2, 64).transpose(1, 2, 0).reshape(128, 16))
    m["are_c"] = col(are); m["aim_c"] = col(aim); m["ldt_c"] = col(np.repeat(ldt[:, None], 64, 1))
    ccol = lambda a: f(a.reshape(16, 2, 16, 64).transpose(1, 3, 0, 2).reshape(128, 16, 16))
    m["cre_c"] = ccol(inp["c_re"][0]); m["cim_c"] = ccol(inp["c_im"][0])
    row = lambda a: f(np.repeat(a.reshape(4, 8, 64).transpose(1, 0, 2)[:, None, :, :], 16, 1).reshape(128, 256))
    m["are_r"] = row(are); m["aim_r"] = row(aim); m["ldt_r"] = row(np.repeat(ldt[:, None], 64, 1))
    brow = lambda a: f(a.reshape(4, 8, 64, 16).transpose(1, 3, 0, 2).reshape(128, 256))
    m["bre_r"] = brow(inp["b_re"][0]); m["bim_r"] = brow(inp["b_im"][0])
    bcol = lambda a: f(a.reshape(16, 2, 64, 16).transpose(1, 2, 0, 3).reshape(128, 16, 16))
    m["bre_c"] = bcol(inp["b_re"][0]); m["bim_c"] = bcol(inp["b_im"][0])
    m["dsk16"] = f(inp["d_skip"][0].reshape(32, 16).T)
    m.update(host_consts())
    return m


_NC_CACHE = {}


def kernel(**inputs):
    inp = {k: np.asarray(v) for k, v in inputs.items()}
    n_cores = 8
    shared = host_layout(inp)
    x = np.ascontiguousarray(inp["x"], dtype=np.float32)
    c = np.asarray(inp["c"], dtype=np.float32)
    in_maps = []
    for r in range(n_cores):
        m = dict(shared)
        m["x"] = x[r * NSEQ:(r + 1) * NSEQ]
        m["ct"] = np.ascontiguousarray(c[r * NSEQ:(r + 1) * NSEQ].T.reshape(8, 128, NSEQ).transpose(1, 0, 2))
        in_maps.append(m)
    if "nc" not in _NC_CACHE:
        _NC_CACHE["nc"] = build_program()
    res = run_bass_kernel_spmd(_NC_CACHE["nc"], in_maps, core_ids=list(range(n_cores)))
    outs = [np.asarray(res.results[r]["out"]) for r in range(n_cores)]
    return np.concatenate(outs, axis=0).astype(np.float32)
```

```python
import math
import numpy as np
import concourse.bass as bass
import concourse.mybir as mybir
from concourse.bass_utils import run_bass_kernel_spmd
from contextlib import ExitStack

F32 = mybir.dt.float32
BF16 = mybir.dt.bfloat16
ALU = mybir.AluOpType
AF = mybir.ActivationFunctionType

S = 2048
D = 1024
NSEQ = 4
NT = 16
EPS = 1e-6
NEG = -30000.0
TOPK = 256
NBIS = 12
DEBUG_STAGE = None
NEAR_MAX = 128
A_BF16_TR = False
A_DMA2 = False
PHASEC_MODE = "all"
FBIS_EF = 1.0
BACK_EF = 1.0


class Buf:
    __slots__ = ("w", "r")

    def __init__(self):
        self.w = None
        self.r = {}


class Emitter:
    ENGS = ("pe", "act", "dve", "pool", "sp")

    def __init__(self, nc, es, n_dma_sems=8):
        self.nc = nc
        self.sem = {}
        self.cnt = {}
        self.prog = {e: [] for e in self.ENGS}
        self.waited = {e: {} for e in self.ENGS}
        for e in self.ENGS:
            self.sem[e] = es.enter_context(nc.semaphore("s_" + e))
            self.cnt[e] = 0
        self.dma_keys = []
        for i in range(n_dma_sems):
            k = "dma%d" % i
            self.sem[k] = es.enter_context(nc.semaphore("s_" + k))
            self.cnt[k] = 0
            self.dma_keys.append(k)
        self.dma_rr = 0

    def _deps(self, eng, reads, writes):
        deps = {}

        def add(tok):
            if tok is None:
                return
            k, c = tok
            if deps.get(k, 0) < c:
                deps[k] = c
        for b in reads:
            add(b.w)
        for b in writes:
            if b.w is not None and b.w[0] != eng:
                add(b.w)
            for k, c in b.r.items():
                if k != eng:
                    add((k, c))
        return deps

    def _emit_waits(self, eng, deps):
        w = self.waited[eng]
        for k, c in deps.items():
            if w.get(k, 0) >= c:
                continue
            w[k] = c
            val = c * 16 if k.startswith("dma") else c
            self.prog[eng].append(("wait", self.sem[k], val))

    def _post(self, key, tok, reads, writes):
        for b in reads:
            if b.r.get(key, 0) < tok[1]:
                b.r[key] = tok[1]
        for b in writes:
            b.w = tok
            b.r = {}

    def op(self, eng, fn, reads=(), writes=()):
        self._emit_waits(eng, self._deps(eng, reads, writes))
        self.cnt[eng] += 1
        tok = (eng, self.cnt[eng])
        self.prog[eng].append(("op", fn, self.sem[eng], 1))
        self._post(eng, tok, reads, writes)
        return tok

    def dma(self, out, in_, reads=(), writes=(), q="sp"):
        self._emit_waits(q, self._deps(q, reads, writes))
        k = self.dma_keys[self.dma_rr % len(self.dma_keys)]
        self.dma_rr += 1
        self.cnt[k] += 1
        tok = (k, self.cnt[k])
        self.prog[q].append(("op", (lambda e: e.dma_start(out=out, in_=in_)), self.sem[k], 16))
        self._post(k, tok, reads, writes)
        return tok

    def barrier(self):
        snap = {k: c for k, c in self.cnt.items() if c > 0}
        for e in self.ENGS:
            self._emit_waits(e, dict(snap))

    def replay(self):
        with self.nc.Block() as block:
            def mk(eng):
                def body(e):
                    for item in self.prog[eng]:
                        if item[0] == "wait":
                            e.wait_ge(item[1], item[2])
                        else:
                            item[1](e).then_inc(item[2], item[3])
                return body
            block.tensor(mk("pe"))
            block.scalar(mk("act"))
            block.vector(mk("dve"))
            block.gpsimd(mk("pool"))
            block.sync(mk("sp"))


class _Stop(Exception):
    pass


def build_program(nseq=NSEQ, stage=99):
    nc = bass.Bass("TRN2", target_bir_lowering=False)

    def din(name, shape, dt=F32):
        return nc.dram_tensor(name, list(shape), dt, kind="ExternalInput")

    x_t = din("x", [nseq, S, D]); x = x_t.ap()
    ct = din("ct", [128, 8, 4]).ap()
    wada = din("wada", [128, 8, 3072]).ap()
    bada4 = din("bada4", [4, 3072]).ap()
    ng = din("ng", [128, 8]).ap()
    winf = din("winf", [21, 128, 8, 128]).ap()
    wint = din("wint", [128, 8, 1032]).ap()
    wout = din("wout", [128, 8, 1024]).ap()
    wglu = din("wglu", [128, 4, 512]).ap()
    bglu = din("bglu", [128, 4]).ap()
    dskip = din("dskip", [128, 4]).ap()
    qg2 = din("qg2", [128, 1]).ap()
    kg2 = din("kg2", [128, 1]).ap()
    relb = din("relb", [32, 8]).ap()
    are_c = din("are_c", [128, 16]).ap(); aim_c = din("aim_c", [128, 16]).ap(); ldt_c = din("ldt_c", [128, 16]).ap()
    cre_c = din("cre_c", [128, 16, 16]).ap(); cim_c = din("cim_c", [128, 16, 16]).ap()
    are_r = din("are_r", [128, 256]).ap(); aim_r = din("aim_r", [128, 256]).ap(); ldt_r = din("ldt_r", [128, 256]).ap()
    bre_r = din("bre_r", [128, 256]).ap(); bim_r = din("bim_r", [128, 256]).ap()
    bre_c = din("bre_c", [128, 16, 16]).ap(); bim_c = din("bim_c", [128, 16, 16]).ap()
    dsk16 = din("dsk16", [16, 32]).ap()
    c_ident = din("c_ident", [128, 128]).ap(); c_jrev = din("c_jrev", [128, 128]).ap()
    c_bones = din("c_bones", [128, 128]).ap(); c_tri = din("c_tri", [128, 128]).ap()
    c_sel = din("c_sel", [4, 4, 128]).ap(); c_oh = din("c_oh", [32, 384]).ap()
    c_mrow = din("c_mrow", [128, 8]).ap(); c_mcol = din("c_mcol", [128, 2]).ap()
    out = nc.dram_tensor("out", [nseq, S, D], F32, kind="ExternalOutput").ap()
    dbg = nc.dram_tensor("dbg", [128, 1024], F32, kind="ExternalOutput").ap() if DEBUG_STAGE is not None else None
    tb_t = nc.dram_tensor("tb_d", [8, 384], F32)
    yst_d = nc.dram_tensor("yst_d", [4, 128, S], BF16).ap()
    az_d = nc.dram_tensor("az_d", [S, 512], BF16).ap()

    es = ExitStack()
    with es:
        em = Emitter(nc, es)

        def sb(name, shape, dt=F32):
            return es.enter_context(nc.sbuf_tensor(name, list(shape), dt))

        PS = [es.enter_context(nc.psum_tensor("ps%d" % i, [128, 512], F32)) for i in range(8)]
        bPS = [Buf() for _ in range(8)]

        def MM(o, l, r, st, sp, rd, wr):
            return em.op("pe", lambda e: e.matmul(o, lhsT=l, rhs=r, start=st, stop=sp), rd, wr)

        def TR(o, i, idn, rd, wr):
            return em.op("pe", lambda e: e.transpose(out=o, in_=i, identity=idn), rd, wr)

        def ACT(o, i, func, rd, wr, bias=None, scale=None, accum=None):
            kw = {}
            if bias is not None:
                kw["bias"] = bias
            if scale is not None:
                kw["scale"] = scale
            if accum is not None:
                kw["accum_out"] = accum
            return em.op("act", lambda e: e.activation(out=o, in_=i, func=func, **kw), rd, wr)

        def TS(eng, o, i, s1, s2, op0, op1, rd, wr, accum=None):
            if s2 is None:
                return em.op(eng, lambda e: e.tensor_scalar(out=o, in0=i, scalar1=s1, scalar2=None, op0=op0), rd, wr)
            if accum is None:
                return em.op(eng, lambda e: e.tensor_scalar(out=o, in0=i, scalar1=s1, scalar2=s2, op0=op0, op1=op1), rd, wr)
            return em.op(eng, lambda e: e.tensor_scalar(out=o, in0=i, scalar1=s1, scalar2=s2, op0=op0, op1=op1, accum_out=accum), rd, wr)

        def TT(eng, o, a, b, op, rd, wr):
            return em.op(eng, lambda e: e.tensor_tensor(out=o, in0=a, in1=b, op=op), rd, wr)

        def STT(eng, o, i0, sc, i1, op0, op1, rd, wr):
            return em.op(eng, lambda e: e.scalar_tensor_tensor(out=o, in0=i0, scalar=sc, in1=i1, op0=op0, op1=op1), rd, wr)

        def CP(eng, o, i, rd, wr):
            if eng == "act":
                return em.op("act", lambda e: e.copy(out=o, in_=i), rd, wr)
            return em.op(eng, lambda e: e.tensor_copy(out=o, in_=i), rd, wr)

        def MS(eng, o, v, wr):
            return em.op(eng, lambda e: e.memset(o, v), (), wr)

        NA = 36864
        ARENA = sb("arena", [128, NA], F32)

        def av(off_b, nbytes, dt=F32):
            sl = ARENA[:, off_b // 4:(off_b + nbytes) // 4]
            return sl.bitcast(BF16) if dt == BF16 else sl

        KB = 1024
        IDENT = sb("IDENT", [128, 128]); IDENTB = sb("IDENTB", [128, 128], BF16)
        JB = sb("JB", [128, 128], BF16); BONESB = sb("BONESB", [128, 128], BF16)
        TRI = sb("TRI", [128, 128]); SEL = sb("SEL", [4, 4, 128])
        MROW = sb("MROW", [128, 8]); MCOL = sb("MCOL", [128, 2]); NMCOL = sb("NMCOL", [128, 2])
        MODG = sb("MODG", [4, 1024]); SHSC = sb("SHSC", [128, 16, 4]); AMOD = sb("AMOD", [128, 8, 4])
        NG = sb("NG", [128, 8]); QG = sb("QG", [128, 1]); KG = sb("KG", [128, 1])
        BGLU = sb("BGLU", [128, 4]); DSKIP = sb("DSKIP", [128, 4])
        TREVB = sb("TREVB", [128, 8, 256], BF16)
        WOUTB = sb("WOUTB", [128, 8, 1024], BF16)
        WGLUB = sb("WGLUB", [128, 4, 512], BF16)
        WB = sb("WB", [128, 4, 4, 2, 128], BF16)
        WK = sb("WK", [128, 4, 4, 128], BF16)
        WCA = sb("WCA", [128, 16, 4, 2, 32], BF16)
        WCA3 = sb("WCA3", [128, 4, 4, 2, 64], BF16)
        PR = sb("PR", [128, 17, 16]); PI = sb("PI", [128, 17, 16]); NPI = sb("NPI", [128, 17, 16])
        WI = sb("WI", [128, 16, 8])
        SMALL = sb("SMALL", [128, 64])
        SMALL2 = sb("SMALL2", [128, 32])
        SMALL3 = sb("SMALL3", [128, 32])
        STEPC = sb("STEPC", [128, 32])
        EPSC = sb("EPSC", [128, 1])
        bC = Buf()

        ld = [(IDENT[:], c_ident), (TRI[:], c_tri), (SEL[:], c_sel), (MROW[:], c_mrow), (MCOL[:], c_mcol),
              (NG[:], ng), (QG[:], qg2), (KG[:], kg2), (BGLU[:], bglu), (DSKIP[:], dskip)]
        bL = Buf()
        for o_, i_ in ld:
            em.dma(o_, i_, writes=[bL])
        ST0 = av(0, 12 * KB)[:, 0:3072]; ST1 = av(12 * KB, 12 * KB)[:, 0:3072]
        bST = [Buf(), Buf()]
        TMPF = av(24 * KB, 4 * KB)
        bTMP = Buf()
        em.dma(TMPF[:, 0:128], c_jrev, writes=[bTMP])
        CP("dve", JB[:], TMPF[:, 0:128], [bTMP], [bC, bTMP])
        em.dma(TMPF[:, 128:256], c_bones, writes=[bTMP])
        CP("dve", BONESB[:], TMPF[:, 128:256], [bTMP], [bC, bTMP])
        CP("dve", IDENTB[:], IDENT[:], [bL], [bC])
        TS("dve", NMCOL[:], MCOL[:], -1.0, None, ALU.mult, None, [bL], [bC])
        TS("dve", QG[:], QG[:], 0.125, None, ALU.mult, None, [bL], [bL])

        CT = av(28 * KB, 128)[:, 0:32].rearrange("p (k b) -> p k b", k=8)
        COND = av(29 * KB, 128)[:, 0:32].rearrange("p (k b) -> p k b", k=8)
        bCT = Buf(); bCOND = Buf()
        em.dma(CT, ct, writes=[bCT])
        ACT(COND, CT, AF.Silu, [bCT], [bCOND])
        STs = [ST0, ST1]
        for k in range(8):
            em.dma(STs[k % 2], wada[:, k, :], writes=[bST[k % 2]])
            for blk in range(6):
                MM(PS[blk][0:4, 0:512], COND[:, k, :], STs[k % 2][:, blk * 512:(blk + 1) * 512], k == 0, k == 7,
                   [bCOND, bST[k % 2]], [bPS[blk]])
        MODROW = av(30 * KB, 12 * KB)[0:4, 0:3072]
        BADA = av(42 * KB, 12 * KB)[0:4, 0:3072]
        bMR = Buf(); bBA = Buf()
        em.dma(BADA, bada4, writes=[bBA])
        for blk in range(6):
            TT("dve", MODROW[:, blk * 512:(blk + 1) * 512], PS[blk][0:4, 0:512], BADA[:, blk * 512:(blk + 1) * 512], ALU.add,
               [bPS[blk], bBA], [bMR])
        CP("dve", MODG[:], MODROW[:, 2048:3072], [bMR], [bC])
        for c in range(16):
            TR(PS[6][:, c * 4:(c + 1) * 4], MODROW[0:4, c * 128:(c + 1) * 128], IDENT[0:4, 0:4], [bMR, bL], [bPS[6]])
        CP("dve", SHSC[:].rearrange("p a b -> p (a b)"), PS[6][:, 0:64], [bPS[6]], [bC])
        for b in range(4):
            TS("dve", AMOD[:, :, b], SHSC[:, 8:16, b], 1.0, None, ALU.add, None, [bC], [bC])
            TT("dve", AMOD[:, :, b], AMOD[:, :, b], NG[:], ALU.mult, [bC, bL], [bC])

        RB = av(54 * KB, 32)[0:32, 0:8]
        OHS = av(55 * KB, 1536)[0:32, 0:384]
        WROW = av(57 * KB, 1536)[0:8, 0:384]
        bRB = Buf(); bWR = Buf(); bTBD = Buf(); bTV = Buf()
        em.dma(RB, relb, writes=[bRB]); em.dma(OHS, c_oh, writes=[bRB])
        MM(PS[7][0:8, 0:384], RB, OHS, True, True, [bRB], [bPS[7]])
        CP("dve", WROW, PS[7][0:8, 0:384], [bPS[7]], [bWR])
        em.dma(tb_t.ap(), WROW, reads=[bWR], writes=[bTBD])
        TREVF = av(60 * KB, 8 * KB)[:, 0:2048].rearrange("p (h c) -> p h c", h=8)
        em.dma(TREVF, bass.AP(tb_t, 0, [[1, 128], [384, 8], [1, 256]]), reads=[bTBD], writes=[bTV])
        CP("dve", TREVB[:], TREVF, [bTV], [bC])

        for k in range(8):
            stg = TMPF
            em.dma(stg, wout[:, k, :], writes=[bTMP])
            CP("pool", WOUTB[:, k, :], stg, [bTMP], [bC, bTMP])
        for k in range(4):
            em.dma(TMPF[:, 0:512], wglu[:, k, :], writes=[bTMP])
            CP("pool", WGLUB[:, k, :], TMPF[:, 0:512], [bTMP], [bC, bTMP])

        TWO_PI = 2.0 * math.pi
        MAGIC = 12582912.0

        def s5_derive(n, ARE, AIM, LDT, base_b, tagbufs):
            names = ["DT", "TH", "T1", "ER", "Y", "K1", "R", "SN", "CS", "LR", "LI", "NR", "DEN", "CR", "CI", "T2", "T3"]
            t = {}
            for idx, nm in enumerate(names):
                t[nm] = av(base_b + idx * n * 4, n * 4)[:, 0:n]
            b = tagbufs
            ACT(t["DT"], LDT, AF.Exp, [b], [b])
            TT("dve", t["TH"], AIM, t["DT"], ALU.mult, [b], [b])
            TT("dve", t["T1"], ARE, t["DT"], ALU.mult, [b], [b])
            ACT(t["ER"], t["T1"], AF.Exp, [b], [b])

            def sincos(dst, shift):
                TS("dve", t["T2"], t["TH"], shift, None, ALU.add, None, [b], [b])
                TS("dve", t["Y"], t["T2"], 1.0 / TWO_PI, None, ALU.mult, None, [b], [b])
                TS("dve", t["K1"], t["Y"], MAGIC, None, ALU.add, None, [b], [b])
                TS("dve", t["K1"], t["K1"], MAGIC, None, ALU.subtract, None, [b], [b])
                STT("dve", t["R"], t["K1"], -TWO_PI, t["T2"], ALU.mult, ALU.add, [b], [b])
                TS("dve", t["R"], t["R"], -3.14159, 3.14159, ALU.max, ALU.min, [b], [b])
                ACT(dst, t["R"], AF.Sin, [b], [b])
            sincos(t["SN"], 0.0)
            sincos(t["CS"], math.pi / 2.0)
            TT("dve", t["LR"], t["ER"], t["CS"], ALU.mult, [b], [b])
            TT("dve", t["LI"], t["ER"], t["SN"], ALU.mult, [b], [b])
            TS("dve", t["NR"], t["LR"], -1.0, None, ALU.add, None, [b], [b])
            TT("dve", t["DEN"], ARE, ARE, ALU.mult, [b], [b])
            TT("dve", t["T1"], AIM, AIM, ALU.mult, [b], [b])
            TT("dve", t["DEN"], t["DEN"], t["T1"], ALU.add, [b], [b])
            em.op("dve", lambda e: e.reciprocal(out=t["DEN"], in_=t["DEN"]), [b], [b])
            TT("dve", t["T1"], t["NR"], ARE, ALU.mult, [b], [b])
            TT("dve", t["T3"], t["LI"], AIM, ALU.mult, [b], [b])
            TT("dve", t["T1"], t["T1"], t["T3"], ALU.add, [b], [b])
            TT("dve", t["CR"], t["T1"], t["DEN"], ALU.mult, [b], [b])
            TT("dve", t["T1"], t["LI"], ARE, ALU.mult, [b], [b])
            TT("dve", t["T3"], t["NR"], AIM, ALU.mult, [b], [b])
            TT("dve", t["T1"], t["T1"], t["T3"], ALU.subtract, [b], [b])
            TT("dve", t["CI"], t["T1"], t["DEN"], ALU.mult, [b], [b])
            return t

        def cmul(o_re, o_im, a_re, a_im, b_re, b_im, t1, t2, b):
            TT("dve", t1, a_re, b_re, ALU.mult, [b], [b])
            TT("dve", t2, a_im, b_im, ALU.mult, [b], [b])
            TT("dve", o_re_tmp_holder[0], t1, t2, ALU.subtract, [b], [b])
            TT("dve", t1, a_re, b_im, ALU.mult, [b], [b])
            TT("dve", t2, a_im, b_re, ALU.mult, [b], [b])
            TT("dve", o_im, t1, t2, ALU.add, [b], [b])
            CP("dve", o_re, o_re_tmp_holder[0], [b], [b])

        bS5 = Buf()
        PC = av(70 * KB, 3 * 64)
        em.dma(PC[:, 0:16], are_c, writes=[bS5]); em.dma(PC[:, 16:32], aim_c, writes=[bS5]); em.dma(PC[:, 32:48], ldt_c, writes=[bS5])
        tc_ = s5_derive(16, PC[:, 0:16], PC[:, 16:32], PC[:, 32:48], 71 * KB, bS5)
        CT1 = av(74 * KB, 64)[:, 0:16]; CT2 = av(74 * KB + 64, 64)[:, 0:16]; CT3 = av(74 * KB + 128, 64)[:, 0:16]
        o_re_tmp_holder = [CT3]
        CP("dve", PR[:, 1, :], tc_["LR"], [bS5], [bS5]); CP("dve", PI[:, 1, :], tc_["LI"], [bS5], [bS5])
        for k in range(2, 5):
            cmul(PR[:, k, :], PI[:, k, :], PR[:, k - 1, :], PI[:, k - 1, :], PR[:, 1, :], PI[:, 1, :], CT1, CT2, bS5)
        CP("dve", PR[:, 5, :], PR[:, 4, :], [bS5], [bS5]); CP("dve", PI[:, 5, :], PI[:, 4, :], [bS5], [bS5])
        for k in range(6, 14):
            cmul(PR[:, k, :], PI[:, k, :], PR[:, k - 1, :], PI[:, k - 1, :], PR[:, k - 1, :], PI[:, k - 1, :], CT1, CT2, bS5)
        for k in range(14, 17):
            MS("dve", PR[:, k, :], 0.0, [bS5]); MS("dve", PI[:, k, :], 0.0, [bS5])
        MS("dve", PR[:, 0, :], 1.0, [bS5]); MS("dve", PI[:, 0, :], 0.0, [bS5])
        TS("dve", NPI[:].rearrange("p a b -> p (a b)"), PI[:].rearrange("p a b -> p (a b)"), -1.0, None, ALU.mult, None, [bS5], [bS5])
        CRE = av(75 * KB, 1024)[:, 0:256].rearrange("p (q c) -> p q c", q=16)
        CIM = av(76 * KB, 1024)[:, 0:256].rearrange("p (q c) -> p q c", q=16)
        em.dma(CRE, cre_c, writes=[bS5]); em.dma(CIM, cim_c, writes=[bS5])
        PRW = av(77 * KB, 5 * KB)
        em.dma(PRW[:, 0:256], are_r, writes=[bS5]); em.dma(PRW[:, 256:512], aim_r, writes=[bS5]); em.dma(PRW[:, 512:768], ldt_r, writes=[bS5])
        em.dma(PRW[:, 768:1024], bre_r, writes=[bS5]); em.dma(PRW[:, 1024:1280], bim_r, writes=[bS5])
        tr_ = s5_derive(256, PRW[:, 0:256], PRW[:, 256:512], PRW[:, 512:768], 82 * KB, bS5)
        BBR = av(100 * KB, 1024)[:, 0:256]; BBI = av(101 * KB, 1024)[:, 0:256]
        RT1 = av(102 * KB, 1024)[:, 0:256]; RT2 = av(103 * KB, 1024)[:, 0:256]; RT3 = av(104 * KB, 1024)[:, 0:256]
        o_re_tmp_holder[0] = RT3
        cmul(BBR, BBI, tr_["CR"], tr_["CI"], PRW[:, 768:1024], PRW[:, 1024:1280], RT1, RT2, bS5)
        for qq in range(4):
            for ri, src in enumerate((BBR, BBI)):
                for g2 in range(2):
                    TS("dve", WB[:, qq, :, ri, g2 * 64:(g2 + 1) * 64], src.rearrange("p (k c) -> p k c", k=4),
                       MROW[:, qq * 2 + g2:qq * 2 + g2 + 1], None, ALU.mult, None, [bS5, bL], [bS5])
        em.barrier()
        def v3(off):
            return av(off, 1024)[:, 0:256].rearrange("p (q c) -> p q c", q=16)
        BCR = v3(106 * KB); BCI = v3(107 * KB); BBCR = v3(108 * KB); BBCI = v3(109 * KB); TA = v3(110 * KB); TB_ = v3(111 * KB)
        BTR = v3(129 * KB); BTI = v3(130 * KB); NCIM = v3(128 * KB); GAR = v3(131 * KB); GAI = v3(132 * KB)
        em.dma(BCR, bre_c, writes=[bS5]); em.dma(BCI, bim_c, writes=[bS5])

        def bc(ap2d):
            return ap2d.unsqueeze(2).to_broadcast([128, 16, 16])

        def cmul_b(o_re, o_im, a_re, a_im, s_re, s_im):
            TT("dve", TA, a_re, bc(s_re), ALU.mult, [bS5], [bS5])
            TT("dve", TB_, a_im, bc(s_im), ALU.mult, [bS5], [bS5])
            TT("dve", o_re, TA, TB_, ALU.subtract, [bS5], [bS5])
            TT("dve", TA, a_re, bc(s_im), ALU.mult, [bS5], [bS5])
            TT("dve", TB_, a_im, bc(s_re), ALU.mult, [bS5], [bS5])
            TT("dve", o_im, TA, TB_, ALU.add, [bS5], [bS5])
        cmul_b(BBCR, BBCI, BCR, BCI, tc_["CR"], tc_["CI"])
        TS("dve", NCIM, CIM, -1.0, None, ALU.mult, None, [bS5], [bS5])
        BM = {}
        for tau in range(4):
            if tau == 0:
                sr, si = BBCR, BBCI
            else:
                cmul_b(BTR, BTI, BBCR, BBCI, PR[:, tau, :], PI[:, tau, :])
                sr, si = BTR, BTI
            for g2 in range(2):
                for ri, src in enumerate((sr, si)):
                    t_ = v3(112 * KB + ((tau * 2 + g2) * 2 + ri) * KB)
                    TS("dve", t_, src, MCOL[:, g2:g2 + 1], None, ALU.mult, None, [bS5, bL], [bS5])
                    BM[(tau, g2, ri)] = t_
        for g in range(32):
            q_ = g // 2; g2 = g % 2
            for tau in range(4):
                col = (g * 4 + tau) * 16
                blk = PS[col // 512][0:16, col % 512:col % 512 + 16]
                MM(blk, BM[(tau, g2, 0)][:, q_, :], CRE[:, q_, :], True, False, [bS5], [bPS[col // 512]])
                MM(blk, BM[(tau, g2, 1)][:, q_, :], NCIM[:, q_, :], False, True, [bS5], [bPS[col // 512]])
        KST = av(8 * KB, 8 * KB)[0:16, 0:2048]
        bKS = Buf()
        for bk in range(4):
            CP("dve", KST[:, bk * 512:(bk + 1) * 512], PS[bk][0:16, :], [bPS[bk]], [bKS])
        KSTv = KST.rearrange("p (g t c) -> p g t c", g=32, t=4)
        DSK16 = av(18 * KB, 128)[0:16, 0:32]
        TMPK = av(16 * KB, 2 * KB)[0:16, 0:512].rearrange("p (g c) -> p g c", g=32)
        em.dma(DSK16, dsk16, writes=[bKS])
        TT("dve", TMPK, IDENT[0:16, 0:16].unsqueeze(1).to_broadcast([16, 32, 16]), DSK16.unsqueeze(2).to_broadcast([16, 32, 16]),
           ALU.mult, [bKS, bL], [bKS])
        TT("dve", KSTv[:, :, 0, :], KSTv[:, :, 0, :], TMPK, ALU.add, [bKS], [bKS])
        WKF = av(0, 8 * KB).rearrange("p (k t c) -> p k t c", k=4, t=4)
        bWKF = Buf()
        MS("dve", av(0, 8 * KB), 0.0, [bWKF])
        for g in range(32):
            g8 = g % 8
            em.dma(WKF[16 * g8:16 * g8 + 16, g // 8, :, 16 * g8:16 * g8 + 16], KSTv[0:16, g, :, :], reads=[bKS], writes=[bWKF])
        CP("dve", WK[:], WKF, [bWKF], [bS5])
        for a in range(4):
            cmul_b(GAR, GAI, CRE, CIM, PR[:, a + 1, :], PI[:, a + 1, :])
            for g2 in range(2):
                TS("dve", WCA[:, :, a, 0, 16 * g2:16 * g2 + 16], GAR, MCOL[:, g2:g2 + 1], None, ALU.mult, None, [bS5, bL], [bS5])
                TS("dve", WCA[:, :, a, 1, 16 * g2:16 * g2 + 16], GAI, NMCOL[:, g2:g2 + 1], None, ALU.mult, None, [bS5, bC], [bS5])
        MS("dve", WCA3[:].rearrange("p k a r c -> p (k a r c)"), 0.0, [bS5])
        for kt in range(4):
            CP("dve", WCA3[:, kt, :, :, 32:64], WCA[:, 4 * kt + 3, :, :, :], [bS5], [bS5])
        MS("dve", EPSC[:, 0:1], EPS, [bC])
        for kq in range(NBIS + 1):
            MS("dve", STEPC[:, kq:kq + 1], -0.25 * (0.5 ** kq), [bC])
        em.barrier()

        O_HT = 0
        O_UT = 32 * KB; O_SZT = 48 * KB; O_X = 64 * KB; O_XB = 96 * KB; O_ZB = 112 * KB; O_ZG = 120 * KB; O_GT = 136 * KB; O_WSA = 96 * KB
        O_QT = 32 * KB; O_KT = 48 * KB; O_QIT = 64 * KB; O_KIT = 80 * KB; O_VA = 84 * KB; O_WSB = 101 * KB; O_C2 = 101 * KB

        HT = av(O_HT, 32 * KB, BF16).rearrange("p (k t) -> p k t", k=8)
        UT = av(O_UT, 16 * KB, BF16).rearrange("p (k t) -> p k t", k=4)
        SZT = av(O_SZT, 16 * KB, BF16).rearrange("p (k t) -> p k t", k=4)
        ZG = av(O_ZG, 16 * KB, BF16).rearrange("p (k t) -> p k t", k=4)
        XX = [av(O_X + i * 16 * KB, 16 * KB).rearrange("p (r t) -> p r t", r=2) for i in range(2)]
        HP = [av(O_XB + j2 * 6 * KB, 6 * KB).rearrange("p (r m) -> p r m", r=2) for j2 in range(2)]
        ZB = av(O_ZB, 8 * KB, BF16).rearrange("p (q r m) -> p q r m", q=4, r=2)
        QT = av(O_QT, 16 * KB, BF16).rearrange("p (k t) -> p k t", k=4)
        KT = av(O_KT, 16 * KB, BF16).rearrange("p (k t) -> p k t", k=4)
        QIT = av(O_QIT, 16 * KB, BF16).rearrange("p (k t) -> p k t", k=4)
        KIT = av(O_KIT, 4 * KB, BF16)
        VA = av(O_VA, 16640, BF16).rearrange("p (j h d) -> p j h d", j=16, h=8)

        for b in range(nseq):
          try:
              if stage == 0:
                  raise _Stop()
              XT = [av(O_WSA + i * 4 * KB, 4 * KB) for i in range(2)] + [av(124 * KB + i * 4 * KB, 4 * KB) for i in range(2)]
              XNs = [av(O_WSA + 8 * KB, 4 * KB), av(O_ZB, 4 * KB)]
              JNK = av(O_ZG, 4 * KB)
              bXT = [Buf() for _ in range(4)]; bXNs = [Buf(), Buf()]; bJ = Buf(); bSSs = [Buf() for _ in range(NT)]
              bHT = [Buf() for _ in range(NT)]
              def a_stats(tt):
                  xt = XT[tt % 4]; XN = XNs[tt % 2]; bXN = bXNs[tt % 2]; bSS = bSSs[tt]
                  SS = SMALL3[:, 2 * tt:2 * tt + 1]; RST = SMALL3[:, 2 * tt + 1:2 * tt + 2]
                  if tt == 0:
                      for t2 in range(3):
                          em.dma(XT[t2], x[b, t2 * 128:(t2 + 1) * 128, :], writes=[bXT[t2]])
                  if tt + 3 < NT:
                      em.dma(XT[(tt + 3) % 4], x[b, (tt + 3) * 128:(tt + 4) * 128, :], writes=[bXT[(tt + 3) % 4]])
                  ACT(JNK, xt, AF.Square, [bXT[tt % 4]], [bJ, bSS], accum=SS)
                  TS("dve", RST, SS, 1.0 / D, EPS, ALU.mult, ALU.add, [bSS], [bSS])
                  ACT(RST, RST, AF.Sqrt, [bSS], [bSS])
                  em.op("dve", lambda e, RST=RST: e.reciprocal(out=RST, in_=RST), [bSS], [bSS])
                  TS("dve", XN, xt, RST, None, ALU.mult, None, [bXT[tt % 4], bSS], [bXN])

              def a_trans(tt):
                  XN = XNs[tt % 2]; bXN = bXNs[tt % 2]
                  b0 = 4 + 2 * (tt % 2)
                  for k in range(8):
                      bank = b0 + (k // 4)
                      TR(PS[bank][:, (k % 4) * 128:(k % 4 + 1) * 128], XN[:, k * 128:(k + 1) * 128], IDENT[:], [bXN], [bPS[bank]])
                  for k in range(8):
                      bank = b0 + (k // 4)
                      src = PS[bank][:, (k % 4) * 128:(k % 4 + 1) * 128]
                      dst = HT[:, k, tt * 128:(tt + 1) * 128]
                      if k % 2 == 0:
                          TS("dve", dst, src, AMOD[:, k, b:b + 1], SHSC[:, k, b:b + 1], ALU.mult, ALU.add, [bPS[bank]], [bHT[tt]])
                      else:
                          ACT(dst, src, AF.Identity, [bPS[bank]], [bHT[tt]], bias=SHSC[:, k, b:b + 1], scale=AMOD[:, k, b:b + 1])
              a_stats(0)
              for tt in range(NT):
                  if tt + 1 < NT:
                      a_stats(tt + 1)
                  a_trans(tt)
              if stage == 1:
                  raise _Stop()
              WSF = [av(O_WSA + i * 4 * KB, 4 * KB).rearrange("p (k c) -> p k c", k=8) for i in range(2)]
              WSH = [av(O_WSA + 8 * KB + i * 2 * KB, 2 * KB, BF16).rearrange("p (k c) -> p k c", k=8) for i in range(2)]
              bWSF = [Buf(), Buf()]; bWSH = [Buf(), Buf()]
              em.barrier()

              def proj_fm(mt, evac, wsf=WSF, wsh=WSH, bwsf=bWSF, bwsh=bWSH, banks=(0, 1, 2, 3)):
                  i2 = mt % 2
                  em.dma(wsf[i2], winf[mt], writes=[bwsf[i2]])
                  CP("pool", wsh[i2], wsf[i2], [bwsf[i2]], [bwsh[i2]])
                  for tb in range(4):
                      bank = banks[tb % len(banks)]
                      for k in range(8):
                          MM(PS[bank][:, :], wsh[i2][:, k, :], HT[:, k, tb * 512:(tb + 1) * 512], k == 0, k == 7,
                             [bwsh[i2]] + bHT[tb * 4:(tb + 1) * 4], [bPS[bank]])
                      evac(mt, tb, bank)

              bUT = [Buf() for _ in range(4)]; bSZT = [Buf() for _ in range(4)]

              def evac_A(mt, tb, bank):
                  if mt < 4:
                      CP("dve", UT[:, mt, tb * 512:(tb + 1) * 512], PS[bank][:, :], [bPS[bank]], [bUT[mt]])
                  else:
                      ACT(SZT[:, mt - 4, tb * 512:(tb + 1) * 512], PS[bank][:, :], AF.Silu, [bPS[bank]], [bSZT[mt - 4]])
              for mt in range(8):
                  proj_fm(mt, evac_A)

              if stage == 2:
                  raise _Stop()
              bXq = [Buf(), Buf()]
              bHq = [[Buf(), Buf()], [Buf(), Buf()]]
              bZB = [Buf() for _ in range(4)]
              bZG = [Buf() for _ in range(4)]
              bGT = Buf()
              em.barrier()
              GT = [av(O_GT, 2 * KB), av(O_GT + 2 * KB, 2 * KB), av(O_GT + 4 * KB, 2 * KB)]
              for j2 in range(2):
                  MS("pool", HP[j2][:, :, 0:256], 0.0, [bHq[0][j2]])
              def emit_bu(q):
                  kt, qq = divmod(q, 4)
                  X = XX[q % 2]; bx = bXq[q % 2]
                  for ri in range(2):
                      for tb in range(4):
                          bank = 4 + ((ri * 4 + tb) % 4)
                          MM(PS[bank][:, :], WB[64 * (qq // 2):64 * (qq // 2) + 64, qq, kt, ri, :], UT[64 * (qq // 2):64 * (qq // 2) + 64, kt, tb * 512:(tb + 1) * 512],
                             True, True, [bUT[kt], bS5], [bPS[bank]])
                          ACT(X[:, ri, tb * 512:(tb + 1) * 512], PS[bank][:, :], AF.Copy, [bPS[bank]], [bx])

              def emit_taps(kt):
                  UT4 = UT[:, kt, :].rearrange("p (m a) -> p m a", a=4)
                  for a in range(4):
                      for tau in range(a + 1):
                          MM(PS[a][:, :], WK[:, kt, tau, :], UT4[:, :, a - tau], tau == 0, False, [bUT[kt], bS5], [bPS[a]])

              def emit_scan(q):
                  kt, qq = divmod(q, 4)
                  X = XX[q % 2]; bx = bXq[q % 2]
                  X4 = X.rearrange("p r (m a) -> p r m a", a=4)
                  pr1 = PR[:, 1, q:q + 1]; pi1 = PI[:, 1, q:q + 1]; npi1 = NPI[:, 1, q:q + 1]
                  H = HP; bh = bHq[0]
                  src = X4[:, :, :, 0]; bsrc = [bx]
                  cur = 0
                  for a in range(1, 4):
                      dst = H[cur][:, :, 256:768]; bd = bh[cur]
                      STT("dve", dst, src, pr1, X4[:, :, :, a], ALU.mult, ALU.add, bsrc + [bx], [bd])
                      STT("dve", dst[:, 0, :], src[:, 1, :], npi1, dst[:, 0, :], ALU.mult, ALU.add, bsrc + [bd], [bd])
                      STT("dve", dst[:, 1, :], src[:, 0, :], pi1, dst[:, 1, :], ALU.mult, ALU.add, bsrc + [bd], [bd])
                      src = dst; bsrc = [bd]
                      cur = 1 - cur
                  cur = 1 - cur
                  for j in range(9):
                      d = 1 << j
                      a_, n_ = H[cur], H[1 - cur]
                      ba, bn = bh[cur], bh[1 - cur]
                      pk = 5 + j
                      pr = PR[:, pk, q:q + 1]; pi = PI[:, pk, q:q + 1]; npi = NPI[:, pk, q:q + 1]
                      STT("dve", n_[:, :, 256:768], a_[:, :, 256 - d:768 - d], pr, a_[:, :, 256:768], ALU.mult, ALU.add, [ba], [bn])
                      STT("dve", n_[:, 0, 256:768], a_[:, 1, 256 - d:768 - d], npi, n_[:, 0, 256:768], ALU.mult, ALU.add, [ba, bn], [bn])
                      STT("dve", n_[:, 1, 256:768], a_[:, 0, 256 - d:768 - d], pi, n_[:, 1, 256:768], ALU.mult, ALU.add, [ba, bn], [bn])
                      cur = 1 - cur
                  CP("act", ZB[:, qq, :, :], H[cur][:, :, 256:768], [bh[cur]], [bZB[qq]])

              def emit_carry(q):
                  kt, qq = divmod(q, 4)
                  for a in range(4):
                      for ri in range(2):
                          last = (qq == 3 and ri == 1)
                          if qq < 3:
                              MM(PS[a][32 * qq:32 * qq + 32, 1:512], WCA[:, q, a, ri, :], ZB[:, qq, ri, 0:511], False, last,
                                 [bZB[qq], bS5], [bPS[a]])
                          else:
                              MM(PS[a][64:128, 1:512], WCA3[:, kt, a, ri, :], ZB[:, qq, ri, 0:511], False, last,
                                 [bZB[qq], bS5], [bPS[a]])

              def emit_gelu(kt):
                  ZG4 = ZG[:, kt, :].rearrange("p (m a) -> p m a", a=4)
                  for a in range(4):
                      y = PS[a][:, :]
                      ACT(GT[0][:, 0:512], y, AF.Square, [bPS[a], bGT], [bGT])
                      TS("dve", GT[0][:, 0:512], GT[0][:, 0:512], 0.044715, 1.0, ALU.mult, ALU.add, [bGT], [bGT])
                      TT("dve", GT[1][:, 0:512], GT[0][:, 0:512], y, ALU.mult, [bGT, bPS[a]], [bGT])
                      ACT(GT[2][:, 0:512], GT[1][:, 0:512], AF.Sigmoid, [bGT], [bGT], scale=1.5957691216057308)
                      TT("dve", ZG4[:, :, a], GT[2][:, 0:512], y, ALU.mult, [bGT, bPS[a]], [bZG[kt]])

              emit_bu(0)
              for q in range(16):
                  kt, qq = divmod(q, 4)
                  if q + 1 < 16:
                      emit_bu(q + 1)
                  if qq == 0:
                      emit_taps(kt)
                  emit_scan(q)
                  emit_carry(q)
                  if qq == 3:
                      emit_gelu(kt)
              em.barrier()
              if stage == 3:
                  raise _Stop()
              YO = [av(O_XB + i * 1 * KB, 1 * KB, BF16) for i in range(2)]
              bYO = [Buf(), Buf()]
              G1 = av(O_XB + 2 * KB, 2 * KB); G2 = av(O_XB + 4 * KB, 2 * KB); bG1 = Buf()
              bYD = Buf()
              cnt = 0
              for nt in range(4):
                  for tb in range(4):
                      bank = 4 + (cnt % 4)
                      for kc in range(4):
                          MM(PS[bank][:, :], WGLUB[:, kc, nt * 128:(nt + 1) * 128], ZG[:, kc, tb * 512:(tb + 1) * 512], kc == 0, kc == 3,
                             bZG + [bC], [bPS[bank]])
                      ACT(G1[:, 0:512], PS[bank][:, :], AF.Sigmoid, [bPS[bank]], [bG1], bias=BGLU[:, nt:nt + 1])
                      TT("dve", G2[:, 0:512], G1[:, 0:512], ZG[:, nt, tb * 512:(tb + 1) * 512], ALU.mult, [bG1] + bZG, [bG1])
                      yo = YO[cnt % 2]
                      TT("dve", yo[:, 0:512], G2[:, 0:512], SZT[:, nt, tb * 512:(tb + 1) * 512], ALU.mult, [bG1, bSZT[nt]], [bYO[cnt % 2]])
                      em.dma(yst_d[nt, :, tb * 512:(tb + 1) * 512], yo[:, 0:512], reads=[bYO[cnt % 2]], writes=[bYD])
                      cnt += 1
              em.barrier()

              if stage == 4:
                  raise _Stop()
              WSF2 = [av(O_WSB + i * 4 * KB, 4 * KB).rearrange("p (k c) -> p k c", k=8) for i in range(2)]
              WSH2 = [av(O_WSB + 8 * KB + i * 2 * KB, 2 * KB, BF16).rearrange("p (k c) -> p k c", k=8) for i in range(2)]
              bWSF2 = [Buf(), Buf()]; bWSH2 = [Buf(), Buf()]
              SQs = [av(O_WSB + 12 * KB + i * KB, 1 * KB, BF16) for i in range(2)]
              RSs = [av(O_WSB + 14 * KB + i * 2 * KB, 2 * KB) for i in range(2)]
              bSQs = [Buf(), Buf()]; bRSs = [Buf(), Buf()]
              bQT = Buf(); bKT = Buf(); bQIT = Buf(); bKIT = Buf()
              pend = []
              nrm = [0]

              def flush_pend():
                  while pend:
                      pend.pop(0)()

              def evac_B(mt, tb, bank):
                  src = PS[bank][:, :]
                  cs = slice(tb * 512, (tb + 1) * 512)
                  if mt < 16:
                      dst = QT[:, mt - 8, cs] if mt < 12 else KT[:, mt - 12, cs]
                      gn = QG if mt < 12 else KG
                      bb = bQT if mt < 12 else bKT
                      n2 = nrm[0] % 2; nrm[0] += 1
                      SQ = SQs[n2]; RS = RSs[n2]; bSQ = bSQs[n2]; bRS = bRSs[n2]; sbank = 5 + n2
                      ACT(SQ[:, 0:512], src, AF.Square, [bPS[bank]], [bSQ])
                      flush_pend()

                      def rest():
                          MM(PS[sbank][:, :], BONESB[:], SQ[:, 0:512], True, True, [bSQ, bC], [bPS[sbank]])
                          ACT(RS[:, 0:512], PS[sbank][:, :], AF.Ln, [bPS[sbank]], [bRS], bias=EPSC[:, 0:1], scale=1.0 / 64.0)
                          ACT(RS[:, 0:512], RS[:, 0:512], AF.Exp, [bRS], [bRS], scale=-0.5)
                          STT("dve", dst, src, gn[:, 0:1], RS[:, 0:512], ALU.mult, ALU.mult, [bPS[bank], bRS, bL], [bb])
                      pend.append(rest)
                  elif mt < 20:
                      flush_pend()
                      CP("dve", QIT[:, mt - 16, cs], src, [bPS[bank]], [bQIT])
                  else:
                      ACT(KIT[:, cs], src, AF.Copy, [bPS[bank]], [bKIT])
              for mt in range(8, 21):
                  proj_fm(mt, evac_B, WSF2, WSH2, bWSF2, bWSH2, banks=(0, 1, 2, 3))
              flush_pend()
              WTS = av(O_WSB, 4 * KB)
              WTB = av(O_WSB + 4 * KB, 8 * KB, BF16).rearrange("p (k c) -> p k c", k=8)
              bWTS = Buf(); bWTB = Buf(); bVA = Buf(); bAZ = Buf(); bWI = Buf(); bAZD = Buf()
              em.barrier()
              MS("dve", VA[:, :, :, 64:65], 1.0, [bVA])
              AZO = [av(O_WSB + 12 * KB + i * KB, 1 * KB, BF16) for i in range(2)]; bAZO = [Buf(), Buf()]
              for part in range(3):
                  ncol = 512 if part < 2 else 8
                  c0 = part * 512
                  for k in range(8):
                      em.dma(WTS[:, 0:ncol], wint[:, k, c0:c0 + ncol], writes=[bWTS])
                      CP("pool", WTB[:, k, 0:ncol], WTS[:, 0:ncol], [bWTS], [bWTB, bWTS])
                  for tt in range(NT):
                      bank = tt % 4
                      for k in range(8):
                          MM(PS[bank][:, 0:ncol], HT[:, k, tt * 128:(tt + 1) * 128], WTB[:, k, 0:ncol], k == 0, k == 7,
                             [bWTB, bHT[tt]], [bPS[bank]])
                      if part == 0:
                          CP("dve", VA[:, tt, :, 0:64], PS[bank][:, :].rearrange("p (h d) -> p h d", h=8), [bPS[bank]], [bVA])
                      elif part == 1:
                          ACT(AZO[tt % 2][:, 0:512], PS[bank][:, :], AF.Silu, [bPS[bank]], [bAZO[tt % 2]])
                          em.dma(az_d[tt * 128:(tt + 1) * 128, :], AZO[tt % 2][:, 0:512], reads=[bAZO[tt % 2]], writes=[bAZD])
                      else:
                          TS("dve", WI[:, tt, :], PS[bank][:, 0:8], 8.0 ** -0.5, None, ALU.mult, None, [bPS[bank]], [bWI])
              em.barrier()

              if stage == 5:
                  raise _Stop()
              SCs = [av(0, 8 * KB), av(O_C2 + 16 * KB, 8 * KB)]
              NM = av(8 * KB, 4 * KB, BF16)
              NMT = [av(12 * KB + i * 4 * KB, 4 * KB, BF16).rearrange("p (j t) -> p j t", j=16) for i in range(2)]
              PT = [av(20 * KB + i * KB, 1 * KB, BF16).rearrange("p (h t) -> p h t", h=4) for i in range(4)] + \
                   [av(O_C2 + 28 * KB + i * KB, 1 * KB, BF16).rearrange("p (h t) -> p h t", h=4) for i in range(2)]
              TMPH = [av(24 * KB + i * 2 * KB, 2 * KB) for i in range(2)]
              XR_ = av(28 * KB, 4 * KB)
              OT = av(O_C2, 4 * KB)
              GATEB = av(O_C2 + 4 * KB, 4 * KB)
              CAT = av(O_C2 + 8 * KB, 2 * KB, BF16).rearrange("p (k t) -> p k t", k=8)
              AZT = av(O_C2 + 10 * KB, 1 * KB, BF16)
              YA = av(O_C2 + 11 * KB, 2 * KB)
              YAB = av(O_C2 + 13 * KB, 1 * KB, BF16)
              RD = SMALL[:, 8:16]
              bSCs = [Buf(), Buf()]; bBIs = [Buf(), Buf()]; bNM = Buf(); bNMT = [Buf(), Buf()]; bPT = [Buf() for _ in range(6)]; bTMPH = [Buf(), Buf()]
              bXR = Buf(); bOT = Buf(); bGB = Buf(); bCAT = Buf(); bAZT = Buf(); bYA = Buf(); bBI = Buf(); bRD = Buf()
              for hh in range(2):
                  MM(PS[6 + hh][:, :], SEL[0:4, b, :], MODG[0:4, hh * 512:(hh + 1) * 512], True, True, [bC, bL], [bPS[6 + hh]])
                  CP("dve", GATEB[:, hh * 512:(hh + 1) * 512], PS[6 + hh][:, :], [bPS[6 + hh]], [bGB])
              LO = SMALL[:, 16:17]; HI = SMALL[:, 17:18]; TAU = SMALL[:, 18:19]; CNTc = SMALL[:, 19:20]; TF = SMALL[:, 20:21]
              STEPS = SMALL[:, 24:24 + NBIS + 1]
              NTAU = SMALL[:, 21:22]; SSUM = SMALL[:, 22:23]; TSG = SMALL[:, 23:24]
              STEPN = SMALL[:, 24:24 + NBIS + 1]
              ptc = [0]
              PSB6 = PS[6].bitcast(BF16); PSB7 = PS[7].bitcast(BF16)

              SM2 = [SMALL[:, 16:16 + 24], SMALL2[:, 0:24]]

              def scal(i):
                  sm = SM2[i % 2]
                  return dict(LO=sm[:, 0:1], HI=sm[:, 1:2], TAU=sm[:, 2:3], TF=sm[:, 3:4], NTAU=sm[:, 4:5], SSUM=sm[:, 5:6],
                              TSG=sm[:, 6:7], STEPN=sm[:, 8:8 + NBIS + 1])

              def fidx(i):
                  nk = (i + 1) * 128
                  ngr = (nk + 511) // 512
                  SC = SCs[i % 2]; bSC = bSCs[i % 2]; bBI = bBIs[i % 2]; v = scal(i)
                  for g in range(ngr):
                      k0 = g * 512; kn = min(512, nk - k0)
                      for h in range(8):
                          bank = h % 2
                          base = 64 * (h % 2)
                          MM(PS[bank][:, 0:kn], QIT[base:base + 64, h // 2, i * 128:(i + 1) * 128], KIT[base:base + 64, k0:k0 + kn],
                             True, True, [bQIT, bKIT], [bPS[bank]])
                          if h == 0:
                              TS("dve", SC[:, k0:k0 + kn], PS[bank][:, 0:kn], 0.0, WI[:, i, h:h + 1], ALU.max, ALU.mult,
                                 [bPS[bank], bWI], [bSC])
                          else:
                              th = TMPH[h % 2]
                              TS("dve", th[:, 0:kn], PS[bank][:, 0:kn], 0.0, WI[:, i, h:h + 1], ALU.max, ALU.mult,
                                 [bPS[bank], bWI], [bTMPH[h % 2]])
                              TT("pool", SC[:, k0:k0 + kn], SC[:, k0:k0 + kn], th[:, 0:kn], ALU.add, [bTMPH[h % 2], bSC], [bSC])
                          yield
                  HI, LO, TF, NTAU, STEPN = v["HI"], v["LO"], v["TF"], v["NTAU"], v["STEPN"]
                  em.op("dve", lambda e: e.tensor_reduce(out=HI, in_=SC[:, 0:nk], axis=mybir.AxisListType.X, op=ALU.max), [bSC], [bBI])
                  em.op("dve", lambda e: e.tensor_reduce(out=LO, in_=SC[:, 0:nk], axis=mybir.AxisListType.X, op=ALU.min), [bSC], [bBI])
                  TT("dve", SC[:, i * 128:(i + 1) * 128], SC[:, i * 128:(i + 1) * 128], TRI[:], ALU.add, [bSC, bL], [bSC])
                  TT("dve", TF, HI, LO, ALU.subtract, [bBI], [bBI])
                  STT("dve", LO, TF, -0.01, LO, ALU.mult, ALU.add, [bBI], [bBI])
                  TT("dve", TF, HI, LO, ALU.subtract, [bBI], [bBI])
                  STT("dve", NTAU, TF, -0.5, LO, ALU.mult, ALU.subtract, [bBI], [bBI])
                  TS("dve", STEPN, STEPC[:, 0:NBIS + 1], TF, None, ALU.mult, None, [bBI, bC], [bBI])
                  yield

              def n_fidx(i):
                  return (((i + 1) * 128 + 511) // 512) * 8 + 1

              def fbis(i):
                  nk = (i + 1) * 128
                  SC = SCs[i % 2]; bSC = bSCs[i % 2]; bBI = bBIs[i % 2]; v = scal(i)
                  NTAU, SSUM, TSG, STEPN, TAU = v["NTAU"], v["SSUM"], v["TSG"], v["STEPN"], v["TAU"]
                  for kq in range(NBIS):
                      ACT(NM[:, 0:nk], SC[:, 0:nk], AF.Sign, [bSC, bBI], [bNM, bBI], bias=NTAU, scale=1.0, accum=SSUM)
                      ACT(TSG, SSUM, AF.Sign, [bBI], [bBI], bias=float(nk) - (2.0 * TOPK - 0.5), scale=1.0)
                      ACT(NTAU, TSG, AF.Identity, [bBI], [bBI], bias=NTAU, scale=STEPN[:, kq:kq + 1])
                      yield
                  STT("dve", TAU, NTAU, -1.0, STEPN[:, NBIS:NBIS + 1], ALU.mult, ALU.add, [bBI], [bBI])
                  em.op("dve", lambda e: e.tensor_scalar(out=NM[:, 0:nk], in0=SC[:, 0:nk], scalar1=TAU, scalar2=NEG,
                                                         op0=ALU.is_lt, op1=ALU.mult), [bSC, bBI], [bNM])
                  nmt = NMT[i % 2]; bnmt = bNMT[i % 2]
                  for j0 in range(0, i + 1, 8):
                      jn = min(8, i + 1 - j0)
                      for jj in range(jn):
                          j = j0 + jj
                          TR(PSB6[:, jj * 128:(jj + 1) * 128], NM[:, j * 128:(j + 1) * 128], IDENTB[:], [bNM, bC], [bPS[6]])
                      CP("dve", nmt[:, j0:j0 + jn, :], PSB6[:, 0:jn * 128].rearrange("p (j t) -> p j t", j=jn), [bPS[6]], [bnmt])
                  yield

              def n_fbis(i):
                  return NBIS + 1

              def attn(i):
                  nmt = NMT[i % 2]; bnmt = bNMT[i % 2]
                  pending = []
                  for h in range(8):
                      base = 64 * (h % 2); ch = h // 2; ob = 4 + h // 4; hh = h % 4
                      for j0 in range(0, i + 1, 4):
                          jn = min(4, i + 1 - j0)
                          bank = (2, 3, 7)[ptc[0] % 3]
                          pt = PT[ptc[0] % 6]; bpt = bPT[ptc[0] % 6]; ptc[0] += 1
                          for jj in range(jn):
                              j = j0 + jj
                              dlt = (i - j) * 128
                              reg = PS[bank][:, jj * 128:(jj + 1) * 128]
                              MM(reg, IDENTB[:], nmt[:, j, :], True, False, [bnmt, bC], [bPS[bank]])
                              if dlt <= NEAR_MAX:
                                  MM(reg, JB[:], TREVB[:, h, dlt:dlt + 128], False, False, [bC], [bPS[bank]])
                              MM(reg, KT[base:base + 64, ch, j * 128:(j + 1) * 128],
                                 QT[base:base + 64, ch, i * 128:(i + 1) * 128], False, True, [bQT, bKT], [bPS[bank]])
                          ACT(pt[:, 0:jn, :], PS[bank][:, 0:jn * 128].rearrange("p (h t) -> p h t", h=jn), AF.Exp, [bPS[bank]], [bpt])
                          while len(pending) > 1:
                              pending.pop(0)()

                          def pv(pt=pt, bpt=bpt, j0=j0, jn=jn, h=h, ob=ob, hh=hh):
                              for jj in range(jn):
                                  j = j0 + jj
                                  MM(PS[ob][:, hh * 65:(hh + 1) * 65], pt[:, jj, :], VA[:, j, h, :], j == 0, j == i, [bpt, bVA], [bPS[ob]])
                          pending.append(pv)
                          yield
                  while pending:
                      pending.pop(0)()
                  yield

              def n_attn(i):
                  return 8 * ((i + 4) // 4) + 1

              def back_a(i):
                  em.dma(AZT[:, 0:512], az_d[i * 128:(i + 1) * 128, :], reads=[bAZD], writes=[bAZT])
                  em.dma(CAT[:, 0:4, :], yst_d[:, :, i * 128:(i + 1) * 128].rearrange("k p t -> p k t"), reads=[bYD], writes=[bCAT])
                  em.dma(XR_, x[b, i * 128:(i + 1) * 128, :], reads=[bOT], writes=[bXR])
                  for g in range(2):
                      O3 = PS[4 + g][:, 0:260].rearrange("p (h d) -> p h d", h=4)
                      em.op("dve", lambda e, O3=O3, g=g: e.reciprocal(out=RD[:, 4 * g:4 * g + 4], in_=O3[:, :, 64]), [bPS[4 + g]], [bRD])
                      TT("dve", YA[:, g * 256:(g + 1) * 256].rearrange("p (h d) -> p h d", h=4), O3[:, :, 0:64],
                         RD[:, 4 * g:4 * g + 4].unsqueeze(2).to_broadcast([128, 4, 64]), ALU.mult, [bPS[4 + g], bRD], [bYA])
                  TT("dve", YAB[:, 0:512], YA[:, 0:512], AZT[:, 0:512], ALU.mult, [bYA, bAZT], [bYA])

              def back_b(i):
                  for c in range(4):
                      TR(PSB6[:, c * 128:(c + 1) * 128], YAB[:, c * 128:(c + 1) * 128], IDENTB[:], [bYA, bC], [bPS[6]])
                  CP("dve", CAT[:, 4:8, :], PSB6[:, 0:512].rearrange("p (c t) -> p c t", c=4), [bPS[6]], [bCAT])
                  yield
                  for hh in range(2):
                      bank = 6
                      for k in range(8):
                          MM(PS[bank][:, :], CAT[:, k, :], WOUTB[:, k, hh * 512:(hh + 1) * 512], k == 0, k == 7, [bCAT, bC], [bPS[bank]])
                      TT("dve", OT[:, hh * 512:(hh + 1) * 512], PS[bank][:, :], GATEB[:, hh * 512:(hh + 1) * 512], ALU.mult,
                         [bPS[bank], bGB], [bOT])
                      TT("dve", OT[:, hh * 512:(hh + 1) * 512], OT[:, hh * 512:(hh + 1) * 512], XR_[:, hh * 512:(hh + 1) * 512], ALU.add,
                         [bOT, bXR], [bOT])
                      if hh == 1:
                          em.dma(out[b, i * 128:(i + 1) * 128, :], OT, reads=[bOT], writes=[bOT])
                      yield

              def merge(streams):
                  st_ = [(t[0], t[1], (t[2] if len(t) > 2 else 1.0)) for t in streams]
                  prog = [0] * len(st_)
                  while True:
                      best = None; bestv = None
                      for idx, (g, n, ef) in enumerate(st_):
                          if prog[idx] < n:
                              v = (prog[idx] + 0.5) * ef / n
                              if best is None or v < bestv:
                                  best = idx; bestv = v
                      if best is None:
                          break
                      next(st_[best][0], None)
                      prog[best] += 1
                  for g, n, ef in st_:
                      for _ in g:
                          pass

              merge([(fidx(0), n_fidx(0))])
              merge([(fidx(1), n_fidx(1)), (fbis(0), n_fbis(0))])
              for i in range(NT):
                  st = []
                  if PHASEC_MODE in ("all", "attn"):
                      st.append((attn(i), n_attn(i)))
                  if PHASEC_MODE in ("all", "front", "fidx"):
                      if i + 2 < NT:
                          st.append((fidx(i + 2), n_fidx(i + 2)))
                  if PHASEC_MODE in ("all", "front", "fbis"):
                      if i + 1 < NT:
                          st.append((fbis(i + 1), n_fbis(i + 1), FBIS_EF))
                  if PHASEC_MODE in ("all", "attn") and i > 0:
                      st.append((back_b(i - 1), 3, BACK_EF))
                  merge(st)
                  if PHASEC_MODE in ("all", "attn"):
                      back_a(i)
              if PHASEC_MODE in ("all", "attn"):
                  merge([(back_b(NT - 1), 3)])
              em.barrier()
          except _Stop:
            break
        em.barrier()
        em.replay()
    return nc


def t5_bucket(dist):
    max_exact = 16
    d = np.maximum(dist, 1).astype(np.float32)
    large = max_exact + (np.log(d / max_exact) / np.float32(math.log(128 / max_exact)) * (32 - max_exact)).astype(np.int32)
    large = np.minimum(large, 31)
    return np.where(dist < max_exact, dist, large)


def host_consts():
    c = {}
    c["c_ident"] = np.eye(128, dtype=np.float32)
    c["c_jrev"] = np.ascontiguousarray(np.eye(128, dtype=np.float32)[::-1])
    bo = np.zeros((128, 128), np.float32); bo[:64, :64] = 1; bo[64:, 64:] = 1
    c["c_bones"] = bo
    t = np.arange(128)
    c["c_tri"] = np.where(t[None, :] <= t[:, None], 0.0, -1e9).astype(np.float32)
    sel = np.zeros((4, 4, 128), np.float32)
    for b in range(4):
        sel[b, b, :] = 1
    c["c_sel"] = sel
    oh = np.zeros((32, 384), np.float32)
    for i in range(127, 384):
        d = i - 127
        oh[t5_bucket(np.array([d]))[0], i] += 1.0
        oh[31, i] -= 1.0
    c["c_oh"] = oh
    p = np.arange(128)
    mrow = np.zeros((128, 8), np.float32); mrow[p, (p // 32) * 2 + ((p // 16) % 2)] = 1
    mcol = np.zeros((128, 2), np.float32); mcol[p, p // 64] = 1
    c["c_mrow"] = mrow; c["c_mcol"] = mcol
    return c


def host_layout(inp):
    f = lambda a: np.ascontiguousarray(a, dtype=np.float32)
    m = {}
    m["wada"] = f(inp["w_ada"][0].reshape(8, 128, 3072).transpose(1, 0, 2))
    m["bada4"] = f(np.broadcast_to(inp["b_ada"][0][None, :], (4, 3072)))
    m["ng"] = f(inp["norm_g"][0].reshape(8, 128).T)
    w = inp["w_in"][0]
    cols_fm = np.concatenate([np.arange(0, 512), np.arange(512, 1024), np.arange(1024, 1536), np.arange(1536, 2048),
                              np.arange(3072, 3584), np.arange(3584, 3648), np.arange(3584, 3648)])
    wf = w[:, cols_fm]
    m["winf"] = f(wf.reshape(8, 128, 21, 128).transpose(2, 1, 0, 3))
    cols_tm = np.concatenate([np.arange(2048, 2560), np.arange(2560, 3072), np.arange(3648, 3656)])
    m["wint"] = f(w[:, cols_tm].reshape(8, 128, 1032).transpose(1, 0, 2))
    m["wout"] = f(inp["w_out"][0].reshape(8, 128, 1024).transpose(1, 0, 2))
    m["wglu"] = f(inp["w_glu"][0].reshape(4, 128, 512).transpose(1, 0, 2))
    m["bglu"] = f(inp["b_glu"][0].reshape(4, 128).T)
    m["dskip"] = f(inp["d_skip"][0].reshape(4, 128).T)
    m["qg2"] = f(np.tile(inp["q_gain"][0], 2).reshape(128, 1))
    m["kg2"] = f(np.tile(inp["k_gain"][0], 2).reshape(128, 1))
    m["relb"] = f(inp["rel_bias"])
    are, aim, ldt = inp["a_re"][0], inp["a_im"][0], inp["log_dt"][0]
    col = lambda a: f(a.reshape(16, 2, 64).transpose(1, 2, 0).reshape(128, 16))
    m["are_c"] = col(are); m["aim_c"] = col(aim); m["ldt_c"] = col(np.repeat(ldt[:, None], 64, 1))
    ccol = lambda a: f(a.reshape(16, 2, 16, 64).transpose(1, 3, 0, 2).reshape(128, 16, 16))
    m["cre_c"] = ccol(inp["c_re"][0]); m["cim_c"] = ccol(inp["c_im"][0])
    row = lambda a: f(np.repeat(a.reshape(4, 8, 64).transpose(1, 0, 2)[:, None, :, :], 16, 1).reshape(128, 256))
    m["are_r"] = row(are); m["aim_r"] = row(aim); m["ldt_r"] = row(np.repeat(ldt[:, None], 64, 1))
    brow = lambda a: f(a.reshape(4, 8, 64, 16).transpose(1, 3, 0, 2).reshape(128, 256))
    m["bre_r"] = brow(inp["b_re"][0]); m["bim_r"] = brow(inp["b_im"][0])
    bcol = lambda a: f(a.reshape(16, 2, 64, 16).transpose(1, 2, 0, 3).reshape(128, 16, 16))
    m["bre_c"] = bcol(inp["b_re"][0]); m["bim_c"] = bcol(inp["b_im"][0])
    m["dsk16"] = f(inp["d_skip"][0].reshape(32, 16).T)
    m.update(host_consts())
    return m


_NC_CACHE = {}


def kernel(**inputs):
    inp = {k: np.asarray(v) for k, v in inputs.items()}
    n_cores = 8
    shared = host_layout(inp)
    x = np.ascontiguousarray(inp["x"], dtype=np.float32)
    c = np.asarray(inp["c"], dtype=np.float32)
    in_maps = []
    for r in range(n_cores):
        m = dict(shared)
        m["x"] = x[r * NSEQ:(r + 1) * NSEQ]
        m["ct"] = np.ascontiguousarray(c[r * NSEQ:(r + 1) * NSEQ].T.reshape(8, 128, NSEQ).transpose(1, 0, 2))
        in_maps.append(m)
    if "nc" not in _NC_CACHE:
        _NC_CACHE["nc"] = build_program()
    res = run_bass_kernel_spmd(_NC_CACHE["nc"], in_maps, core_ids=list(range(n_cores)))
    outs = [np.asarray(res.results[r]["out"]) for r in range(n_cores)]
    return np.concatenate(outs, axis=0).astype(np.float32)
```
